# Optimizing a Trainium2 kernel written in Bass

```python
import math
import jax
import jax.numpy as jnp
from jax import lax
import numpy as np

D_MODEL = 1024
BATCH = 16
SEQ = 256
DEPTH = 2
DEC_BATCH = 4
DEC_SEQ = 2048
PAST_LEN = 512

GRID_W = 64
HEAD_DIM = 64
M_HEADS = 4
M_WIDTH = M_HEADS * HEAD_DIM
G_HEADS = 4
G_WIDTH = G_HEADS * HEAD_DIM
CONV_W = 5
A_HEADS = 8
A_KV_HEADS = 2
A_WIDTH = A_HEADS * HEAD_DIM
KV_WIDTH = A_KV_HEADS * HEAD_DIM
MIX_WIDTH = M_WIDTH + G_WIDTH + A_WIDTH
N_DIR = 2
CHUNK = 64
Q_BLOCK = 128
D_FF = 2816
ROPE_BASE = 10000.0
EPS = 1e-6
N_MOD = 9

SPLIT_SIZES = (M_WIDTH, M_WIDTH, M_WIDTH, M_WIDTH, N_DIR * M_HEADS, N_DIR * M_HEADS,
               3 * G_WIDTH, G_WIDTH, N_DIR * G_HEADS, N_DIR * G_HEADS,
               A_WIDTH, KV_WIDTH, KV_WIDTH)
IN_WIDTH = sum(SPLIT_SIZES)

kernel_name = 'hybrid_mlstm_deltanet_gqa_prefix_dit'


def _rms(x, g):
    xf = x.astype(jnp.float32)
    y = xf * lax.rsqrt(jnp.mean(xf * xf, axis=-1, keepdims=True) + EPS)
    return (y * g.astype(jnp.float32)).astype(x.dtype)


def _swiglu(h, w_in, w_out):
    gate, up = jnp.split(h @ w_in, 2, axis=-1)
    return (jax.nn.silu(gate) * up) @ w_out


def _modulation(cond, ada_w, ada_b):
    m = jax.nn.silu(cond) @ ada_w + ada_b
    return m.reshape(m.shape[0], N_MOD, 1, D_MODEL)


def _axial_rope(n):
    rows = n // GRID_W
    row = jnp.repeat(jnp.arange(rows, dtype=jnp.float32), GRID_W)
    col = jnp.tile(jnp.arange(GRID_W, dtype=jnp.float32), rows)
    n_freq = HEAD_DIM // 4
    inv = ROPE_BASE ** (-jnp.arange(n_freq, dtype=jnp.float32) / n_freq)
    ang = jnp.stack([row[:, None] * inv, col[:, None] * inv], axis=1)
    return jnp.cos(ang), jnp.sin(ang)


def _apply_rope(x, cos, sin):
    B, T, H, _ = x.shape
    xf = x.astype(jnp.float32).reshape(B, T, H, 2, 2, HEAD_DIM // 4)
    x1, x2 = xf[..., 0, :], xf[..., 1, :]
    c = cos[None, :, None]
    s = sin[None, :, None]
    out = jnp.stack([x1 * c - x2 * s, x2 * c + x1 * s], axis=-2)
    return out.reshape(x.shape).astype(x.dtype)


def _short_conv(x, w):
    C = x.shape[-1]
    return lax.conv_general_dilated(
        x, w[:, None, :].astype(x.dtype), window_strides=(1,),
        padding=((CONV_W // 2, CONV_W // 2),),
        dimension_numbers=('NWC', 'WIO', 'NWC'), feature_group_count=C)


def _attention(q, k, v):
    B, Sq, H, Dh = q.shape
    kvh = k.shape[2]
    grp = H // kvh
    nb = Sq // Q_BLOCK
    qb = q.reshape(B, nb, Q_BLOCK, kvh, grp, Dh).transpose(1, 0, 2, 3, 4, 5)
    scale = 1.0 / math.sqrt(Dh)

    def block(qblk):
        s = jnp.einsum('bqkgd,bskd->bkgqs', qblk, k).astype(jnp.float32) * scale
        p = jax.nn.softmax(s, axis=-1).astype(v.dtype)
        return jnp.einsum('bkgqs,bskd->bqkgd', p, v)

    o = lax.map(block, qb)
    return o.transpose(1, 0, 2, 3, 4, 5).reshape(B, Sq, H * Dh)


def _mlstm_scan(q, k, v, ig, lf, state):
    B, T, H, Dh = q.shape
    nc = T // CHUNK
    causal = jnp.tril(jnp.ones((CHUNK, CHUNK), bool))[None, :, :, None]

    def chunks(a):
        return jnp.moveaxis(a.reshape(B, nc, CHUNK, *a.shape[2:]), 1, 0)

    def step(carry, xs):
        C, n, m = carry
        qc, kc, vc, ic, fc = xs
        b = jnp.cumsum(fc, axis=1)
        inter = b + m[:, None, :]
        dlog = jnp.where(causal, b[:, :, None, :] - b[:, None, :, :] + ic[:, None, :, :], -jnp.inf)
        mt = jnp.maximum(inter, dlog.max(axis=2))
        s = jnp.einsum('bthd,bjhd->btjh', qc, kc) * jnp.exp(dlog - mt[:, :, None, :])
        a_int = jnp.exp(inter - mt)
        num = jnp.einsum('btjh,bjhe->bthe', s, vc) + a_int[..., None] * jnp.einsum('bthd,bhde->bthe', qc, C)
        den = s.sum(axis=2) + a_int * jnp.einsum('bthd,bhd->bth', qc, n)
        h = num / jnp.maximum(jnp.abs(den), jnp.exp(-mt))[..., None]
        bl = b[:, -1]
        lw = bl[:, None, :] - b + ic
        m_new = jnp.maximum(bl + m, lw.max(axis=1))
        wj = jnp.exp(lw - m_new[:, None, :])
        dec = jnp.exp(bl + m - m_new)
        C = dec[..., None, None] * C + jnp.einsum('bjh,bjhd,bjhe->bhde', wj, kc, vc)
        n = dec[..., None] * n + jnp.einsum('bjh,bjhd->bhd', wj, kc)
        return (C, n, m_new), h

    carry, hs = lax.scan(step, state, (chunks(q), chunks(k), chunks(v), chunks(ig), chunks(lf)))
    return jnp.moveaxis(hs, 0, 1).reshape(B, T, H, Dh), carry


def _delta_scan(q, k, v, g, beta, S0):
    B, T, H, Dh = q.shape
    nc = T // CHUNK

    def chunks(a):
        return jnp.moveaxis(a.reshape(B, nc, CHUNK, H, *a.shape[3:]), 3, 2)

    qc, kc, vc, gc, bc = (chunks(a) for a in (q, k, v, g, beta))
    gcum = jnp.cumsum(gc, axis=-1)
    lower = jnp.tril(jnp.ones((CHUNK, CHUNK), bool))
    strict = jnp.tril(jnp.ones((CHUNK, CHUNK), bool), -1)
    decay = jnp.exp(jnp.where(lower, gcum[..., :, None] - gcum[..., None, :], -jnp.inf))
    kb = kc * bc[..., None]
    a_mat = jnp.eye(CHUNK, dtype=jnp.float32) + jnp.where(
        strict, jnp.einsum('bnhtd,bnhjd->bnhtj', kb, kc) * decay, 0.0)
    u = lax.linalg.triangular_solve(a_mat, vc * bc[..., None], left_side=True, lower=True, unit_diagonal=True)
    w = lax.linalg.triangular_solve(a_mat, kb * jnp.exp(gcum)[..., None], left_side=True, lower=True, unit_diagonal=True)
    qk = jnp.einsum('bnhtd,bnhjd->bnhtj', qc, kc) * decay
    q_dec = qc * jnp.exp(gcum)[..., None]
    k_dec = kc * jnp.exp(gcum[..., -1:] - gcum)[..., None]
    g_last = jnp.exp(gcum[..., -1])

    def step(S, xs):
        qd, kd, uc, wc, qkc, gl = xs
        v_new = uc - jnp.einsum('bhtd,bhde->bhte', wc, S)
        o = jnp.einsum('bhtd,bhde->bhte', qd, S) + jnp.einsum('bhtj,bhje->bhte', qkc, v_new)
        S = S * gl[..., None, None] + jnp.einsum('bhtd,bhte->bhde', kd, v_new)
        return S, o

    xs = tuple(jnp.moveaxis(a, 1, 0) for a in (q_dec, k_dec, u, w, qk, g_last))
    S, o = lax.scan(step, S0, xs)
    o = jnp.moveaxis(jnp.moveaxis(o, 0, 1), 2, 3).reshape(B, T, H, Dh)
    return o, S


def _flip(a):
    return jnp.flip(a, axis=1)


def _mlstm_bidir(q, k, v, ig, lf, C0, n0, m0):
    hf, (Cf, nf, mf) = _mlstm_scan(q, k, v, ig[:, :, 0], lf[:, :, 0], (C0[:, 0], n0[:, 0], m0[:, 0]))
    hb, (Cb, nb, mb) = _mlstm_scan(_flip(q), _flip(k), _flip(v), _flip(ig[:, :, 1]), _flip(lf[:, :, 1]),
                                   (C0[:, 1], n0[:, 1], m0[:, 1]))
    states = (jnp.stack([Cf, Cb], axis=1), jnp.stack([nf, nb], axis=1), jnp.stack([mf, mb], axis=1))
    return hf + _flip(hb), states


def _delta_bidir(q, k, v, g, beta, S0):
    of, Sf = _delta_scan(q, k, v, g[:, :, 0], beta[:, :, 0], S0[:, 0])
    ob, Sb = _delta_scan(_flip(q), _flip(k), _flip(v), _flip(g[:, :, 1]), _flip(beta[:, :, 1]), S0[:, 1])
    return of + _flip(ob), jnp.stack([Sf, Sb], axis=1)


def _mix(h, lw, ctx):
    B, T, _ = h.shape
    f32 = jnp.float32
    z = h @ lw['w_in']
    (mq, mk, mv, mo, mi, mf, gqkv, gz, ga, gb, aq, ak, av) = jnp.split(
        z, np.cumsum(SPLIT_SIZES)[:-1].tolist(), axis=-1)

    def heads(a, n):
        return a.reshape(B, T, n, HEAD_DIM)

    if ctx is None:
        C0 = jnp.zeros((B, N_DIR, M_HEADS, HEAD_DIM, HEAD_DIM), f32)
        n0 = jnp.zeros((B, N_DIR, M_HEADS, HEAD_DIM), f32)
        m0 = jnp.zeros((B, N_DIR, M_HEADS), f32)
        S0 = jnp.zeros((B, N_DIR, G_HEADS, HEAD_DIM, HEAD_DIM), f32)
    else:
        ck, cv, C0, n0, m0, S0 = ctx
        C0, n0, m0, S0 = C0.astype(f32), n0.astype(f32), m0.astype(f32), S0.astype(f32)

    m_q = heads(mq, M_HEADS).astype(f32)
    m_k = heads(mk, M_HEADS).astype(f32) / math.sqrt(HEAD_DIM)
    m_v = heads(mv, M_HEADS).astype(f32)
    ig = mi.reshape(B, T, N_DIR, M_HEADS).astype(f32)
    lf = jax.nn.log_sigmoid(mf.reshape(B, T, N_DIR, M_HEADS).astype(f32) + lw['m_fbias'].astype(f32))
    mh, (Cn, nn_, mn) = _mlstm_bidir(m_q, m_k, m_v, ig, lf, C0, n0, m0)
    m_out = (_rms(mh, lw['m_norm'].reshape(M_HEADS, HEAD_DIM)).reshape(B, T, M_WIDTH)
             * jax.nn.sigmoid(mo.astype(f32))).astype(h.dtype)

    qkv = jax.nn.silu(_short_conv(gqkv, lw['g_conv']))
    gq, gk, gv = jnp.split(qkv, 3, axis=-1)

    def l2(a):
        return a * lax.rsqrt(jnp.sum(a * a, axis=-1, keepdims=True) + EPS)

    d_q = l2(heads(gq, G_HEADS).astype(f32)) / math.sqrt(HEAD_DIM)
    d_k = l2(heads(gk, G_HEADS).astype(f32))
    d_v = heads(gv, G_HEADS).astype(f32)
    g_log = -jnp.exp(lw['g_alog'].astype(f32)) * jax.nn.softplus(
        ga.reshape(B, T, N_DIR, G_HEADS).astype(f32) + lw['g_dtb'].astype(f32))
    beta = jax.nn.sigmoid(gb.reshape(B, T, N_DIR, G_HEADS).astype(f32))
    go, Sn = _delta_bidir(d_q, d_k, d_v, g_log, beta, S0)
    g_out = (_rms(go, lw['g_norm']) * jax.nn.silu(heads(gz, G_HEADS).astype(f32))).reshape(
        B, T, G_WIDTH).astype(h.dtype)

    a_q = _rms(heads(aq, A_HEADS), lw['q_norm'])
    a_k = _rms(heads(ak, A_KV_HEADS), lw['k_norm'])
    a_v = heads(av, A_KV_HEADS)
    if ctx is None:
        keys, vals = a_k, a_v
    else:
        cos, sin = _axial_rope(T)
        a_q = _apply_rope(a_q, cos, sin)
        keys = jnp.concatenate([ck.astype(h.dtype), _apply_rope(a_k, cos, sin)], axis=1)
        vals = jnp.concatenate([cv.astype(h.dtype), a_v], axis=1)
    a_out = _attention(a_q, keys, vals)

    out = jnp.concatenate([m_out, g_out, a_out], axis=-1) @ lw['w_out']
    new_ctx = (a_k, a_v, Cn, nn_, mn, Sn) if ctx is None else None
    return out, new_ctx


def _layer(x, mod, lw, ctx):
    h = _rms(x, lw['norm'][0]) * (1 + mod[:, 0]) + mod[:, 1]
    x = x + 0.5 * mod[:, 2] * _swiglu(h, lw['ffn_in'][0], lw['ffn_out'][0])
    h = _rms(x, lw['norm'][1]) * (1 + mod[:, 3]) + mod[:, 4]
    mix, new_ctx = _mix(h, lw, ctx)
    x = x + mod[:, 5] * mix
    h = _rms(x, lw['norm'][2]) * (1 + mod[:, 6]) + mod[:, 7]
    x = x + 0.5 * mod[:, 8] * _swiglu(h, lw['ffn_in'][1], lw['ffn_out'][1])
    return x, new_ctx


def setup_inputs(seed: int = 0) -> dict:
    key = jax.random.key(seed)
    ks = jax.random.split(key, 26)
    f32 = jnp.float32

    def nrm(k, shape, s=1.0):
        return s * jax.random.normal(k, shape, f32)

    dt = jnp.exp(jax.random.uniform(ks[21], (DEPTH, N_DIR, G_HEADS), f32, math.log(1e-3), math.log(1e-1)))
    return {
        'x_prompt': nrm(ks[0], (BATCH, SEQ, D_MODEL)),
        'x_sample': nrm(ks[1], (DEC_BATCH, DEC_SEQ, D_MODEL)),
        'c': nrm(ks[2], (DEC_BATCH, D_MODEL)),
        'cache_k': nrm(ks[3], (DEC_BATCH, DEPTH, PAST_LEN, A_KV_HEADS, HEAD_DIM)),
        'cache_v': nrm(ks[4], (DEC_BATCH, DEPTH, PAST_LEN, A_KV_HEADS, HEAD_DIM)),
        'state_mlstm_C': nrm(ks[5], (DEC_BATCH, DEPTH, N_DIR, M_HEADS, HEAD_DIM, HEAD_DIM), 0.1),
        'state_mlstm_n': nrm(ks[6], (DEC_BATCH, DEPTH, N_DIR, M_HEADS, HEAD_DIM), 0.1),
        'state_mlstm_m': nrm(ks[7], (DEC_BATCH, DEPTH, N_DIR, M_HEADS)),
        'state_delta_S': nrm(ks[8], (DEC_BATCH, DEPTH, N_DIR, G_HEADS, HEAD_DIM, HEAD_DIM), 0.1),
        'c_ctx': nrm(ks[9], (D_MODEL,)),
        'ada_w': nrm(ks[10], (DEPTH, D_MODEL, N_MOD * D_MODEL), 0.5 * D_MODEL ** -0.5),
        'ada_b': nrm(ks[11], (DEPTH, N_MOD * D_MODEL), 0.01),
        'norm_g': 1.0 + nrm(ks[12], (DEPTH, 3, D_MODEL), 0.02),
        'ffn_w_in': nrm(ks[13], (DEPTH, 2, D_MODEL, 2 * D_FF), D_MODEL ** -0.5),
        'ffn_w_out': nrm(ks[14], (DEPTH, 2, D_FF, D_MODEL), D_FF ** -0.5),
        'w_in': nrm(ks[15], (DEPTH, D_MODEL, IN_WIDTH), D_MODEL ** -0.5),
        'w_out': nrm(ks[16], (DEPTH, MIX_WIDTH, D_MODEL), MIX_WIDTH ** -0.5),
        'mlstm_f_bias': 3.0 + 3.0 * jax.random.uniform(ks[17], (DEPTH, N_DIR, M_HEADS), f32),
        'mlstm_norm': 1.0 + nrm(ks[18], (DEPTH, M_WIDTH), 0.02),
        'delta_conv': nrm(ks[19], (DEPTH, CONV_W, 3 * G_WIDTH), CONV_W ** -0.5),
        'delta_a_log': jnp.log(jax.random.uniform(ks[20], (DEPTH, N_DIR, G_HEADS), f32, 1.0, 16.0)),
        'delta_dt_bias': dt + jnp.log(-jnp.expm1(-dt)),
        'delta_norm': 1.0 + nrm(ks[22], (DEPTH, HEAD_DIM), 0.02),
        'attn_q_norm': 1.0 + nrm(ks[23], (DEPTH, HEAD_DIM), 0.02),
        'attn_k_norm': 1.0 + nrm(ks[24], (DEPTH, HEAD_DIM), 0.02),
        'final_norm': 1.0 + nrm(ks[25], (D_MODEL,), 0.02),
    }


def reference(x_prompt, x_sample, c, cache_k, cache_v, state_mlstm_C, state_mlstm_n,
              state_mlstm_m, state_delta_S, c_ctx, ada_w, ada_b, norm_g, ffn_w_in,
              ffn_w_out, w_in, w_out, mlstm_f_bias, mlstm_norm, delta_conv, delta_a_log,
              delta_dt_bias, delta_norm, attn_q_norm, attn_k_norm, final_norm):
    xp, xs = x_prompt, x_sample
    ks, vs, Cs, ns, ms, Ss = [], [], [], [], [], []
    for l in range(DEPTH):
        lw = {
            'norm': norm_g[l], 'ffn_in': ffn_w_in[l], 'ffn_out': ffn_w_out[l],
            'w_in': w_in[l], 'w_out': w_out[l],
            'm_fbias': mlstm_f_bias[l], 'm_norm': mlstm_norm[l],
            'g_conv': delta_conv[l], 'g_alog': delta_a_log[l], 'g_dtb': delta_dt_bias[l],
            'g_norm': delta_norm[l], 'q_norm': attn_q_norm[l], 'k_norm': attn_k_norm[l],
        }
        xp, (k_l, v_l, C_l, n_l, m_l, S_l) = _layer(xp, _modulation(c_ctx[None, :], ada_w[l], ada_b[l]), lw, None)
        ks.append(k_l)
        vs.append(v_l)
        Cs.append(C_l)
        ns.append(n_l)
        ms.append(m_l)
        Ss.append(S_l)
        ctx = (cache_k[:, l], cache_v[:, l], state_mlstm_C[:, l], state_mlstm_n[:, l],
               state_mlstm_m[:, l], state_delta_S[:, l])
        xs, _ = _layer(xs, _modulation(c, ada_w[l], ada_b[l]), lw, ctx)
    y_prompt = _rms(xp, final_norm)
    y_sample = _rms(xs, final_norm)
    new_k = jnp.stack(ks, axis=1)
    new_v = jnp.stack(vs, axis=1)
    new_C = jnp.stack(Cs, axis=1)
    new_n = jnp.stack(ns, axis=1)
    new_m = jnp.stack(ms, axis=1)
    new_S = jnp.stack(Ss, axis=1)
    return (y_prompt, y_sample, new_k, new_v, new_C, new_n, new_m, new_S)
```

```python
import numpy as np
import concourse.bass as bass
import concourse.mybir as mybir
from contextlib import ExitStack

F32 = mybir.dt.float32
BF16 = mybir.dt.bfloat16
ALU = mybir.AluOpType
AF = mybir.ActivationFunctionType
AX = mybir.AxisListType

ENGS = ('pe', 'act', 'dve', 'pool', 'sp')
EPOCH = 16000


class Op:
    __slots__ = ('eng', 'idx', 'fn', 'waits', 'dwaits', 'signal', 'sigval', 'chan', 'ccount', 'known')


class Prog:
    def __init__(self, nc):
        self.nc = nc
        self.ops = {e: [] for e in ENGS}
        self.lastw = {}
        self.readers = {}
        self.seen = {e: {} for e in ENGS}
        self.dseen = {e: {} for e in ENGS}
        self.chan_count = {}
        self.stack = ExitStack()
        self.n_sb = 0
        self.bar = None
        self.rec = None

    def sb(self, name, shape, dtype=F32):
        return self.stack.enter_context(self.nc.sbuf_tensor("S_" + name, list(shape), dtype))

    def ps(self, name, shape, dtype=F32):
        return self.stack.enter_context(self.nc.psum_tensor("P_" + name, list(shape), dtype))

    def thread_begin(self):
        self.rec = []

    def thread_end(self):
        r = self.rec
        self.rec = None
        return r

    def interleave(self, threads):
        n = max(len(t) for t in threads)
        for i in range(n):
            for t in threads:
                if i < len(t):
                    self.add(*t[i][0], **t[i][1])

    def add(self, eng, fn, reads=(), writes=(), chan=None):
        if getattr(self, 'rec', None) is not None:
            self.rec.append(((eng, fn), dict(reads=list(reads), writes=list(writes), chan=chan)))
            return None
        op = Op()
        op.eng = eng
        op.idx = len(self.ops[eng])
        op.fn = fn
        op.waits = {}
        op.dwaits = {}
        op.signal = False
        op.sigval = 0
        op.chan = chan
        op.ccount = 0
        deps = []
        for k in reads:
            w = self.lastw.get(k)
            if w is not None:
                deps.append((w, False))
        for k in writes:
            w = self.lastw.get(k)
            if w is not None:
                deps.append((w, False))
            for r in self.readers.get(k, ()):
                deps.append((r, True))
        seen = self.seen[eng]
        dseen = self.dseen[eng]
        for d, is_war in deps:
            if d.chan is not None:
                if dseen.get(d.chan, 0) < d.ccount:
                    op.dwaits[d.chan] = max(op.dwaits.get(d.chan, 0), d.ccount)
                    dseen[d.chan] = d.ccount
                continue
            if d.eng == eng:
                if eng == 'pe':
                    continue
            if seen.get(d.eng, -1) >= d.idx:
                continue
            op.waits[d.eng] = max(op.waits.get(d.eng, -1), d.idx)
        if self.bar is not None and seen.get('dve', -1) < self.bar.idx:
            op.waits['dve'] = max(op.waits.get('dve', -1), self.bar.idx)
        for e2, idx in op.waits.items():
            dop = self.ops[e2][idx]
            dop.signal = True
            if seen.get(e2, -1) < idx:
                seen[e2] = idx
            for e3, i3 in dop.known.items():
                if e3 != eng and seen.get(e3, -1) < i3:
                    seen[e3] = i3
        op.known = dict(seen)
        if chan is not None:
            c = self.chan_count.get(chan, 0) + 1
            self.chan_count[chan] = c
            op.ccount = c
        self.ops[eng].append(op)
        for k in writes:
            self.lastw[k] = op
            self.readers[k] = []
        for k in reads:
            self.readers.setdefault(k, []).append(op)
        return op

    def barrier(self, fn):
        self.bar = None
        op = self.add('dve', fn)
        for e in ENGS:
            if e == 'dve':
                continue
            for o in reversed(self.ops[e]):
                if o.chan is None and o.fn is not None:
                    if self.seen['dve'].get(e, -1) < o.idx:
                        op.waits[e] = o.idx
                        o.signal = True
                        self.seen['dve'][e] = o.idx
                    break
        for ch, cnt in self.chan_count.items():
            if self.dseen['dve'].get(ch, 0) < cnt:
                op.dwaits[ch] = cnt
                self.dseen['dve'][ch] = cnt
        op.known = dict(self.seen['dve'])
        self.bar = op
        return op

    def final_wait(self, eng='sp'):
        op = Op()
        op.eng = eng
        op.idx = len(self.ops[eng])
        op.fn = None
        op.waits = {}
        op.dwaits = dict(self.chan_count)
        op.signal = False
        op.sigval = 0
        op.chan = None
        op.ccount = 0
        op.known = {}
        self.ops[eng].append(op)

    def emit(self):
        nc = self.nc
        nsig = {}
        for e in ENGS:
            c = 0
            for op in self.ops[e]:
                if op.chan is None and op.signal:
                    c += 1
                    op.sigval = c
            nsig[e] = c
        sems = {}
        for e in ENGS:
            for ep in range((nsig[e] + EPOCH - 1) // EPOCH):
                sems[(e, ep)] = self.stack.enter_context(nc.semaphore(f"s_{e}_{ep}"))
        csems = {}
        for ch in self.chan_count:
            csems[ch] = self.stack.enter_context(nc.semaphore(f"c_{ch}"))
        self.n_sems = len(sems) + len(csems)

        def sem_of(e, sigval):
            ep = (sigval - 1) // EPOCH
            return sems[(e, ep)], (sigval - 1) % EPOCH + 1

        def replay(ename, eobj):
            for op in self.ops[ename]:
                for e2, idx in op.waits.items():
                    s, v = sem_of(e2, self.ops[e2][idx].sigval)
                    eobj.wait_ge(s, v)
                for ch, cnt in op.dwaits.items():
                    eobj.wait_ge(csems[ch], 16 * cnt)
                if op.fn is None:
                    continue
                ins = op.fn(eobj)
                if op.chan is not None:
                    ins.then_inc(csems[op.chan], 16)
                elif op.signal:
                    s, v = sem_of(ename, op.sigval)
                    ins.then_inc(s, 1)

        with nc.Block() as block:
            @block.tensor
            def _(e):
                replay('pe', e)

            @block.scalar
            def _(e):
                replay('act', e)

            @block.vector
            def _(e):
                replay('dve', e)

            @block.gpsimd
            def _(e):
                replay('pool', e)

            @block.sync
            def _(e):
                replay('sp', e)
        self.stack.close()

from concourse.bass_utils import run_bass_kernel_spmd

D = 1024
NTOK = 2560
TT = 512
NT = 5
DFF = 2816
EPS = 1e-6
SEQS = [(0, 256, False), (256, 256, False), (512, 2048, True)]
INW = 2848


class K:
    pass


def build(cfg):
    nc = bass.Bass("TRN2", target_bir_lowering=False)
    P = Prog(nc)
    k = K()
    k.nc, k.P, k.cfg = nc, P, cfg
    L = cfg.get('layers', 2)

    def din(name, shape):
        return nc.dram_tensor(name, list(shape), F32, kind="ExternalInput").ap()

    def dout(name, shape):
        return nc.dram_tensor(name, list(shape), F32, kind="ExternalOutput").ap()

    k.xT = din("xT", [D, NTOK])
    k.condT = din("condT", [128, 8, 2])
    k.ada_w = din("ada_w", [2, D, 9 * D])
    k.ada_bT = din("ada_bT", [2, 128, 72])
    k.norm_gT = din("norm_gT", [128, 2, 3, 8])
    k.final_gT = din("final_gT", [128, 8])
    k.ffn_w_in = din("ffn_w_in", [2, 2, D, 2 * DFF])
    k.ffn_w_out = din("ffn_w_out", [2, 2, DFF, D])
    k.identD = din("ident", [128, 128])
    k.yT = dout("yT", [D, NTOK])
    if cfg.get('mix', True):
        mix_decl(k, din, dout)

    k.x = P.sb("x", [128, 8, NTOK], F32)
    k.h = P.sb("h", [128, 8, NTOK], BF16)
    k.ident = P.sb("identS", [128, 128], F32)
    k.ones_bf = P.sb("ones_bf", [128, 128], BF16)
    k.modT = P.sb("modT", [128, 72, 2], F32)
    k.dv = P.sb("dv", [128, 2, 6, 8], F32)
    k.ngT = P.sb("ngT", [128, 2, 3, 8], F32)
    k.fgT = P.sb("fgT", [128, 8], F32)
    k.abT = P.sb("abT", [128, 2, 72], F32)
    k.cond = P.sb("cond", [128, 8, 2], F32)
    k.sc = P.sb("sc", [128, 8, 2], BF16)
    k.rstd = P.sb("rstd", [128, 512], F32)
    k.row = k.rstd
    k.sq = [P.sb(f"sq{i}", [128, 512], BF16) for i in range(2)]
    k.tmp = [P.sb(f"tmp{i}", [128, 512], F32) for i in range(2)]
    k.W8 = P.sb("W8", [128, 4, 8, 256], BF16)
    k.wg = [k.W8[:, 2 * i] for i in range(2)]
    k.wu = [k.W8[:, 2 * i + 1] for i in range(2)]
    k.aux = P.sb("aux", [128, 3584], F32)
    auxb = k.aux[:].bitcast(BF16)
    k.wo = [auxb[:, i * 2048:(i + 1) * 2048].rearrange("p (a b) -> p a b", a=2) for i in range(2)]
    k.actt = [auxb[:, 4096 + i * 1024:4096 + (i + 1) * 1024].rearrange("p (a b) -> p a b", a=2) for i in range(2)]
    k.aux_ptr = 0
    k.sg = [P.sb(f"sg{i}", [128, 512], F32) for i in range(2)]
    k.psb = [P.ps(f"pb{i}", [128, 512], F32) for i in range(8)]
    k.bank_i = 0
    k.cnt = {}

    k.reserved = set()

    k.bank_pool = None
    k.tid = ''

    def bank():
        if k.bank_pool is not None:
            lo, n = k.bank_pool
            c = k.cnt.get(('bp', lo), 0)
            k.cnt[('bp', lo)] = c + 1
            return lo + c % n
        b = k.bank_i
        while b in k.reserved:
            b = (b + 1) % 8
        k.bank_i = (b + 1) % 8
        return b
    k.bank = bank

    def rot(name, n):
        c = k.cnt.get(name, 0)
        k.cnt[name] = c + 1
        return c % n
    k.rot = rot

    P.add('sp', lambda e: e.dma_start(out=k.ident[:], in_=k.identD), writes=['ident'], chan='c_ident')
    P.add('sp', lambda e: e.dma_start(out=k.ngT[:], in_=k.norm_gT), writes=['ngT'], chan='c_ngT')
    P.add('sp', lambda e: e.dma_start(out=k.fgT[:], in_=k.final_gT), writes=['fgT'], chan='c_fgT')
    P.add('sp', lambda e: e.dma_start(out=k.abT[:], in_=k.ada_bT.rearrange("l p c -> p l c")), writes=['abT'], chan='c_abT')
    P.add('sp', lambda e: e.dma_start(out=k.cond[:], in_=k.condT), writes=['cond'], chan='c_cond')
    for kk in range(8):
        P.add('sp', lambda e, kk=kk: e.dma_start(out=k.x[:, kk, :], in_=k.xT[kk * 128:(kk + 1) * 128, :]),
              writes=[f'x{tt}_{kk}' for tt in range(NT)], chan=f'c_xin{kk}')
    P.add('dve', lambda e: e.memset(k.ones_bf[:], 1.0), writes=['ones_bf'])
    P.add('act', lambda e: e.activation(k.sc[:], k.cond[:], AF.Silu), reads=['cond'], writes=['sc'])
    if cfg.get('mix', True):
        mix_setup(k)

    for l in range(L):
        mod_phase(k, l)
        ffn_phase(k, l, 0, 0, 0)
        if cfg.get('mix', True):
            mixer_phase(k, l)
        ffn_phase(k, l, 1, 2, 2)
    final_phase(k)
    P.final_wait('sp')
    P.emit()
    return nc


def tsl(tt):
    return slice(tt * TT, (tt + 1) * TT)


def mod_phase(k, l):
    P = k.P
    awv = k.ada_w[l].rearrange("(kk p) c -> p kk c", p=128)
    for pc in range(18):
        s = k.rot('w8', 2)
        P.add('pool', lambda e, s=s, pc=pc: e.dma_start(out=k.wg[s], in_=awv[:, :, pc * 512:pc * 512 + 256]),
              writes=[f'wg{s}'], chan=f'c_wg{s}')
        P.add('pool', lambda e, s=s, pc=pc: e.dma_start(out=k.wu[s], in_=awv[:, :, pc * 512 + 256:pc * 512 + 512]),
              writes=[f'wu{s}'], chan=f'c_wu{s}')
        pb = k.bank()
        for half, wt, wk in ((0, k.wg[s], f'wg{s}'), (1, k.wu[s], f'wu{s}')):
            for kk in range(8):
                P.add('pe', lambda e, wt=wt, kk=kk, pb=pb, half=half: e.matmul(
                    k.psb[pb][0:2, half * 256:(half + 1) * 256], k.sc[:, kk, :], wt[:, kk, :],
                    start=(kk == 0), stop=(kk == 7)), reads=[wk, 'sc'], writes=[f'ps{pb}'])
        P.add('dve', lambda e, pb=pb: e.tensor_copy(k.row[0:2, :], k.psb[pb][0:2, :]), reads=[f'ps{pb}'], writes=['rstd'])
        pb2 = k.bank()
        for i in range(4):
            P.add('pe', lambda e, i=i, pb2=pb2: e.transpose(k.psb[pb2][:, 2 * i:2 * i + 2], k.row[0:2, i * 128:(i + 1) * 128],
                                                           k.ident[0:2, 0:2]), reads=['rstd', 'ident'], writes=[f'ps{pb2}'])
        P.add('dve', lambda e, pc=pc, pb2=pb2: e.tensor_tensor(
            k.modT[:, pc * 4:(pc + 1) * 4, :], k.psb[pb2][:, 0:8].rearrange("p (c j) -> p c j", j=2),
            k.abT[:, l, pc * 4:(pc + 1) * 4].unsqueeze(2).broadcast_to([128, 4, 2]), ALU.add),
            reads=[f'ps{pb2}', 'abT'], writes=['modT'])
    m4 = k.modT[:].rearrange("p (i kk) j -> p i kk j", kk=8)
    for j in range(2):
        for n in range(3):
            P.add('dve', lambda e, j=j, n=n: e.scalar_tensor_tensor(
                k.dv[:, j, n, :], m4[:, 3 * n, :, j], 1.0, k.ngT[:, l, n, :], ALU.add, ALU.mult),
                reads=['modT', 'ngT'], writes=['dv'])
            sc_ = 1.0 if n == 1 else 0.5
            P.add('dve', lambda e, j=j, n=n, sc_=sc_: e.tensor_scalar_mul(
                k.dv[:, j, 3 + n, :], m4[:, 3 * n + 2, :, j], sc_),
                reads=['modT'], writes=['dv'])


def norm_phase(k, l, n, out_h=True):
    P = k.P
    for tt in range(NT):
        j = 0 if tt == 0 else 1
        pb = k.bank()
        for kk in range(8):
            s = k.rot('sq', 2)
            P.add('act', lambda e, s=s, kk=kk, tt=tt: e.activation(k.sq[s][:], k.x[:, kk, tsl(tt)], AF.Square),
                  reads=[f'x{tt}_{kk}'], writes=[f'sq{s}'])
            P.add('pe', lambda e, s=s, kk=kk, pb=pb: e.matmul(k.psb[pb][:], k.ones_bf[:], k.sq[s][:], start=(kk == 0), stop=(kk == 7)),
                  reads=[f'sq{s}', 'ones_bf'], writes=[f'ps{pb}'])
        P.add('act', lambda e, pb=pb: e.activation(k.rstd[:], k.psb[pb][:], AF.Sqrt, bias=EPS, scale=1.0 / D),
              reads=[f'ps{pb}'], writes=['rstd'])
        P.add('dve', lambda e: e.reciprocal(k.rstd[:], k.rstd[:]), reads=['rstd'], writes=['rstd'])
        for kk in range(8):
            s = k.rot('tmp', 2)
            P.add('dve', lambda e, s=s, kk=kk, tt=tt: e.tensor_tensor(k.tmp[s][:], k.x[:, kk, tsl(tt)], k.rstd[:], ALU.mult),
                  reads=[f'x{tt}_{kk}', 'rstd'], writes=[f'tmp{s}'])
            if out_h:
                P.add('act', lambda e, s=s, kk=kk, tt=tt, j=j: e.activation(
                    k.h[:, kk, tsl(tt)], k.tmp[s][:], AF.Identity,
                    bias=k.modT[:, (3 * n + 1) * 8 + kk, j:j + 1], scale=k.dv[:, j, n, kk:kk + 1]),
                    reads=[f'tmp{s}', 'dv', 'modT'], writes=[f'h{tt}_{kk}'])
            else:
                s2 = k.rot('sg', 2)
                P.add('act', lambda e, s=s, s2=s2, kk=kk: e.activation(
                    k.sg[s2][:], k.tmp[s][:], AF.Identity, bias=0.0, scale=k.fgT[:, kk:kk + 1]),
                    reads=[f'tmp{s}', 'fgT'], writes=[f'sg{s2}'])
                P.add('sp', lambda e, s2=s2, kk=kk, tt=tt: e.dma_start(out=k.yT[kk * 128:(kk + 1) * 128, tsl(tt)], in_=k.sg[s2][:]),
                      reads=[f'sg{s2}'], chan=f'c_y{s2}')


def final_phase(k):
    norm_phase(k, 0, 0, out_h=False)


def ffn_phase(k, l, i, n, gi):
    P = k.P
    norm_phase(k, l, n)
    wiv = k.ffn_w_in[l, i].rearrange("(kk p) c -> p kk c", p=128)
    wov = k.ffn_w_out[l, i].rearrange("(kk p) c -> p kk c", p=128)
    for g in range(11):
        s = k.rot('w8', 2)
        so = k.rot('wo', 2)
        P.add('pool', lambda e, s=s, g=g: e.dma_start(out=k.wg[s], in_=wiv[:, :, g * 256:(g + 1) * 256]),
              writes=[f'wg{s}'], chan=f'c_wg{s}')
        P.add('pool', lambda e, s=s, g=g: e.dma_start(out=k.wu[s], in_=wiv[:, :, DFF + g * 256:DFF + (g + 1) * 256]),
              writes=[f'wu{s}'], chan=f'c_wu{s}')
        P.add('pool', lambda e, so=so, g=g: e.dma_start(out=k.wo[so], in_=wov[:, 2 * g:2 * g + 2, :]),
              writes=[f'wo{so}'], chan=f'c_wo{so}')
        for tt in range(NT):
            j = 0 if tt == 0 else 1
            bg = [k.bank(), k.bank()]
            bu = [k.bank(), k.bank()]
            for c in range(2):
                for wt, wk, bb in ((k.wg[s], f'wg{s}', bg[c]), (k.wu[s], f'wu{s}', bu[c])):
                    for kk in range(8):
                        P.add('pe', lambda e, wt=wt, kk=kk, bb=bb, c=c, tt=tt: e.matmul(
                            k.psb[bb][:], wt[:, kk, c * 128:(c + 1) * 128], k.h[:, kk, tsl(tt)],
                            start=(kk == 0), stop=(kk == 7)), reads=[wk, f'h{tt}_{kk}'], writes=[f'ps{bb}'])
            sa = k.rot('actt', 2)
            for c in range(2):
                s2 = k.rot('sg', 2)
                P.add('act', lambda e, s2=s2, c=c, bg=bg: e.activation(k.sg[s2][:], k.psb[bg[c]][:], AF.Silu),
                      reads=[f'ps{bg[c]}'], writes=[f'sg{s2}'])
                P.add('dve', lambda e, s2=s2, c=c, bu=bu, sa=sa: e.tensor_tensor(k.actt[sa][:, c, :], k.sg[s2][:], k.psb[bu[c]][:], ALU.mult),
                      reads=[f'sg{s2}', f'ps{bu[c]}'], writes=[f'actt{sa}_{c}'])
            for o in range(8):
                bo = k.bank()
                for c in range(2):
                    P.add('pe', lambda e, o=o, c=c, bo=bo, so=so, sa=sa: e.matmul(
                        k.psb[bo][:], k.wo[so][:, c, o * 128:(o + 1) * 128], k.actt[sa][:, c, :],
                        start=(c == 0), stop=(c == 1)), reads=[f'wo{so}', f'actt{sa}_{c}'], writes=[f'ps{bo}'])
                P.add('dve', lambda e, o=o, bo=bo, tt=tt, j=j: e.scalar_tensor_tensor(
                    k.x[:, o, tsl(tt)], k.psb[bo][:], k.dv[:, j, 3 + gi, o:o + 1], k.x[:, o, tsl(tt)], ALU.mult, ALU.add),
                    reads=[f'ps{bo}', 'dv', f'x{tt}_{o}'], writes=[f'x{tt}_{o}'])

ZQ, ZK, ZV, ZO = 0, 256, 512, 768
ZGQ, ZGK, ZGV, ZGZ = 1024, 1280, 1536, 1792
ZAQ, ZAK, ZAV = 2048, 2560, 2688
ZMI, ZMF, ZGA, ZGB = 2816, 2824, 2832, 2840
PAST = 512


def mix_decl(k, din, dout):
    nc = k.nc
    k.w_in = din("w_in_p", [2, D, INW])
    k.w_out = din("w_out", [2, D, D])
    k.ckT = din("ckT", [2, 128, PAST])
    k.cv = din("cv", [2, PAST, 128])
    k.C0 = din("C0", [2, 2, 128, 2, 65])
    k.m0 = din("m0", [2, 2, 4])
    k.S0 = din("S0", [2, 2, 128, 2, 64])
    k.gpar = din("gpar", [128, 2, 2, 3])
    k.fpar = din("fpar", [128, 2, 5])
    k.convw = din("convw", [128, 2, 6, 5])
    k.cosD = din("cosT", [128, 2048])
    k.sinD = din("sinT", [128, 2048])
    k.permD = din("perm", [128, 128])
    k.masksD = din("masks", [128, 10, 64])
    k.selD = din("sel", [128, 32, 4])
    k.bonesD = din("bones", [128, 128])
    if k.cfg.get('dbg'):
        k.zT = dout("zT_scr", [INW, NTOK])
        k.kTaD = nc.dram_tensor("kTaD", [128, PAST + 2048], BF16, kind="ExternalOutput").ap()
        k.QTD = nc.dram_tensor("QTD", [128, 4, 2048], BF16, kind="ExternalOutput").ap()
        k.VaD = nc.dram_tensor("VaD", [128, 2600], BF16, kind="ExternalOutput").ap()
        k.FBD = dout("FBD", [128, 6, 2048])
        k.DD = dout("DD", [16, 128, 2, 64])
        k.GSD = dout("GSD", [128, 6, 2])
        k.RdD = dout("RdD", [2, 128, 6, 64])
        k.RmD = dout("RmD", [2, 128, 6, 64])
    else:
        k.zT = nc.dram_tensor("zT_scr", [INW, NTOK], F32).ap()
    k.xpark = nc.dram_tensor("xpark", [128, 8 * NTOK], F32).ap()
    k.nkT = dout("nkT", [2, 128, 512])
    k.nvT = dout("nvT", [2, 128, 512])
    k.Cout = dout("Cout", [2, 2, 2, 128, 2, 65])
    k.Sout = dout("Sout", [2, 2, 2, 128, 2, 64])
    k.mout = dout("mout", [2, 2, 2, 4])
    if k.cfg.get('dbg'):
        k.catD = nc.dram_tensor("catD", [128, 8, NTOK], BF16, kind="ExternalOutput").ap()


def mix_setup(k):
    P = k.P
    k.wout = k.W8[:].rearrange("p a b c -> p (a b c)").rearrange("p (kk c) -> p kk c", kk=8)
    k.gparS = P.sb("gparS", [128, 2, 2, 3], F32)
    k.negA = P.sb("negA", [128, 2, 2], F32)
    k.fparS = P.sb("fparS", [128, 2, 5], F32)
    k.convS = P.sb("convS", [128, 2, 6, 5], F32)
    k.permS = P.sb("permS", [128, 128], F32)
    k.masks = P.sb("masks", [128, 10, 64], F32)
    k.sel = P.sb("sel", [128, 32, 4], F32)
    k.bones = P.sb("bones", [128, 128], F32)
    for nm, t, dsrc in (("gparS", k.gparS, k.gpar), ("fparS", k.fparS, k.fpar), ("convS", k.convS, k.convw),
                        ("permS", k.permS, k.permD),
                        ("masks", k.masks, k.masksD), ("sel", k.sel, k.selD), ("bones", k.bones, k.bonesD)):
        P.add('sp', lambda e, t=t, dsrc=dsrc: e.dma_start(out=t[:], in_=dsrc), writes=[nm], chan='c_' + nm)
    P.add('act', lambda e: e.activation(k.negA[:], k.gparS[:, :, :, 1], AF.Exp), reads=['gparS'], writes=['negA'])
    P.add('dve', lambda e: e.tensor_scalar_mul(k.negA[:], k.negA[:], -1.0), reads=['negA'], writes=['negA'])
    xb = k.x[:].rearrange("p a b -> p (a b)")
    k.FB = xb[:, 0:12288].rearrange("p (a b) -> p a b", a=6)
    k.HF = xb[:, 12288:16384].rearrange("p (a b) -> p a b", a=2)
    k.QT = xb[:, 16384:20480].bitcast(BF16).rearrange("p (a b) -> p a b", a=4)
    k.cosS = k.FB[:, 0, :]
    k.sinS = k.FB[:, 1, :]
    k.kTa = k.FB[:, 2, :].bitcast(BF16)[:, 0:PAST + 2048]
    k.Va = k.FB[:, 3, :].bitcast(BF16)[:, 0:2600].rearrange("p (a g e) -> p a g e", a=20, g=2)
    k.PT = [k.FB[:, 4, :].bitcast(BF16)[:, i * 512:(i + 1) * 512] for i in range(2)]
    k.stg = [k.HF[:, i, :] for i in range(2)]
    k.Gin = [P.sb(f"Gin{d}", [128, 4, 64], F32) for d in range(2)]
    k.Rm = [P.sb(f"Rm{d}", [128, 6, 64], F32) for d in range(2)]
    k.Rd = [P.sb(f"Rd{d}", [128, 6, 64], F32) for d in range(2)]
    k.gw = [P.sb(f"gw{i}", [128, 64], F32) for i in range(6)]
    k.ST = P.sb("ST", [128, 8], F32)
    k.PC = P.sb("PC", [128, 8], F32)
    k.RW = P.sb("RW", [1, 3, 128], F32)
    k.MM = [P.sb(f"MM{d}", [1, 4, 33], F32) for d in range(2)]
    k.MR = P.sb("MR", [1, 3, 4, 32], F32)
    k.m0t = [P.sb(f"m0t{d}", [1, 4], F32) for d in range(2)]
    k.Cst = [P.sb(f"Cst{d}", [128, 2, 65], F32) for d in range(2)]
    k.Sst = [P.sb(f"Sst{d}", [128, 2, 64], F32) for d in range(2)]
    k.wk = {}
    k.identb = P.sb("identb", [128, 128], BF16)
    P.add('dve', lambda e: e.tensor_copy(k.identb[:], k.ident[:]), reads=['ident'], writes=['identb'])

    def work(name, shape, dtype=F32, n=None):
        if n is None:
            n = 2 if name in ('GS', 'RX') else 1
        name = name + k.tid
        if name not in k.wk:
            lst_ = []
            for i in range(n):
                sz = int(np.prod(shape[1:]))
                if dtype == BF16:
                    sz = (sz + 1) // 2
                if k.aux_ptr + sz <= 3072:
                    v = k.aux[:, k.aux_ptr:k.aux_ptr + sz]
                    if dtype == BF16:
                        v = v.bitcast(BF16)[:, 0:int(np.prod(shape[1:]))]
                    k.aux_ptr += sz
                    if len(shape) == 3:
                        v = v.rearrange("p (a b) -> p a b", a=shape[1])
                    elif len(shape) == 4:
                        v = v.rearrange("p (a b c) -> p a b c", a=shape[1], b=shape[2])
                    lst_.append(v)
                else:
                    lst_.append(P.sb(f"wk_{name}{i}", shape, dtype))
            k.wk[name] = lst_
        lst = k.wk[name]
        i = k.rot('wk_' + name, len(lst))
        return lst[i], f'wk_{name}{i}'
    k.work = work


def MM(e, out, lhsT, rhs, start=True, stop=True):
    kp = lhsT.partition_size()
    mp = out.partition_size()
    if kp < 128:
        return e.matmul(out, lhsT, rhs, start=start, stop=stop, tile_position=(lhsT.base_partition(), out.base_partition()))
    return e.matmul(out, lhsT, rhs, start=start, stop=stop)


def zkeys(blk, seq):
    tok0, T, samp = seq
    return [f'zT{blk}_{tt}' for tt in range(tok0 // TT, (tok0 + T + TT - 1) // TT)]


def head_sl(h):
    return slice(64 * (h % 2), 64 * (h % 2) + 64), h // 2


def cumsum64(k, src, skey, d):
    P = k.P
    cur, ckey = src, skey
    for si, s in enumerate((1, 2, 4, 8, 16, 32)):
        dst = k.gw[4 + (si % 2)][:]
        dkey = f'gw{4 + (si % 2)}'
        P.add('act', lambda e, dst=dst, cur=cur: e.activation(dst, cur, AF.Identity), reads=[ckey], writes=[dkey])
        if d == 0:
            P.add('dve', lambda e, dst=dst, cur=cur, s=s: e.tensor_tensor(dst[:, s:64], cur[:, s:64], cur[:, 0:64 - s], ALU.add),
                  reads=[ckey, dkey], writes=[dkey])
        else:
            P.add('dve', lambda e, dst=dst, cur=cur, s=s: e.tensor_tensor(dst[:, 0:64 - s], cur[:, 0:64 - s], cur[:, s:64], ALU.add),
                  reads=[ckey, dkey], writes=[dkey])
        cur, ckey = dst, dkey
    return cur, ckey


def gate_prepass(k, l, seq, d, m0_src):
    P = k.P
    tok0, T, samp = seq
    nch = T // 64
    G = k.Gin[d]
    RmF, RdF = k.Rm[d], k.Rd[d]
    Rm, Rd = RmF[:, :, 0:64], RdF[:, :, 0:64]
    last = 63 if d == 0 else 0
    P.add('dve', lambda e: e.memset(G[:], 0.0), writes=[f'Gin{d}'])
    for q, row0 in enumerate((ZMI, ZMF, ZGA, ZGB)):
        for h in range(4):
            r = row0 + d * 4 + h
            src = k.zT[r:r + 1, tok0:tok0 + T].rearrange("o (c t) -> (o c) t", t=64)
            P.add('sp', lambda e, q=q, h=h, src=src: e.dma_start(out=G[32 * h:32 * h + nch, q, :], in_=src),
                  reads=zkeys(22, seq), writes=[f'Gin{d}'], chan=f'c_gin{d}')
    gp = k.gparS
    gw = k.gw
    P.add('act', lambda e: e.activation(gw[0][:], G[:, 1, :], AF.Exp, bias=gp[:, l, d, 0:1], scale=-1.0),
          reads=[f'Gin{d}', 'gparS'], writes=['gw0'])
    P.add('act', lambda e: e.activation(gw[0][:], gw[0][:], AF.Ln, bias=1.0, scale=1.0), reads=['gw0'], writes=['gw0'])
    P.add('dve', lambda e: e.tensor_scalar_mul(Rm[:, 0, :], gw[0][:], -1.0), reads=['gw0'], writes=[f'Rm{d}'])
    b, bkey = cumsum64(k, Rm[:, 0, :], f'Rm{d}', d)
    ST = k.ST
    P.add('dve', lambda e: e.scalar_tensor_tensor(gw[1][:], G[:, 0, :], b[:, last:last + 1], b, ALU.add, ALU.subtract),
          reads=[f'Gin{d}', bkey], writes=['gw1'])
    P.add('dve', lambda e: e.tensor_copy(ST[:, 0:1], b[:, last:last + 1]), reads=[bkey], writes=['ST'])
    P.add('dve', lambda e: e.tensor_reduce(ST[:, 1:2], gw[1][:], AX.X, ALU.max), reads=['gw1'], writes=['ST'])
    P.add('dve', lambda e: e.tensor_reduce(ST[:, 2:3], G[:, 0, :], AX.X, ALU.max), reads=[f'Gin{d}'], writes=['ST'])
    pb = k.bank()
    for q in range(3):
        P.add('pe', lambda e, q=q, pb=pb: e.transpose(k.psb[pb][0:1, q * 128:(q + 1) * 128], ST[:, q:q + 1], k.ident[:]),
              reads=['ST', 'ident'], writes=[f'ps{pb}'])
    P.add('dve', lambda e, pb=pb: e.tensor_copy(k.RW[:].rearrange("o q n -> o (q n)"), k.psb[pb][0:1, 0:384]),
          reads=[f'ps{pb}'], writes=['RW'])
    RW4 = k.RW[:].rearrange("o q (h c) -> o q h c", h=4)
    MM = k.MM[d]
    mk = f'MM{d}'
    init_c = 0 if d == 0 else nch
    if m0_src is None:
        P.add('dve', lambda e: e.memset(MM[:], 0.0), writes=[mk])
    else:
        P.add('dve', lambda e: e.memset(MM[:], 0.0), writes=[mk])
        P.add('sp', lambda e: e.dma_start(out=k.m0t[d][:], in_=m0_src), writes=[f'm0t{d}'], chan=f'c_m0{d}')
        P.add('dve', lambda e: e.tensor_copy(MM[:, :, init_c], k.m0t[d][:]), reads=[f'm0t{d}'], writes=[mk])
    for s in range(nch):
        c = s if d == 0 else nch - 1 - s
        cb, ca = (c, c + 1) if d == 0 else (c + 1, c)
        P.add('dve', lambda e, c=c, cb=cb, ca=ca: e.tensor_tensor(MM[:, :, ca], MM[:, :, cb], RW4[:, 0, :, c], ALU.add),
              reads=[mk, 'RW'], writes=[mk])
        P.add('dve', lambda e, c=c, ca=ca: e.tensor_tensor(MM[:, :, ca], MM[:, :, ca], RW4[:, 1, :, c], ALU.max),
              reads=[mk, 'RW'], writes=[mk])
    MR = k.MR
    bsl, asl = (slice(0, nch), slice(1, nch + 1)) if d == 0 else (slice(1, nch + 1), slice(0, nch))
    P.add('dve', lambda e: e.memset(MR[:], 0.0), writes=['MR'])
    P.add('dve', lambda e: e.tensor_copy(MR[:, 0, :, 0:nch], MM[:, :, bsl]), reads=[mk], writes=['MR'])
    P.add('dve', lambda e: e.tensor_copy(MR[:, 1, :, 0:nch], MM[:, :, asl]), reads=[mk], writes=['MR'])
    P.add('dve', lambda e: e.tensor_tensor(MR[:, 2, :, 0:nch], MM[:, :, bsl], RW4[:, 2, :, 0:nch], ALU.max),
          reads=[mk, 'RW'], writes=['MR'])
    pb = k.bank()
    for q in range(3):
        P.add('pe', lambda e, q=q, pb=pb: e.transpose(k.psb[pb][:, q:q + 1], MR[:, q, :, :].rearrange("o h c -> o (h c)"),
                                                     k.ident[0:1, 0:1]), reads=['MR', 'ident'], writes=[f'ps{pb}'])
    PC = k.PC
    P.add('dve', lambda e, pb=pb: e.tensor_copy(PC[:, 0:3], k.psb[pb][:, 0:3]), reads=[f'ps{pb}'], writes=['PC'])
    P.add('dve', lambda e: e.tensor_tensor(PC[:, 3:4], PC[:, 0:1], PC[:, 2:3], ALU.subtract), reads=['PC'], writes=['PC'])
    P.add('dve', lambda e: e.tensor_tensor(PC[:, 4:5], PC[:, 0:1], PC[:, 1:2], ALU.subtract), reads=['PC'], writes=['PC'])
    P.add('dve', lambda e: e.tensor_tensor(PC[:, 4:5], PC[:, 4:5], ST[:, 0:1], ALU.add), reads=['PC', 'ST'], writes=['PC'])
    rk = f'Rm{d}'
    P.add('dve', lambda e: e.tensor_scalar(Rm[:, 1, :], G[:, 0, :], PC[:, 2:3], 0.0, ALU.subtract, ALU.add), reads=[f'Gin{d}', 'PC'], writes=[rk])
    P.add('dve', lambda e: e.tensor_scalar(Rm[:, 2, :], b, PC[:, 3:4], 0.0, ALU.add, ALU.add), reads=[bkey, 'PC'], writes=[rk])
    P.add('dve', lambda e: e.tensor_scalar(Rm[:, 3, :], gw[1][:], PC[:, 1:2], 0.0, ALU.subtract, ALU.add), reads=['gw1', 'PC'], writes=[rk])
    P.add('dve', lambda e: e.tensor_scalar(Rm[:, 4, :], b, 0.0, PC[:, 2:3], ALU.mult, ALU.subtract), reads=[bkey, 'PC'], writes=[rk])
    P.add('dve', lambda e: e.tensor_scalar(Rm[:, 5, :], b, 0.0, PC[:, 4:5], ALU.mult, ALU.add), reads=[bkey, 'PC'], writes=[rk])
    P.add('act', lambda e: e.activation(Rm[:, 2:6, :], Rm[:, 2:6, :], AF.Exp), reads=[rk], writes=[rk])
    dk = f'Rd{d}'
    P.add('act', lambda e: e.activation(gw[2][:], G[:, 2, :], AF.Exp, bias=gp[:, l, d, 2:3], scale=1.0),
          reads=[f'Gin{d}', 'gparS'], writes=['gw2'])
    P.add('act', lambda e: e.activation(gw[2][:], gw[2][:], AF.Ln, bias=1.0, scale=1.0), reads=['gw2'], writes=['gw2'])
    P.add('dve', lambda e: e.tensor_scalar(Rd[:, 0, :], gw[2][:], k.negA[:, l, d:d + 1], 0.0, ALU.mult, ALU.add), reads=['gw2', 'negA'], writes=[dk])
    P.add('act', lambda e: e.activation(Rd[:, 1, :], G[:, 3, :], AF.Sigmoid), reads=[f'Gin{d}'], writes=[dk])
    gc, gckey = cumsum64(k, Rd[:, 0, :], dk, d)
    P.add('act', lambda e: e.activation(Rd[:, 2, :], gc, AF.Exp), reads=[gckey], writes=[dk])
    P.add('act', lambda e: e.activation(Rd[:, 3, :], gc, AF.Exp, bias=gc[:, last:last + 1], scale=-1.0), reads=[gckey], writes=[dk])
    P.add('dve', lambda e: e.scalar_tensor_tensor(Rd[:, 4, :], Rd[:, 1, :], -1.0, Rd[:, 2, :], ALU.mult, ALU.mult), reads=[dk], writes=[dk])
    P.add('act', lambda e: e.activation(Rd[:, 5, :], gc, AF.Exp, bias=gc[:, last:last + 1], scale=0.0), reads=[gckey], writes=[dk])


def load_F(k, row0, nrows_tiles, seq, dst_list):
    P = k.P
    tok0, T, samp = seq
    for i, (dst, dkey) in enumerate(dst_list):
        r = row0 + 128 * i
        blk = r // 128
        P.add('sp', lambda e, dst=dst, r=r: e.dma_start(out=dst, in_=k.zT[r:r + 128, tok0:tok0 + T]),
              reads=zkeys(blk, seq), writes=[dkey], chan='c_' + dkey)


def gs_all(k, R, rkey, nch, name):
    P = k.P
    if name not in k.wk:
        k.wk[name] = P.sb("gsall_" + name, [128, 32, 6, 2], F32)
    G = k.wk[name]
    gkey = 'gsall_' + name
    n = nch * 4
    rhs = k.sel[:, 0:nch, :].rearrange("p c h -> p (c h)")
    for q0, nq in ((0, 4), (4, 2)):
        pb = k.bank()
        for qq in range(nq):
            for a in range(2):
                P.add('pe', lambda e, qq=qq, q0=q0, pb=pb, a=a: MM(e, k.psb[pb][64 * a:64 * a + 64, qq * 128:qq * 128 + n], R[:, q0 + qq, :], rhs, start=True, stop=True),
                      reads=[rkey, 'sel'], writes=[f'ps{pb}'])
        for a in range(2):
            sl = slice(64 * a, 64 * a + 64)
            src = k.psb[pb][sl, 0:nq * 128].rearrange("p (q c hh a) -> p q c hh a", q=nq, c=32, hh=2)[:, :, 0:nch, :, a]
            dst = G[sl, 0:nch, q0:q0 + nq, :].rearrange("p c q hh -> p q c hh")
            P.add('dve', lambda e, src=src, dst=dst: e.tensor_copy(dst, src), reads=[f'ps{pb}'], writes=[gkey])
    return G, gkey


def to_T(k, t0, cols, name, aug=False, on='act'):
    P = k.P
    pb = k.bank()
    for hh in range(2):
        for a in range(2):
            sl = slice(64 * a, 64 * a + 64)
            P.add('pe', lambda e, hh=hh, sl=sl, pb=pb: MM(e, k.psb[pb][sl, hh * 64:(hh + 1) * 64], k.FB[sl, t0 + hh, cols],
                                                        k.ident[sl, sl], start=True, stop=True),
                  reads=[f'FB{t0 + hh}', 'ident'], writes=[f'ps{pb}'])
    W = 65 if aug else 64
    t, tkey = k.work(name, [128, 2, W])
    src = k.psb[pb][:, 0:128].rearrange("p (h e) -> p h e", h=2)
    if on == 'act':
        P.add('act', lambda e: e.activation(t[:, :, 0:64], src, AF.Identity), reads=[f'ps{pb}'], writes=[tkey])
    else:
        P.add('dve', lambda e: e.tensor_copy(t[:, :, 0:64], src), reads=[f'ps{pb}'], writes=[tkey])
    if aug:
        P.add('dve', lambda e: e.memset(t[:, :, 64:65], 1.0), writes=[tkey])
    return t, tkey


def out_to_HF(k, Hout, hkey, cols):
    P = k.P
    pb = k.bank()
    for hh in range(2):
        for a in range(2):
            sl = slice(64 * a, 64 * a + 64)
            P.add('pe', lambda e, hh=hh, sl=sl, pb=pb: MM(e, k.psb[pb][sl, hh * 64:(hh + 1) * 64], Hout[sl, hh, 0:64], k.ident[sl, sl], start=True, stop=True),
                  reads=[hkey, 'ident'], writes=[f'ps{pb}'])
    P.add('dve', lambda e, pb=pb: e.tensor_tensor(k.HF[:, :, cols], k.HF[:, :, cols],
                                                  k.psb[pb][:, 0:128].rearrange("p (a b) -> p a b", a=2), ALU.add),
          reads=[f'ps{pb}', 'HF0', 'HF1'], writes=['HF0', 'HF1'])


def hmm(k, pb, w, lhs_fn, rhs_fn, reads, start=True, stop=True):
    P = k.P
    for hh in range(2):
        for a in range(2):
            sl = slice(64 * a, 64 * a + 64)
            P.add('pe', lambda e, hh=hh, sl=sl: MM(e, k.psb[pb][sl, hh * w:(hh + 1) * w], lhs_fn(sl, hh), rhs_fn(sl, hh), start=start, stop=stop),
                  reads=reads, writes=[f'ps{pb}'])


def pv3(k, pb, w):
    return k.psb[pb][:, 0:2 * w].rearrange("p (h e) -> p h e", h=2)


def mlstm_p1(k, seq, d, c):
    P = k.P
    cols = slice(c * 64, (c + 1) * 64)
    MLE = k.masks[:, 2 * d, :]
    B3 = [128, 2, 64]
    GS, gkey = k.gsall[('m', d)][0][:, c], k.gsall[('m', d)][1]
    kTl, kkey = to_T(k, 2, cols, 'kTl')
    vA, vkey = k.work('hbA', [128, 2, 65], BF16, n=2)
    pbv = k.bank()
    for hh in range(2):
        for a in range(2):
            sl = slice(64 * a, 64 * a + 64)
            P.add('pe', lambda e, hh=hh, sl=sl: MM(e, k.psb[pbv][sl, hh * 64:(hh + 1) * 64], k.FB[sl, 4 + hh, cols], k.ident[sl, sl], start=True, stop=True),
                  reads=[f'FB{4 + hh}', 'ident'], writes=[f'ps{pbv}'])
    P.add('act', lambda e: e.activation(vA[:, :, 0:64], pv3(k, pbv, 64), AF.Identity), reads=[f'ps{pbv}'], writes=[vkey])
    P.add('pool', lambda e: e.memset(vA[:, :, 64:65], 1.0), writes=[vkey])
    A1, akey = k.work('A1', B3)
    P.add('pool', lambda e: e.tensor_tensor(A1[:], MLE.unsqueeze(1).broadcast_to(B3), GS[:, 0, :].unsqueeze(2).broadcast_to(B3), ALU.mult),
          reads=['masks', gkey], writes=[akey])
    pC, pD = k.bank(), k.bank()
    for a in range(2):
        sl = slice(64 * a, 64 * a + 64)
        P.add('pe', lambda e, sl=sl: MM(e, k.psb[pC][sl, 0:128], k.masks[sl, 2 * d + 1, :], A1[sl, :, :].rearrange("p a b -> p (a b)"), start=True, stop=True),
              reads=['masks', akey], writes=[f'ps{pC}'])
    hmm(k, pD, 64, lambda sl, hh: k.FB[sl, 2 + hh, cols], lambda sl, hh: k.FB[sl, 0 + hh, cols], ['FB0', 'FB1', 'FB2', 'FB3'])
    E, ekey = k.work('E', B3)
    for hh in range(2):
        P.add('act', lambda e, hh=hh: e.activation(E[:, hh, :], k.psb[pC][:, hh * 64:(hh + 1) * 64], AF.Exp, bias=GS[:, 1, hh:hh + 1], scale=1.0),
              reads=[f'ps{pC}', gkey], writes=[ekey])
    PTm, pkey = k.work('PTm', B3)
    P.add('dve', lambda e: e.tensor_tensor(PTm[:], pv3(k, pD, 64), MLE.unsqueeze(1).broadcast_to(B3), ALU.mult),
          reads=[f'ps{pD}', 'masks'], writes=[pkey])
    PTb, pbkey = k.work('bPTm', B3, BF16)
    P.add('pool', lambda e: e.tensor_tensor(PTb[:], PTm[:], E[:], ALU.mult), reads=[pkey, ekey], writes=[pbkey])
    pE = k.bank()
    hmm(k, pE, 65, lambda sl, hh: PTb[sl, hh, :], lambda sl, hh: vA[sl, hh, :], [pbkey, vkey])
    INTRA, ikey = k.work('h1', [128, 2, 65], n=2)
    P.add('act', lambda e: e.activation(INTRA[:], pv3(k, pE, 65), AF.Identity), reads=[f'ps{pE}'], writes=[ikey])
    KW, wkey = k.work('hbK', B3, BF16, n=2)
    P.add('pool', lambda e: e.tensor_tensor(KW[:], kTl[:, :, 0:64], GS[:, 3, :].unsqueeze(2).broadcast_to(B3), ALU.mult),
          reads=[kkey, gkey], writes=[wkey])
    return dict(GS=GS, gkey=gkey, vA=vA, vkey=vkey, INTRA=INTRA, ikey=ikey, KW=KW, wkey=wkey)


def mlstm_p2(k, seq, d, c, H):
    P = k.P
    cols = slice(c * 64, (c + 1) * 64)
    B3 = [128, 2, 64]
    GS, gkey, vA, vkey, INTRA, ikey, KW, wkey = (H[x] for x in ('GS', 'gkey', 'vA', 'vkey', 'INTRA', 'ikey', 'KW', 'wkey'))
    Cst = k.Cst[d]
    pF = k.bank()
    hmm(k, pF, 65, lambda sl, hh: k.FB[sl, 0 + hh, cols], lambda sl, hh: Cst[sl, hh, :], ['FB0', 'FB1', f'Cst{d}'])
    NUM, nkey = k.work('NUM', [128, 2, 65])
    P.add('dve', lambda e: e.tensor_tensor(NUM[:], pv3(k, pF, 65), GS[:, 2, :].unsqueeze(2).broadcast_to([128, 2, 65]), ALU.mult),
          reads=[f'ps{pF}', gkey], writes=[nkey])
    P.add('dve', lambda e: e.tensor_tensor(NUM[:], NUM[:], INTRA[:], ALU.add), reads=[ikey, nkey], writes=[nkey])
    DEN, dkey = k.work('DEN', [128, 2])
    P.add('act', lambda e: e.activation(DEN[:], NUM[:, :, 64], AF.Abs), reads=[nkey], writes=[dkey])
    P.add('dve', lambda e: e.tensor_tensor(DEN[:], DEN[:], GS[:, 4, :], ALU.max), reads=[dkey, gkey], writes=[dkey])
    P.add('dve', lambda e: e.reciprocal(DEN[:], DEN[:]), reads=[dkey], writes=[dkey])
    Hout, hkey = k.work('Hout', B3)
    P.add('dve', lambda e: e.tensor_tensor(Hout[:], NUM[:, :, 0:64], DEN[:].unsqueeze(2).broadcast_to(B3), ALU.mult),
          reads=[nkey, dkey], writes=[hkey])
    out_to_HF(k, Hout, hkey, cols)
    pH = k.bank()
    hmm(k, pH, 65, lambda sl, hh: KW[sl, hh, :], lambda sl, hh: vA[sl, hh, :], [wkey, vkey])
    P.add('dve', lambda e: e.tensor_tensor(Cst[:], Cst[:], GS[:, 5, :].unsqueeze(2).broadcast_to([128, 2, 65]), ALU.mult),
          reads=[f'Cst{d}', gkey], writes=[f'Cst{d}'])
    P.add('dve', lambda e: e.tensor_tensor(Cst[:], Cst[:], pv3(k, pH, 65), ALU.add), reads=[f'Cst{d}', f'ps{pH}'], writes=[f'Cst{d}'])


def post_norm(k, l, seq, src_list, normcol, gate_row0, gate_func, cat0):
    P = k.P
    tok0, T, samp = seq
    for i, (src, skey) in enumerate(src_list):
        for ct in range((T + 511) // 512):
            w = min(512, T - ct * 512)
            cs = slice(ct * 512, ct * 512 + w)
            s = k.rot('sg', 2)
            P.add('act', lambda e, s=s, src=src, cs=cs, w=w: e.activation(k.sg[s][:, 0:w], src[:, cs], AF.Square), reads=[skey], writes=[f'sg{s}'])
            pb = k.bank()
            P.add('pe', lambda e, s=s, pb=pb, w=w: MM(e, k.psb[pb][:, 0:w], k.bones[:], k.sg[s][:, 0:w], start=True, stop=True),
                  reads=[f'sg{s}', 'bones'], writes=[f'ps{pb}'])
            P.add('act', lambda e, pb=pb, w=w: e.activation(k.rstd[:, 0:w], k.psb[pb][:, 0:w], AF.Sqrt, bias=EPS, scale=1.0 / 64), reads=[f'ps{pb}'], writes=['rstd'])
            P.add('dve', lambda e, w=w: e.reciprocal(k.rstd[:, 0:w], k.rstd[:, 0:w]), reads=['rstd'], writes=['rstd'])
            s2 = k.rot('tmp', 2)
            r = gate_row0 + 128 * i
            P.add('sp', lambda e, s2=s2, r=r, cs=cs, w=w: e.dma_start(out=k.tmp[s2][:, 0:w], in_=k.zT[r:r + 128, tok0 + cs.start:tok0 + cs.start + w]),
                  reads=zkeys(r // 128, seq), writes=[f'tmp{s2}'], chan=f'c_tmp{s2}')
            P.add('act', lambda e, s2=s2, w=w: e.activation(k.tmp[s2][:, 0:w], k.tmp[s2][:, 0:w], gate_func), reads=[f'tmp{s2}'], writes=[f'tmp{s2}'])
            P.add('dve', lambda e, s2=s2, w=w: e.tensor_tensor(k.tmp[s2][:, 0:w], k.tmp[s2][:, 0:w], k.rstd[:, 0:w], ALU.mult),
                  reads=[f'tmp{s2}', 'rstd'], writes=[f'tmp{s2}'])
            nc_ = normcol + (i if normcol == 0 else 0)
            P.add('dve', lambda e, s2=s2, src=src, cs=cs, w=w, nc_=nc_, i=i: e.scalar_tensor_tensor(
                k.h[:, cat0 + i, tok0 + cs.start:tok0 + cs.start + w], src[:, cs], k.fparS[:, l, nc_:nc_ + 1], k.tmp[s2][:, 0:w], ALU.mult, ALU.mult),
                reads=[skey, f'tmp{s2}', 'fparS'], writes=[f'cat{cat0 + i}'])

def heads_F(k, t0):
    return [(k.FB[64 * (h % 2):64 * (h % 2) + 64, t0 + h // 2, :], f'FB{t0 + h // 2}', 64 * (h % 2)) for h in range(4)]


def delta_prepass(k, l, seq):
    P = k.P
    tok0, T, samp = seq
    for i in range(6):
        s = k.rot('stg', 2)
        r = ZGQ + 128 * i
        P.add('sp', lambda e, s=s, r=r: e.dma_start(out=k.stg[s][:, 0:T], in_=k.zT[r:r + 128, tok0:tok0 + T]),
              reads=zkeys(r // 128, seq), writes=[f'HF{s}'], chan=f'c_stg{s}')
        dst = k.FB[:, i, :]
        fk = f'FB{i}'
        src = k.stg[s]
        P.add('act', lambda e, dst=dst, src=src, i=i: e.activation(dst[:, 0:T], src[:, 0:T], AF.Identity, bias=0.0, scale=k.convS[:, l, i, 2:3]),
              reads=[f'HF{s}', 'convS'], writes=[fk])
        for tap in (0, 1, 3, 4):
            sh = tap - 2
            o0, o1 = max(0, -sh), T - max(0, sh)
            P.add('dve', lambda e, dst=dst, src=src, i=i, tap=tap, sh=sh, o0=o0, o1=o1: e.scalar_tensor_tensor(
                dst[:, o0:o1], src[:, o0 + sh:o1 + sh], k.convS[:, l, i, tap:tap + 1], dst[:, o0:o1], ALU.mult, ALU.add),
                reads=[f'HF{s}', 'convS', fk], writes=[fk])
        P.add('act', lambda e, dst=dst: e.activation(dst[:, 0:T], dst[:, 0:T], AF.Silu), reads=[fk], writes=[fk])
        if i < 4:
            for ct in range((T + 511) // 512):
                w = min(512, T - ct * 512)
                cs = slice(ct * 512, ct * 512 + w)
                s2 = k.rot('sg', 2)
                P.add('act', lambda e, s2=s2, dst=dst, cs=cs, w=w: e.activation(k.sg[s2][:, 0:w], dst[:, cs], AF.Square), reads=[fk], writes=[f'sg{s2}'])
                pb = k.bank()
                P.add('pe', lambda e, s2=s2, pb=pb, w=w: MM(e, k.psb[pb][:, 0:w], k.bones[:], k.sg[s2][:, 0:w], start=True, stop=True),
                      reads=[f'sg{s2}', 'bones'], writes=[f'ps{pb}'])
                P.add('act', lambda e, pb=pb, w=w: e.activation(k.rstd[:, 0:w], k.psb[pb][:, 0:w], AF.Sqrt, bias=EPS, scale=1.0), reads=[f'ps{pb}'], writes=['rstd'])
                P.add('dve', lambda e, w=w: e.reciprocal(k.rstd[:, 0:w], k.rstd[:, 0:w]), reads=['rstd'], writes=['rstd'])
                scl = 0.125 if i < 2 else 1.0
                P.add('dve', lambda e, dst=dst, cs=cs, w=w, scl=scl: e.scalar_tensor_tensor(dst[:, cs], dst[:, cs], scl, k.rstd[:, 0:w], ALU.mult, ALU.mult),
                      reads=[fk, 'rstd'], writes=[fk])


def delta_p1(k, seq, d, c):
    P = k.P
    cols = slice(c * 64, (c + 1) * 64)
    MLE = k.masks[:, 2 * d, :]
    MST = k.masks[:, 2 * d + 1, :]
    I2 = k.masks[:, 4, :]
    B3 = [128, 2, 64]
    bc = lambda ap: ap.unsqueeze(1).broadcast_to(B3)
    gb = lambda q: GS[:, q, :].unsqueeze(2).broadcast_to(B3)
    idb = lambda sl, hh: k.ident[sl, sl]
    GS, gkey = k.gsall[('d', d)][0][:, c], k.gsall[('d', d)][1]
    kTl, kkey = to_T(k, 2, cols, 'kTl')
    vTl, vkey = to_T(k, 4, cols, 'vA', aug=True, on='dve')
    A1, akey = k.work('A1', B3)
    P.add('pool', lambda e: e.tensor_tensor(A1[:], bc(MLE), gb(0), ALU.mult), reads=['masks', gkey], writes=[akey])
    pA, pB = k.bank(), k.bank()
    for a in range(2):
        sl = slice(64 * a, 64 * a + 64)
        P.add('pe', lambda e, sl=sl: MM(e, k.psb[pA][sl, 0:128], k.masks[sl, 2 * d + 1, :], A1[sl, :, :].rearrange("p a b -> p (a b)"), start=True, stop=True),
              reads=['masks', akey], writes=[f'ps{pA}'])
    hmm(k, pB, 64, lambda sl, hh: A1[sl, hh, :], lambda sl, hh: k.masks[sl, 2 * d + 1, :], ['masks', akey])
    DTm, dtkey = k.work('E', B3)
    DB, dbkey = k.work('PTm', B3)
    P.add('act', lambda e: e.activation(DTm[:], pv3(k, pA, 64), AF.Exp), reads=[f'ps{pA}'], writes=[dtkey])
    P.add('act', lambda e: e.activation(DB[:], pv3(k, pB, 64), AF.Exp), reads=[f'ps{pB}'], writes=[dbkey])
    P.add('pool', lambda e: e.tensor_tensor(DTm[:], DTm[:], bc(MLE), ALU.mult), reads=[dtkey, 'masks'], writes=[dtkey])
    P.add('pool', lambda e: e.tensor_tensor(DB[:], DB[:], bc(MST), ALU.mult), reads=[dbkey, 'masks'], writes=[dbkey])
    P.add('dve', lambda e: e.tensor_tensor(DB[:], DB[:], gb(1), ALU.mult), reads=[dbkey, gkey], writes=[dbkey])
    pG, pQ = k.bank(), k.bank()
    hmm(k, pG, 64, lambda sl, hh: k.FB[sl, 2 + hh, cols], lambda sl, hh: k.FB[sl, 2 + hh, cols], ['FB2', 'FB3'])
    hmm(k, pQ, 64, lambda sl, hh: k.FB[sl, 2 + hh, cols], lambda sl, hh: k.FB[sl, 0 + hh, cols], ['FB0', 'FB1', 'FB2', 'FB3'])
    wb = lambda nm, n=None: k.work(nm, B3, BF16, n=n)
    Z, zkey = wb('bZ')
    QKM, qkkey = wb('hb0', 2)
    P.add('dve', lambda e: e.tensor_tensor(Z[:], pv3(k, pG, 64), DB[:], ALU.mult), reads=[f'ps{pG}', dbkey], writes=[zkey])
    P.add('dve', lambda e: e.tensor_tensor(QKM[:], pv3(k, pQ, 64), DTm[:], ALU.mult), reads=[f'ps{pQ}', dtkey], writes=[qkkey])
    idbb = lambda sl, hh: k.identb[sl, sl]
    pX = k.bank()
    hmm(k, pX, 64, lambda sl, hh: Z[sl, hh, :], idbb, [zkey, 'identb'])
    X0, xkey = wb('bX0')
    P.add('act', lambda e: e.activation(X0[:], pv3(k, pX, 64), AF.Identity), reads=[f'ps{pX}'], writes=[xkey])
    BD16 = k.masks[:, 5, :]
    M16 = k.masks[:, 6 + 2 * d, :]
    M32 = k.masks[:, 7 + 2 * d, :]
    M16T = k.masks[:, 6 + 2 * (1 - d), :]
    cp_i = [0]

    def evac(dst, pb, dkey):
        if cp_i[0] % 2 == 0:
            P.add('act', lambda e: e.activation(dst[:], pv3(k, pb, 64), AF.Identity), reads=[f'ps{pb}'], writes=[dkey])
        else:
            P.add('dve', lambda e: e.tensor_copy(dst[:], pv3(k, pb, 64)), reads=[f'ps{pb}'], writes=[dkey])
        cp_i[0] += 1

    def mm4(lhs, lkey, rhs, rkey_):
        pb = k.bank()
        hmm(k, pb, 64, lambda sl, hh: lhs[sl, hh, :], lambda sl, hh: rhs[sl, hh, :], [lkey, rkey_])
        return pb
    ND, ndk = wb('bND')
    XD, xdk = wb('bXD')
    P.add('pool', lambda e: e.tensor_tensor(ND[:], Z[:], bc(BD16), ALU.mult), reads=[zkey, 'masks'], writes=[ndk])
    P.add('pool', lambda e: e.tensor_tensor(XD[:], X0[:], bc(BD16), ALU.mult), reads=[xkey, 'masks'], writes=[xdk])
    RX, rxkey = k.work('bRX', [128, 2, 2, 64], BF16, n=2)
    P.add('dve', lambda e, RX=RX: e.tensor_tensor(RX[:, :, 0, :], bc(I2), XD[:], ALU.subtract), reads=['masks', xdk], writes=[rxkey])
    p1 = mm4(ND, ndk, XD, xdk)
    p2 = mm4(XD, xdk, ND, ndk)
    P.add('act', lambda e, RX=RX: e.activation(RX[:, :, 1, :], pv3(k, p1, 64), AF.Identity), reads=[f'ps{p1}'], writes=[rxkey])
    Zk, zkkey = wb('bZk')
    evac(Zk, p2, zkkey)
    for lev in range(1, 4):
        pa = k.bank()
        if lev < 3:
            hmm(k, pa, 128, lambda sl, hh, Zk=Zk: Zk[sl, hh, :], lambda sl, hh, RX=RX: RX[sl, hh, :, :].rearrange("p a b -> p (a b)"), [zkkey, rxkey])
            pz = k.bank()
            hmm(k, pz, 64, lambda sl, hh, RX=RX: RX[sl, hh, 1, :], lambda sl, hh, Zk=Zk: Zk[sl, hh, :], [zkkey, rxkey])
        else:
            hmm(k, pa, 64, lambda sl, hh, Zk=Zk: Zk[sl, hh, :], lambda sl, hh, RX=RX: RX[sl, hh, 0, :], [zkkey, rxkey])
        RXn, rxnkey = k.work('bRX', [128, 2, 2, 64], BF16, n=2)
        pav = k.psb[pa][:, 0:256].rearrange("p (h a b) -> p h a b", h=2, a=2)
        pa0 = pav[:, :, 0, :] if lev < 3 else pv3(k, pa, 64)
        P.add('dve', lambda e, RX=RX, RXn=RXn, pa0=pa0: e.tensor_tensor(RXn[:, :, 0, :], RX[:, :, 0, :], pa0, ALU.add),
              reads=[rxkey, f'ps{pa}'], writes=[rxnkey])
        if lev < 3:
            P.add('act', lambda e, RXn=RXn, pav=pav: e.activation(RXn[:, :, 1, :], pav[:, :, 1, :], AF.Identity), reads=[f'ps{pa}'], writes=[rxnkey])
            Zn, znkey = wb('bZk')
            evac(Zn, pz, znkey)
            Zk, zkkey = Zn, znkey
        RX, rxkey = RXn, rxnkey
    DT, dtk = wb('bDT')
    P.add('dve', lambda e, RX=RX: e.tensor_copy(DT[:], RX[:, :, 0, :]), reads=[rxkey], writes=[dtk])
    pb = k.bank()
    hmm(k, pb, 64, lambda sl, hh: DT[sl, hh, :], idbb, [dtk, 'identb'])
    Dm, dmk = wb('bDm')
    evac(Dm, pb, dmk)
    Cm, cmk = wb('bND')
    CT, ctk = wb('bXD')
    P.add('pool', lambda e: e.tensor_tensor(Cm[:], Z[:], bc(M16), ALU.mult), reads=[zkey, 'masks'], writes=[cmk])
    P.add('pool', lambda e: e.tensor_tensor(CT[:], X0[:], bc(M16T), ALU.mult), reads=[xkey, 'masks'], writes=[ctk])
    pb = mm4(CT, ctk, Dm, dmk)
    T1, t1k = wb('bT1')
    evac(T1, pb, t1k)
    pb2 = mm4(Cm, cmk, DT, dtk)
    T1p, t1pk = wb('bT1p')
    evac(T1p, pb2, t1pk)
    pb = mm4(DT, dtk, T1, t1k)
    pb2 = mm4(Dm, dmk, T1p, t1pk)
    D32, d32k = wb('bD32')
    DT32, dt32k = wb('bDT32')
    P.add('dve', lambda e, pb=pb: e.tensor_tensor(D32[:], Dm[:], pv3(k, pb, 64), ALU.subtract), reads=[dmk, f'ps{pb}'], writes=[d32k])
    P.add('dve', lambda e, pb2=pb2: e.tensor_tensor(DT32[:], DT[:], pv3(k, pb2, 64), ALU.subtract), reads=[dtk, f'ps{pb2}'], writes=[dt32k])
    Cm2, cm2k = wb('bND')
    P.add('pool', lambda e: e.tensor_tensor(Cm2[:], Z[:], bc(M32), ALU.mult), reads=[zkey, 'masks'], writes=[cm2k])
    pb = mm4(Cm2, cm2k, DT32, dt32k)
    T1q, t1qk = wb('bT1p')
    evac(T1q, pb, t1qk)
    pb = mm4(D32, d32k, T1q, t1qk)
    TIt, tik = wb('hb1', 2)
    P.add('dve', lambda e, pb=pb: e.tensor_tensor(TIt[:], DT32[:], pv3(k, pb, 64), ALU.subtract), reads=[dt32k, f'ps{pb}'], writes=[tik])
    VB, vbkey = wb('hb2', 2)
    KBG, kbkey = wb('bKBG')
    KD, kdkey = wb('hb3', 2)
    P.add('pool', lambda e: e.tensor_tensor(VB[:], vTl[:, :, 0:64], gb(1), ALU.mult), reads=[vkey, gkey], writes=[vbkey])
    P.add('pool', lambda e: e.tensor_tensor(KBG[:], kTl[:, :, 0:64], gb(4), ALU.mult), reads=[kkey, gkey], writes=[kbkey])
    P.add('pool', lambda e: e.tensor_tensor(KD[:], kTl[:, :, 0:64], gb(3), ALU.mult), reads=[kkey, gkey], writes=[kdkey])
    pW = k.bank()
    hmm(k, pW, 64, lambda sl, hh: KBG[sl, hh, :], lambda sl, hh: TIt[sl, hh, :], [kbkey, tik])
    WT_, wtkey = k.work('h4', [128, 2, 65], n=2)
    WT = WT_[:, :, 0:64]
    P.add('act', lambda e: e.activation(WT[:], pv3(k, pW, 64), AF.Identity), reads=[f'ps{pW}'], writes=[wtkey])
    return dict(GS=GS, gkey=gkey, TIt=TIt, tik=tik, VB=VB, vbkey=vbkey, WT=WT, wtkey=wtkey, QKM=QKM, qkkey=qkkey, KD=KD, kdkey=kdkey)


def delta_p2(k, seq, d, c, H):
    P = k.P
    cols = slice(c * 64, (c + 1) * 64)
    B3 = [128, 2, 64]
    GS, gkey, TIt, tik, VB, vbkey, WT, wtkey, QKM, qkkey, KD, kdkey = (H[x] for x in ('GS', 'gkey', 'TIt', 'tik', 'VB', 'vbkey', 'WT', 'wtkey', 'QKM', 'qkkey', 'KD', 'kdkey'))
    gb = lambda q: GS[:, q, :].unsqueeze(2).broadcast_to(B3)
    Sst = k.Sst[d]
    skey = f'Sst{d}'
    pV = k.bank()
    for hh in range(2):
        for a in range(2):
            sl = slice(64 * a, 64 * a + 64)
            P.add('pe', lambda e, hh=hh, sl=sl: MM(e, k.psb[pV][sl, hh * 64:(hh + 1) * 64], TIt[sl, hh, :], VB[sl, hh, :], start=True, stop=False),
                  reads=[tik, vbkey], writes=[f'ps{pV}'])
            P.add('pe', lambda e, hh=hh, sl=sl: MM(e, k.psb[pV][sl, hh * 64:(hh + 1) * 64], WT[sl, hh, :], Sst[sl, hh, :], start=False, stop=True),
                  reads=[wtkey, skey], writes=[f'ps{pV}'])
    VN, vnkey = k.work('pVNb', B3, BF16)
    P.add('act', lambda e: e.activation(VN[:], pv3(k, pV, 64), AF.Identity), reads=[f'ps{pV}'], writes=[vnkey])
    pO1, pO2 = k.bank(), k.bank()
    hmm(k, pO1, 64, lambda sl, hh: k.FB[sl, 0 + hh, cols], lambda sl, hh: Sst[sl, hh, :], ['FB0', 'FB1', skey])
    hmm(k, pO2, 64, lambda sl, hh: QKM[sl, hh, :], lambda sl, hh: VN[sl, hh, :], [qkkey, vnkey])
    OO, ookey = k.work('Hout', B3)
    P.add('dve', lambda e: e.tensor_tensor(OO[:], pv3(k, pO1, 64), gb(2), ALU.mult), reads=[f'ps{pO1}', gkey], writes=[ookey])
    P.add('dve', lambda e: e.tensor_tensor(OO[:], OO[:], pv3(k, pO2, 64), ALU.add), reads=[f'ps{pO2}', ookey], writes=[ookey])
    out_to_HF(k, OO, ookey, cols)
    pS = k.bank()
    hmm(k, pS, 64, lambda sl, hh: KD[sl, hh, :], lambda sl, hh: VN[sl, hh, :], [kdkey, vnkey])
    P.add('dve', lambda e: e.tensor_tensor(Sst[:], Sst[:], gb(5), ALU.mult), reads=[skey, gkey], writes=[skey])
    P.add('dve', lambda e: e.tensor_tensor(Sst[:], Sst[:], pv3(k, pS, 64), ALU.add), reads=[skey, f'ps{pS}'], writes=[skey])


def qk_norm_tile(k, l, seq, src, skey, normcol, rope, dst_bf, dkey, out_f32=None):
    P = k.P
    tok0, T, samp = seq
    for ct in range((T + 511) // 512):
        w = min(512, T - ct * 512)
        cs = slice(ct * 512, ct * 512 + w)
        s2 = k.rot('sg', 2)
        P.add('act', lambda e, s2=s2, cs=cs, w=w: e.activation(k.sg[s2][:, 0:w], src[:, cs], AF.Square), reads=[skey], writes=[f'sg{s2}'])
        pb = k.bank()
        P.add('pe', lambda e, s2=s2, pb=pb, w=w: MM(e, k.psb[pb][:, 0:w], k.bones[:], k.sg[s2][:, 0:w], start=True, stop=True),
              reads=[f'sg{s2}', 'bones'], writes=[f'ps{pb}'])
        P.add('act', lambda e, pb=pb, w=w: e.activation(k.rstd[:, 0:w], k.psb[pb][:, 0:w], AF.Sqrt, bias=EPS, scale=1.0 / 64), reads=[f'ps{pb}'], writes=['rstd'])
        P.add('dve', lambda e, w=w: e.reciprocal(k.rstd[:, 0:w], k.rstd[:, 0:w]), reads=['rstd'], writes=['rstd'])
        s3 = k.rot('tmp', 2)
        xn = k.tmp[s3]
        P.add('dve', lambda e, xn=xn, cs=cs, w=w: e.scalar_tensor_tensor(xn[:, 0:w], src[:, cs], k.fparS[:, l, normcol:normcol + 1], k.rstd[:, 0:w], ALU.mult, ALU.mult),
              reads=[skey, 'fparS', 'rstd'], writes=[f'tmp{s3}'])
        if rope:
            pb2 = k.bank()
            P.add('pe', lambda e, pb2=pb2, xn=xn, w=w: MM(e, k.psb[pb2][:, 0:w], k.permS[:], xn[:, 0:w], start=True, stop=True),
                  reads=[f'tmp{s3}', 'permS'], writes=[f'ps{pb2}'])
            s4 = k.rot('sg', 2)
            P.add('dve', lambda e, s4=s4, pb2=pb2, cs=cs, w=w: e.tensor_tensor(k.sg[s4][:, 0:w], k.psb[pb2][:, 0:w], k.sinS[:, cs], ALU.mult),
                  reads=[f'ps{pb2}', 'sinS'], writes=[f'sg{s4}'])
            P.add('dve', lambda e, xn=xn, cs=cs, w=w: e.tensor_tensor(xn[:, 0:w], xn[:, 0:w], k.cosS[:, cs], ALU.mult),
                  reads=[f'tmp{s3}', 'cosS'], writes=[f'tmp{s3}'])
            P.add('dve', lambda e, s4=s4, xn=xn, cs=cs, w=w: e.tensor_tensor(dst_bf[:, cs], xn[:, 0:w], k.sg[s4][:, 0:w], ALU.add),
                  reads=[f'tmp{s3}', f'sg{s4}'], writes=[dkey])
        else:
            P.add('act', lambda e, xn=xn, cs=cs, w=w: e.activation(dst_bf[:, cs], xn[:, 0:w], AF.Identity), reads=[f'tmp{s3}'], writes=[dkey])
            if out_f32 is not None:
                P.add('sp', lambda e, xn=xn, cs=cs, w=w: e.dma_start(out=out_f32[:, cs], in_=xn[:, 0:w]), reads=[f'tmp{s3}'], chan=f'c_tmpo{s3}')


def attention(k, l, seq, si):
    P = k.P
    tok0, T, samp = seq
    koff = PAST if samp else 0
    nkt = (koff + T) // 128
    AK = [f'FB{i}' for i in range(6)] + ['kTa', 'Va', 'PT0', 'PT1', 'cosS', 'sinS']
    P.add('dve', lambda e: e.memset(k.ST[:, 6:7], 0.0), reads=[], writes=AK)
    if samp:
        P.add('sp', lambda e: e.dma_start(out=k.cosS, in_=k.cosD), writes=['cosS'], chan='c_cos')
        P.add('sp', lambda e: e.dma_start(out=k.sinS, in_=k.sinD), writes=['sinS'], chan='c_sin')
    s = k.rot('stg', 2)
    P.add('sp', lambda e, s=s: e.dma_start(out=k.stg[s][:, 0:T], in_=k.zT[ZAK:ZAK + 128, tok0:tok0 + T]), reads=zkeys(20, seq), writes=[f'HF{s}'], chan=f'c_stg{s}')
    qk_norm_tile(k, l, seq, k.stg[s], f'HF{s}', 4, samp, k.kTa[:, koff:koff + T], 'kTa',
                 out_f32=None if samp else k.nkT[l][:, tok0:tok0 + T])
    if samp:
        P.add('pool', lambda e: e.dma_start(out=k.kTa[:, 0:PAST], in_=k.ckT[l]), writes=['kTa'], chan='c_ck')
        for kt in range(4):
            P.add('pool', lambda e, kt=kt: e.dma_start(out=k.Va[:, kt, :, 0:64], in_=k.cv[l][kt * 128:(kt + 1) * 128, :].rearrange("p (g e) -> p g e", g=2)),
                  writes=['Va'], chan='c_cv')
    else:
        P.add('sp', lambda e: e.dma_start(out=k.nvT[l][:, tok0:tok0 + T], in_=k.zT[ZAV:ZAV + 128, tok0:tok0 + T]), reads=zkeys(21, seq), chan='c_nv')
    s = k.rot('stg', 2)
    P.add('sp', lambda e, s=s: e.dma_start(out=k.stg[s][:, 0:T], in_=k.zT[ZAV:ZAV + 128, tok0:tok0 + T]), reads=zkeys(21, seq), writes=[f'HF{s}'], chan=f'c_stg{s}')
    P.add('dve', lambda e: e.memset(k.Va[:, :, :, 64:65], 1.0), writes=['Va'])
    for b in range(T // 128):
        pb = k.bank()
        P.add('pe', lambda e, s=s, b=b, pb=pb: e.transpose(k.psb[pb][:, 0:128], k.stg[s][:, b * 128:(b + 1) * 128], k.ident[:]),
              reads=[f'HF{s}', 'ident'], writes=[f'ps{pb}'])
        P.add('act', lambda e, b=b, pb=pb: e.activation(k.Va[:, koff // 128 + b, :, 0:64], k.psb[pb][:, 0:128].rearrange("p (g e) -> p g e", g=2), AF.Identity),
              reads=[f'ps{pb}'], writes=['Va'])
    for i in range(4):
        s = k.rot('stg', 2)
        r = ZAQ + 128 * i
        P.add('sp', lambda e, s=s, r=r: e.dma_start(out=k.stg[s][:, 0:T], in_=k.zT[r:r + 128, tok0:tok0 + T]), reads=zkeys(r // 128, seq), writes=[f'HF{s}'], chan=f'c_stg{s}')
        qk_norm_tile(k, l, seq, k.stg[s], f'HF{s}', 3, samp, k.QT[:, i, 0:T], f'QT{i}')
    for g in range(2):
        gp = 64 * g
        for qi in range(T // 128):
            qsl = slice(qi * 128, (qi + 1) * 128)
            pOs = []
            for _ in range(4):
                b_ = k.bank()
                k.reserved.add(b_)
                pOs.append(b_)
            def scores(kt):
                pS_ = k.bank()
                P.add('pe', lambda e, kt=kt, pS=pS_, gp=gp, qsl=qsl: MM(e, k.psb[pS][:, 0:512], k.kTa[gp:gp + 64, kt * 128:(kt + 1) * 128], k.QT[gp:gp + 64, :, qsl], start=True, stop=True),
                      reads=['kTa'] + [f'QT{i}' for i in range(4)], writes=[f'ps{pS_}'])
                return pS_
            pS_next = scores(0)
            for kt in range(nkt):
                pS = pS_next
                if kt + 1 < nkt:
                    pS_next = scores(kt + 1)
                sp_ = k.rot('PT', 2)
                P.add('act', lambda e, sp_=sp_, pS=pS: e.activation(k.PT[sp_][:], k.psb[pS][:, 0:512], AF.Exp, bias=0.0, scale=0.125), reads=[f'ps{pS}'], writes=[f'PT{sp_}'])
                for hig in range(4):
                    P.add('pe', lambda e, kt=kt, hig=hig, sp_=sp_, pO=pOs[hig], g=g: MM(e, k.psb[pO][:, 0:65], k.PT[sp_][:, hig * 128:(hig + 1) * 128], k.Va[:, kt, g, :],
                                                                               start=(kt == 0), stop=(kt == nkt - 1)),
                          reads=[f'PT{sp_}', 'Va'], writes=[f'ps{pOs[hig]}'])
            for b_ in pOs:
                k.reserved.discard(b_)
            REC, rkey = k.work('REC', [128, 4])
            AO, aokey = k.work('AO', [128, 4, 64])
            for hig in range(4):
                pO = pOs[hig]
                P.add('dve', lambda e, pO=pO, REC=REC, hig=hig: e.reciprocal(REC[:, hig:hig + 1], k.psb[pO][:, 64:65]), reads=[f'ps{pO}'], writes=[rkey])
                P.add('act', lambda e, pO=pO, REC=REC, AO=AO, hig=hig: e.activation(AO[:, hig, :], k.psb[pO][:, 0:64], AF.Identity, bias=0.0, scale=REC[:, hig:hig + 1]),
                      reads=[f'ps{pO}', rkey], writes=[aokey])
            pT = k.bank()
            for a in range(2):
                P.add('pe', lambda e, a=a, pT=pT, AO=AO: e.transpose(k.psb[pT][:, a * 128:(a + 1) * 128], AO[:, 2 * a:2 * a + 2, :].rearrange("p a b -> p (a b)"), k.ident[:]),
                      reads=[aokey, 'ident'], writes=[f'ps{pT}'])
            P.add('act', lambda e, pT=pT, qi=qi, g=g: e.activation(k.h[:, 4 + 2 * g:6 + 2 * g, tok0 + qi * 128:tok0 + (qi + 1) * 128],
                                                             k.psb[pT][:, 0:256].rearrange("p (a q) -> p a q", a=2), AF.Identity),
                  reads=[f'ps{pT}'], writes=[f'cat{4 + 2 * g}', f'cat{5 + 2 * g}'])
    if k.cfg.get('dbg') and samp:
        P.add('sp', lambda e: e.dma_start(out=k.kTaD, in_=k.kTa), reads=['kTa'], chan='c_dbg1')
        for i in range(4):
            P.add('sp', lambda e, i=i: e.dma_start(out=k.QTD[:, i, :], in_=k.QT[:, i, :]), reads=[f'QT{i}'], chan='c_dbg2')
        P.add('sp', lambda e: e.dma_start(out=k.VaD, in_=k.FB[:, 3, :].bitcast(BF16)[:, 0:2600]), reads=['Va'], chan='c_dbg3')
    P.add('dve', lambda e: e.memset(k.ST[:, 6:7], 0.0), reads=[], writes=AK)


def scans(k, l, seq, si, which):
    P = k.P
    tok0, T, samp = seq
    nch = T // 64
    st = k.Cst if which == 'm' else k.Sst
    stn = 'Cst' if which == 'm' else 'Sst'
    W = 65 if which == 'm' else 64
    for d in range(2):
        if samp:
            src = (k.C0 if which == 'm' else k.S0)[l, d]
            P.add('sp', lambda e, d=d, src=src: e.dma_start(out=st[d][:], in_=src), writes=[f'{stn}{d}'], chan=f'c_{stn}{d}')
        else:
            P.add('dve', lambda e, d=d: e.memset(st[d][:], 0.0), writes=[f'{stn}{d}'])
    P.add('dve', lambda e: e.memset(k.HF[:, :, 0:T], 0.0), writes=['HF0', 'HF1'])
    if not hasattr(k, 'gsall'):
        k.gsall = {}
    for d in range(2):
        R = k.Rm[d] if which == 'm' else k.Rd[d]
        k.gsall[(which, d)] = gs_all(k, R, ('Rm' if which == 'm' else 'Rd') + str(d), nch, 'g' + str(d))
    p1f, p2f = (mlstm_p1, mlstm_p2) if which == 'm' else (delta_p1, delta_p2)
    rec1 = {}
    rec2 = {}
    for d in range(2):
        for s in range(nch):
            c = s if d == 0 else nch - 1 - s
            P.thread_begin()
            k.bank_pool = (4 * d, 2)
            k.tid = f'_t{d}'
            H = p1f(k, seq, d, c)
            rec1[(d, s)] = P.thread_end()
            P.thread_begin()
            k.bank_pool = (4 * d + 2, 2)
            p2f(k, seq, d, c, H)
            rec2[(d, s)] = P.thread_end()
    k.bank_pool = None
    k.tid = ''
    for s in range(nch + 1):
        th = []
        for d in range(2):
            if s < nch:
                th.append(rec1[(d, s)])
            if s >= 1:
                th.append(rec2[(d, s - 1)])
        P.interleave(th)
    if not samp:
        for d in range(2):
            dst = (k.Cout if which == 'm' else k.Sout)[si, l, d]
            P.add('sp', lambda e, d=d, dst=dst: e.dma_start(out=dst, in_=st[d][:]), reads=[f'{stn}{d}'], chan=f'c_{stn}o{d}')


def mixer_phase(k, l):
    P = k.P
    cfg = k.cfg
    norm_phase(k, l, 1)
    wv = k.w_in[l].rearrange("(kk p) c -> p kk c", p=128)
    for blk in range(23):
        ncol = 128 if blk < 22 else 32
        s = k.rot('w8', 2)
        P.add('pool', lambda e, s=s, blk=blk, ncol=ncol: e.dma_start(out=k.wg[s][:, :, 0:ncol], in_=wv[:, :, blk * 128:blk * 128 + ncol]),
              writes=[f'wg{s}'], chan=f'c_wg{s}')
        for tt in range(NT):
            pb = k.bank()
            for kk in range(8):
                P.add('pe', lambda e, s=s, kk=kk, pb=pb, tt=tt, ncol=ncol: MM(e, k.psb[pb][0:ncol, :], k.wg[s][:, kk, 0:ncol], k.h[:, kk, tsl(tt)],
                                                                                   start=(kk == 0), stop=(kk == 7)),
                      reads=[f'wg{s}', f'h{tt}_{kk}'], writes=[f'ps{pb}'])
            s2 = k.rot('sg', 2)
            if (blk * NT + tt) % 2 == 0:
                P.add('act', lambda e, s2=s2, pb=pb, ncol=ncol: e.activation(k.sg[s2][0:ncol, :], k.psb[pb][0:ncol, :], AF.Identity), reads=[f'ps{pb}'], writes=[f'sg{s2}'])
            else:
                P.add('dve', lambda e, s2=s2, pb=pb, ncol=ncol: e.tensor_copy(k.sg[s2][0:ncol, :], k.psb[pb][0:ncol, :]), reads=[f'ps{pb}'], writes=[f'sg{s2}'])
            P.add('sp', lambda e, s2=s2, blk=blk, tt=tt, ncol=ncol: e.dma_start(out=k.zT[blk * 128:blk * 128 + ncol, tsl(tt)], in_=k.sg[s2][0:ncol, :]),
                  reads=[f'sg{s2}'], writes=[f'zT{blk}_{tt}'], chan=f'c_sgo{s2}')
    wov = k.w_out[l].rearrange("(kk p) c -> p kk c", p=128)
    W8K = ['wg0', 'wu0', 'wg1', 'wu1']
    for q4 in range(4):
        P.add('pool', lambda e, q4=q4: e.dma_start(out=k.wout[:, 2 * q4:2 * q4 + 2, :], in_=wov[:, 2 * q4:2 * q4 + 2, :]), writes=W8K, chan='c_wout')
    xkeys = [f'x{tt}_{kk}' for tt in range(NT) for kk in range(8)]
    mixkeys = [f'FB{i}' for i in range(6)] + ['HF0', 'HF1'] + [f'QT{i}' for i in range(4)] + ['kTa', 'Va', 'PT0', 'PT1', 'cosS', 'sinS']
    hkeys = [f'h{tt}_{kk}' for tt in range(NT) for kk in range(8)]
    catkeys = [f'cat{i}' for i in range(8)]
    for a in range(8):
        P.add('sp', lambda e, a=a: e.dma_start(out=k.xpark[:, a * NTOK:(a + 1) * NTOK], in_=k.x[:, a, :]), reads=xkeys, writes=mixkeys + ['xpark'], chan='c_park')
    P.add('dve', lambda e: e.memset(k.ST[:, 7:8], 0.0), reads=[], writes=hkeys + catkeys)
    parts = cfg.get('parts', 'gmda')
    P.barrier(lambda e: e.memset(k.ST[:, 7:8], 0.0))
    for si, seq in enumerate(SEQS):
        tok0, T, samp = seq
        if si not in cfg.get('seqs', (0, 1, 2)):
            continue
        if 'g' in parts:
          for d in range(2):
            gate_prepass(k, l, seq, d, k.m0[l, d:d + 1, :] if samp else None)
        if 'm' in parts:
          load_F(k, ZQ, 6, seq, [(k.FB[:, i, 0:T], f'FB{i}') for i in range(6)])
          for i in (2, 3):
            P.add('act', lambda e, i=i, T=T: e.activation(k.FB[:, i, 0:T], k.FB[:, i, 0:T], AF.Identity, bias=0.0, scale=0.125), reads=[f'FB{i}'], writes=[f'FB{i}'])
          if cfg.get('msub', 15) & 2:
            scans(k, l, seq, si, 'm')
          if not samp and cfg.get('msub', 15) & 4:
            for d in range(2):
                fin = T // 64 if d == 0 else 0
                P.add('dve', lambda e, d=d, fin=fin: e.tensor_copy(k.m0t[d][:], k.MM[d][:, :, fin]), reads=[f'MM{d}'], writes=[f'm0t{d}'])
                P.add('sp', lambda e, d=d, si=si: e.dma_start(out=k.mout[si, l, d:d + 1, :], in_=k.m0t[d][:]), reads=[f'm0t{d}'], chan=f'c_mo{d}')
          if cfg.get('msub', 15) & 8:
            post_norm(k, l, seq, [(k.HF[:, i, :], f'HF{i}') for i in range(2)], 0, ZO, AF.Sigmoid, 0)
        if 'd' in parts:
          delta_prepass(k, l, seq)
          if cfg.get('dbg'):
            for i in range(6):
                P.add('sp', lambda e, i=i: e.dma_start(out=k.FBD[:, i, :], in_=k.FB[:, i, :]), reads=[f'FB{i}'], chan='c_dbg4')
            for d in range(2):
                P.add('sp', lambda e, d=d: e.dma_start(out=k.RdD[d], in_=k.Rd[d][:]), reads=[f'Rd{d}'], chan='c_dbg5')
                P.add('sp', lambda e, d=d: e.dma_start(out=k.RmD[d], in_=k.Rm[d][:]), reads=[f'Rm{d}'], chan='c_dbg5')
          scans(k, l, seq, si, 'd')
          post_norm(k, l, seq, [(k.HF[:, i, :], f'HF{i}') for i in range(2)], 2, ZGZ, AF.Silu, 2)
        if 'a' in parts:
          attention(k, l, seq, si)
    if cfg.get('dbg'):
        for kk in range(8):
            P.add('sp', lambda e, kk=kk: e.dma_start(out=k.catD[:, kk, :], in_=k.h[:, kk, :]), reads=catkeys, chan='c_catD')
    P.barrier(lambda e: e.memset(k.ST[:, 7:8], 0.0))
    for a in range(8):
        P.add('sp', lambda e, a=a: e.dma_start(out=k.x[:, a, :], in_=k.xpark[:, a * NTOK:(a + 1) * NTOK]), reads=['xpark'], writes=xkeys + mixkeys, chan='c_unpark')
    for tt in range(NT):
        j = 0 if tt == 0 else 1
        for o in range(8):
            bo = k.bank()
            for kk in range(8):
                P.add('pe', lambda e, o=o, kk=kk, bo=bo, tt=tt: MM(e, k.psb[bo][:], k.wout[:, kk, o * 128:(o + 1) * 128], k.h[:, kk, tsl(tt)],
                                                                        start=(kk == 0), stop=(kk == 7)),
                      reads=W8K + [f'cat{kk}'], writes=[f'ps{bo}'])
            P.add('dve', lambda e, o=o, bo=bo, tt=tt, j=j: e.scalar_tensor_tensor(
                k.x[:, o, tsl(tt)], k.psb[bo][:], k.dv[:, j, 4, o:o + 1], k.x[:, o, tsl(tt)], ALU.mult, ALU.add),
                reads=[f'ps{bo}', 'dv', f'x{tt}_{o}'], writes=[f'x{tt}_{o}'])
    P.add('dve', lambda e: e.memset(k.ST[:, 7:8], 0.0), reads=[], writes=hkeys + catkeys)

def _consts():
    c = {}
    c['ident'] = np.eye(128, dtype=np.float32)
    i = np.arange(64)
    m = np.zeros((64, 10, 64), np.float32)
    tt_, jj_ = i[:, None], i[None, :]
    m[:, 5, :] = (tt_ // 16 == jj_ // 16)
    m16 = ((tt_ // 16) % 2 == 1) & (jj_ // 16 == tt_ // 16 - 1)
    m32 = (tt_ // 32 == 1) & (jj_ // 32 == 0)
    m[:, 6, :] = m16
    m[:, 7, :] = m32
    m[:, 8, :] = m16.T
    m[:, 9, :] = m32.T
    m[:, 0, :] = (i[:, None] <= i[None, :])
    m[:, 1, :] = (i[None, :] < i[:, None])
    m[:, 2, :] = (i[:, None] >= i[None, :])
    m[:, 3, :] = (i[None, :] > i[:, None])
    m[:, 4, :] = (i[:, None] == i[None, :])
    c['masks'] = np.concatenate([m, m], 0)
    sel = np.zeros((128, 32, 4), np.float32)
    for h in range(4):
        for cc in range(32):
            sel[32 * h + cc, cc, h] = 1.0
    c['sel'] = sel
    b = np.zeros((128, 128), np.float32)
    b[:64, :64] = 1.0
    b[64:, 64:] = 1.0
    c['bones'] = b
    p = np.arange(128)
    dd = p % 64
    half = dd // 32
    r = dd % 32
    f = r % 16
    second = r // 16
    inv = (10000.0 ** (-(np.arange(16, dtype=np.float32)) / 16.0)).astype(np.float32)
    t = np.arange(2048)
    row = (t // 64).astype(np.float32)
    col = (t % 64).astype(np.float32)
    pos = np.where(half[:, None] == 0, row[None, :], col[None, :]).astype(np.float32)
    ang = (pos * inv[f][:, None]).astype(np.float32)
    c['cosT'] = np.cos(ang).astype(np.float32)
    c['sinT'] = (np.sin(ang) * np.where(second[:, None] == 0, -1.0, 1.0)).astype(np.float32)
    perm = np.zeros((128, 128), np.float32)
    for mm in range(128):
        src = mm + 16 if (mm % 32) < 16 else mm - 16
        perm[src, mm] = 1.0
    c['perm'] = perm
    return c


def _win_perm():
    mq, mk, mv, mo, mi, mf, gq, gk, gv, gz, ga, gb, aq, ak, av = (0, 256, 512, 768, 1024, 1032, 1040, 1296, 1552, 1808, 2064, 2072, 2080, 2592, 2720)
    idx = []
    for o in (mq, mk, mv, mo, gq, gk, gv, gz):
        idx += list(range(o, o + 256))
    for hig in range(4):
        for g in range(2):
            h = g * 4 + hig
            idx += list(range(aq + h * 64, aq + h * 64 + 64))
    idx += list(range(ak, ak + 128)) + list(range(av, av + 128))
    for o in (mi, mf, ga, gb):
        idx += list(range(o, o + 8))
    return np.array(idx)


def _shared(inp):
    f = np.float32
    sh = {}
    sh['ada_w'] = np.ascontiguousarray(inp['ada_w'], f)
    sh['ada_bT'] = np.ascontiguousarray(inp['ada_b'].reshape(2, 72, 128).transpose(0, 2, 1), f)
    sh['norm_gT'] = np.ascontiguousarray(inp['norm_g'].reshape(2, 3, 8, 128).transpose(3, 0, 1, 2), f)
    sh['final_gT'] = np.ascontiguousarray(inp['final_norm'].reshape(8, 128).T, f)
    sh['ffn_w_in'] = np.ascontiguousarray(inp['ffn_w_in'], f)
    sh['ffn_w_out'] = np.ascontiguousarray(inp['ffn_w_out'], f)
    sh['w_in_p'] = np.ascontiguousarray(inp['w_in'][:, :, _win_perm()], f)
    sh['w_out'] = np.ascontiguousarray(inp['w_out'], f)
    p = np.arange(128)
    gpar = np.zeros((128, 2, 2, 3), f)
    hh = p // 32
    gpar[:, :, :, 0] = -inp['mlstm_f_bias'].transpose(2, 0, 1)[hh]
    gpar[:, :, :, 1] = inp['delta_a_log'].transpose(2, 0, 1)[hh]
    gpar[:, :, :, 2] = inp['delta_dt_bias'].transpose(2, 0, 1)[hh]
    sh['gpar'] = gpar
    fpar = np.zeros((128, 2, 5), f)
    fpar[:, :, 0] = inp['mlstm_norm'][:, 0:128].T
    fpar[:, :, 1] = inp['mlstm_norm'][:, 128:256].T
    fpar[:, :, 2] = inp['delta_norm'][:, p % 64].T
    fpar[:, :, 3] = inp['attn_q_norm'][:, p % 64].T
    fpar[:, :, 4] = inp['attn_k_norm'][:, p % 64].T
    sh['fpar'] = fpar
    sh['convw'] = np.ascontiguousarray(inp['delta_conv'].reshape(2, 5, 6, 128).transpose(3, 0, 2, 1), f)
    sh.update(_consts())
    return sh


def _state_layout(C, n=None):
    l, dr, H, dk, e = C.shape
    W = e + (1 if n is not None else 0)
    out = np.zeros((l, dr, 128, 2, W), np.float32)
    for h in range(4):
        out[:, :, 64 * (h % 2):64 * (h % 2) + 64, h // 2, :e] = C[:, :, h]
        if n is not None:
            out[:, :, 64 * (h % 2):64 * (h % 2) + 64, h // 2, e] = n[:, :, h]
    return out


def prep_core(inp, core, sh, mix=True):
    f = np.float32
    b = core // 2
    xp = inp['x_prompt'][2 * core:2 * core + 2].reshape(512, 1024)
    xs = inp['x_sample'][b]
    d = dict(sh) if mix else {kk: sh[kk] for kk in ('ada_w', 'ada_bT', 'norm_gT', 'final_gT', 'ffn_w_in', 'ffn_w_out', 'ident')}
    d['xT'] = np.ascontiguousarray(np.concatenate([xp, xs], 0).T, f)
    cond = np.stack([inp['c_ctx'], inp['c'][b]], -1)
    d['condT'] = np.ascontiguousarray(cond.reshape(8, 128, 2).transpose(1, 0, 2), f)
    if mix:
        d['ckT'] = np.ascontiguousarray(inp['cache_k'][b].reshape(2, 512, 128).transpose(0, 2, 1), f)
        d['cv'] = np.ascontiguousarray(inp['cache_v'][b].reshape(2, 512, 128), f)
        d['C0'] = _state_layout(inp['state_mlstm_C'][b], inp['state_mlstm_n'][b])
        d['S0'] = _state_layout(inp['state_delta_S'][b])
        d['m0'] = np.ascontiguousarray(inp['state_mlstm_m'][b], f)
    return d


def assemble(results):
    f = np.float32
    y_prompt = np.zeros((16, 256, 1024), f)
    y_sample = np.zeros((4, 2048, 1024), f)
    new_k = np.zeros((16, 2, 256, 2, 64), f)
    new_v = np.zeros((16, 2, 256, 2, 64), f)
    new_C = np.zeros((16, 2, 2, 4, 64, 64), f)
    new_n = np.zeros((16, 2, 2, 4, 64), f)
    new_m = np.zeros((16, 2, 2, 4), f)
    new_S = np.zeros((16, 2, 2, 4, 64, 64), f)
    for core, r in enumerate(results):
        yT = r['yT']
        for i in range(2):
            bi = 2 * core + i
            y_prompt[bi] = yT[:, i * 256:(i + 1) * 256].T
            new_k[bi] = r['nkT'][:, :, i * 256:(i + 1) * 256].reshape(2, 2, 64, 256).transpose(0, 3, 1, 2)
            new_v[bi] = r['nvT'][:, :, i * 256:(i + 1) * 256].reshape(2, 2, 64, 256).transpose(0, 3, 1, 2)
            Co = r['Cout'][i]
            So = r['Sout'][i]
            for h in range(4):
                blk = Co[:, :, 64 * (h % 2):64 * (h % 2) + 64, h // 2, :]
                new_C[bi, :, :, h] = blk[..., :64]
                new_n[bi, :, :, h] = blk[..., 64]
                new_S[bi, :, :, h] = So[:, :, 64 * (h % 2):64 * (h % 2) + 64, h // 2, :]
            new_m[bi] = r['mout'][i]
        if core % 2 == 0:
            y_sample[core // 2] = yT[:, 512:].T
    return (y_prompt, y_sample, new_k, new_v, new_C, new_n, new_m, new_S)


_NC_CACHE = {}


def kernel(**inputs):
    inp = {kk: np.asarray(v) for kk, v in inputs.items()}
    if 'nc' not in _NC_CACHE:
        _NC_CACHE['nc'] = build(dict(mix=True, layers=2))
    nc = _NC_CACHE['nc']
    sh = _shared(inp)
    in_maps = [prep_core(inp, c, sh) for c in range(8)]
    res = run_bass_kernel_spmd(nc, in_maps, core_ids=list(range(8)))
    return assemble(res.results)
```

```python
import numpy as np
import concourse.bass as bass
import concourse.mybir as mybir
from contextlib import ExitStack

F32 = mybir.dt.float32
BF16 = mybir.dt.bfloat16
ALU = mybir.AluOpType
AF = mybir.ActivationFunctionType
AX = mybir.AxisListType

ENGS = ('pe', 'act', 'dve', 'pool', 'sp')
EPOCH = 16000


class Op:
    __slots__ = ('eng', 'idx', 'fn', 'waits', 'dwaits', 'signal', 'sigval', 'chan', 'ccount', 'known')


class Prog:
    def __init__(self, nc):
        self.nc = nc
        self.ops = {e: [] for e in ENGS}
        self.lastw = {}
        self.readers = {}
        self.seen = {e: {} for e in ENGS}
        self.dseen = {e: {} for e in ENGS}
        self.chan_count = {}
        self.stack = ExitStack()
        self.n_sb = 0
        self.bar = None
        self.rec = None

    def sb(self, name, shape, dtype=F32):
        return self.stack.enter_context(self.nc.sbuf_tensor("S_" + name, list(shape), dtype))

    def ps(self, name, shape, dtype=F32):
        return self.stack.enter_context(self.nc.psum_tensor("P_" + name, list(shape), dtype))

    def thread_begin(self):
        self.rec = []

    def thread_end(self):
        r = self.rec
        self.rec = None
        return r

    def interleave(self, threads):
        n = max(len(t) for t in threads)
        for i in range(n):
            for t in threads:
                if i < len(t):
                    self.add(*t[i][0], **t[i][1])

    def add(self, eng, fn, reads=(), writes=(), chan=None):
        if getattr(self, 'rec', None) is not None:
            self.rec.append(((eng, fn), dict(reads=list(reads), writes=list(writes), chan=chan)))
            return None
        op = Op()
        op.eng = eng
        op.idx = len(self.ops[eng])
        op.fn = fn
        op.waits = {}
        op.dwaits = {}
        op.signal = False
        op.sigval = 0
        op.chan = chan
        op.ccount = 0
        deps = []
        for k in reads:
            w = self.lastw.get(k)
            if w is not None:
                deps.append((w, False))
        for k in writes:
            w = self.lastw.get(k)
            if w is not None:
                deps.append((w, False))
            for r in self.readers.get(k, ()):
                deps.append((r, True))
        seen = self.seen[eng]
        dseen = self.dseen[eng]
        for d, is_war in deps:
            if d.chan is not None:
                if dseen.get(d.chan, 0) < d.ccount:
                    op.dwaits[d.chan] = max(op.dwaits.get(d.chan, 0), d.ccount)
                    dseen[d.chan] = d.ccount
                continue
            if d.eng == eng:
                if eng == 'pe':
                    continue
            if seen.get(d.eng, -1) >= d.idx:
                continue
            op.waits[d.eng] = max(op.waits.get(d.eng, -1), d.idx)
        if self.bar is not None and seen.get('dve', -1) < self.bar.idx:
            op.waits['dve'] = max(op.waits.get('dve', -1), self.bar.idx)
        for e2, idx in op.waits.items():
            dop = self.ops[e2][idx]
            dop.signal = True
            if seen.get(e2, -1) < idx:
                seen[e2] = idx
            for e3, i3 in dop.known.items():
                if e3 != eng and seen.get(e3, -1) < i3:
                    seen[e3] = i3
        op.known = dict(seen)
        if chan is not None:
            c = self.chan_count.get(chan, 0) + 1
            self.chan_count[chan] = c
            op.ccount = c
        self.ops[eng].append(op)
        for k in writes:
            self.lastw[k] = op
            self.readers[k] = []
        for k in reads:
            self.readers.setdefault(k, []).append(op)
        return op

    def barrier(self, fn):
        self.bar = None
        op = self.add('dve', fn)
        for e in ENGS:
            if e == 'dve':
                continue
            for o in reversed(self.ops[e]):
                if o.chan is None and o.fn is not None:
                    if self.seen['dve'].get(e, -1) < o.idx:
                        op.waits[e] = o.idx
                        o.signal = True
                        self.seen['dve'][e] = o.idx
                    break
        for ch, cnt in self.chan_count.items():
            if self.dseen['dve'].get(ch, 0) < cnt:
                op.dwaits[ch] = cnt
                self.dseen['dve'][ch] = cnt
        op.known = dict(self.seen['dve'])
        self.bar = op
        return op

    def final_wait(self, eng='sp'):
        op = Op()
        op.eng = eng
        op.idx = len(self.ops[eng])
        op.fn = None
        op.waits = {}
        op.dwaits = dict(self.chan_count)
        op.signal = False
        op.sigval = 0
        op.chan = None
        op.ccount = 0
        op.known = {}
        self.ops[eng].append(op)

    def emit(self):
        nc = self.nc
        nsig = {}
        for e in ENGS:
            c = 0
            for op in self.ops[e]:
                if op.chan is None and op.signal:
                    c += 1
                    op.sigval = c
            nsig[e] = c
        sems = {}
        for e in ENGS:
            for ep in range((nsig[e] + EPOCH - 1) // EPOCH):
                sems[(e, ep)] = self.stack.enter_context(nc.semaphore(f"s_{e}_{ep}"))
        csems = {}
        for ch in self.chan_count:
            csems[ch] = self.stack.enter_context(nc.semaphore(f"c_{ch}"))
        self.n_sems = len(sems) + len(csems)

        def sem_of(e, sigval):
            ep = (sigval - 1) // EPOCH
            return sems[(e, ep)], (sigval - 1) % EPOCH + 1

        def replay(ename, eobj):
            for op in self.ops[ename]:
                for e2, idx in op.waits.items():
                    s, v = sem_of(e2, self.ops[e2][idx].sigval)
                    eobj.wait_ge(s, v)
                for ch, cnt in op.dwaits.items():
                    eobj.wait_ge(csems[ch], 16 * cnt)
                if op.fn is None:
                    continue
                ins = op.fn(eobj)
                if op.chan is not None:
                    ins.then_inc(csems[op.chan], 16)
                elif op.signal:
                    s, v = sem_of(ename, op.sigval)
                    ins.then_inc(s, 1)

        with nc.Block() as block:
            @block.tensor
            def _(e):
                replay('pe', e)

            @block.scalar
            def _(e):
                replay('act', e)

            @block.vector
            def _(e):
                replay('dve', e)

            @block.gpsimd
            def _(e):
                replay('pool', e)

            @block.sync
            def _(e):
                replay('sp', e)
        self.stack.close()

from concourse.bass_utils import run_bass_kernel_spmd

D = 1024
NTOK = 2560
TT = 512
NT = 5
DFF = 2816
EPS = 1e-6
SEQS = [(0, 256, False), (256, 256, False), (512, 2048, True)]
INW = 2848


class K:
    pass


def build(cfg):
    nc = bass.Bass("TRN2", target_bir_lowering=False)
    P = Prog(nc)
    k = K()
    k.nc, k.P, k.cfg = nc, P, cfg
    L = cfg.get('layers', 2)

    def din(name, shape):
        return nc.dram_tensor(name, list(shape), F32, kind="ExternalInput").ap()

    def dout(name, shape):
        return nc.dram_tensor(name, list(shape), F32, kind="ExternalOutput").ap()

    k.xT = din("xT", [D, NTOK])
    k.condT = din("condT", [128, 8, 2])
    k.ada_w = din("ada_w", [2, D, 9 * D])
    k.ada_bT = din("ada_bT", [2, 128, 72])
    k.norm_gT = din("norm_gT", [128, 2, 3, 8])
    k.final_gT = din("final_gT", [128, 8])
    k.ffn_w_in = din("ffn_w_in", [2, 2, D, 2 * DFF])
    k.ffn_w_out = din("ffn_w_out", [2, 2, DFF, D])
    k.identD = din("ident", [128, 128])
    k.yT = dout("yT", [D, NTOK])
    if cfg.get('mix', True):
        mix_decl(k, din, dout)

    k.x = P.sb("x", [128, 8, NTOK], F32)
    k.h = P.sb("h", [128, 8, NTOK], BF16)
    k.ident = P.sb("identS", [128, 128], F32)
    k.ones_bf = P.sb("ones_bf", [128, 128], BF16)
    k.modT = P.sb("modT", [128, 72, 2], F32)
    k.dv = P.sb("dv", [128, 2, 6, 8], F32)
    k.ngT = P.sb("ngT", [128, 2, 3, 8], F32)
    k.fgT = P.sb("fgT", [128, 8], F32)
    k.abT = P.sb("abT", [128, 2, 72], F32)
    k.cond = P.sb("cond", [128, 8, 2], F32)
    k.sc = P.sb("sc", [128, 8, 2], BF16)
    k.rstd = P.sb("rstd", [128, 512], F32)
    k.row = k.rstd
    k.sq = [P.sb(f"sq{i}", [128, 512], BF16) for i in range(2)]
    k.tmp = [P.sb(f"tmp{i}", [128, 512], F32) for i in range(2)]
    k.W8 = P.sb("W8", [128, 4, 8, 256], BF16)
    k.wg = [k.W8[:, 2 * i] for i in range(2)]
    k.wu = [k.W8[:, 2 * i + 1] for i in range(2)]
    k.aux = P.sb("aux", [128, 3584], F32)
    auxb = k.aux[:].bitcast(BF16)
    k.wo = [auxb[:, i * 2048:(i + 1) * 2048].rearrange("p (a b) -> p a b", a=2) for i in range(2)]
    k.actt = [auxb[:, 4096 + i * 1024:4096 + (i + 1) * 1024].rearrange("p (a b) -> p a b", a=2) for i in range(2)]
    k.aux_ptr = 0
    k.sg = [P.sb(f"sg{i}", [128, 512], F32) for i in range(2)]
    k.psb = [P.ps(f"pb{i}", [128, 512], F32) for i in range(8)]
    k.bank_i = 0
    k.cnt = {}

    k.reserved = set()

    k.bank_pool = None
    k.tid = ''

    def bank():
        if k.bank_pool is not None:
            lo, n = k.bank_pool
            c = k.cnt.get(('bp', lo), 0)
            k.cnt[('bp', lo)] = c + 1
            return lo + c % n
        b = k.bank_i
        while b in k.reserved:
            b = (b + 1) % 8
        k.bank_i = (b + 1) % 8
        return b
    k.bank = bank

    def rot(name, n):
        c = k.cnt.get(name, 0)
        k.cnt[name] = c + 1
        return c % n
    k.rot = rot

    P.add('sp', lambda e: e.dma_start(out=k.ident[:], in_=k.identD), writes=['ident'], chan='c_ident')
    P.add('sp', lambda e: e.dma_start(out=k.ngT[:], in_=k.norm_gT), writes=['ngT'], chan='c_ngT')
    P.add('sp', lambda e: e.dma_start(out=k.fgT[:], in_=k.final_gT), writes=['fgT'], chan='c_fgT')
    P.add('sp', lambda e: e.dma_start(out=k.abT[:], in_=k.ada_bT.rearrange("l p c -> p l c")), writes=['abT'], chan='c_abT')
    P.add('sp', lambda e: e.dma_start(out=k.cond[:], in_=k.condT), writes=['cond'], chan='c_cond')
    for kk in range(8):
        P.add('sp', lambda e, kk=kk: e.dma_start(out=k.x[:, kk, :], in_=k.xT[kk * 128:(kk + 1) * 128, :]),
              writes=[f'x{tt}_{kk}' for tt in range(NT)], chan=f'c_xin{kk}')
    P.add('dve', lambda e: e.memset(k.ones_bf[:], 1.0), writes=['ones_bf'])
    P.add('act', lambda e: e.activation(k.sc[:], k.cond[:], AF.Silu), reads=['cond'], writes=['sc'])
    if cfg.get('mix', True):
        mix_setup(k)

    for l in range(L):
        mod_phase(k, l)
        ffn_phase(k, l, 0, 0, 0)
        if cfg.get('mix', True):
            mixer_phase(k, l)
        ffn_phase(k, l, 1, 2, 2)
    final_phase(k)
    P.final_wait('sp')
    P.emit()
    return nc


def tsl(tt):
    return slice(tt * TT, (tt + 1) * TT)


def mod_phase(k, l):
    P = k.P
    awv = k.ada_w[l].rearrange("(kk p) c -> p kk c", p=128)
    for pc in range(18):
        s = k.rot('w8', 2)
        P.add('pool', lambda e, s=s, pc=pc: e.dma_start(out=k.wg[s], in_=awv[:, :, pc * 512:pc * 512 + 256]),
              writes=[f'wg{s}'], chan=f'c_wg{s}')
        P.add('pool', lambda e, s=s, pc=pc: e.dma_start(out=k.wu[s], in_=awv[:, :, pc * 512 + 256:pc * 512 + 512]),
              writes=[f'wu{s}'], chan=f'c_wu{s}')
        pb = k.bank()
        for half, wt, wk in ((0, k.wg[s], f'wg{s}'), (1, k.wu[s], f'wu{s}')):
            for kk in range(8):
                P.add('pe', lambda e, wt=wt, kk=kk, pb=pb, half=half: e.matmul(
                    k.psb[pb][0:2, half * 256:(half + 1) * 256], k.sc[:, kk, :], wt[:, kk, :],
                    start=(kk == 0), stop=(kk == 7)), reads=[wk, 'sc'], writes=[f'ps{pb}'])
        P.add('dve', lambda e, pb=pb: e.tensor_copy(k.row[0:2, :], k.psb[pb][0:2, :]), reads=[f'ps{pb}'], writes=['rstd'])
        pb2 = k.bank()
        for i in range(4):
            P.add('pe', lambda e, i=i, pb2=pb2: e.transpose(k.psb[pb2][:, 2 * i:2 * i + 2], k.row[0:2, i * 128:(i + 1) * 128],
                                                           k.ident[0:2, 0:2]), reads=['rstd', 'ident'], writes=[f'ps{pb2}'])
        P.add('dve', lambda e, pc=pc, pb2=pb2: e.tensor_tensor(
            k.modT[:, pc * 4:(pc + 1) * 4, :], k.psb[pb2][:, 0:8].rearrange("p (c j) -> p c j", j=2),
            k.abT[:, l, pc * 4:(pc + 1) * 4].unsqueeze(2).broadcast_to([128, 4, 2]), ALU.add),
            reads=[f'ps{pb2}', 'abT'], writes=['modT'])
    m4 = k.modT[:].rearrange("p (i kk) j -> p i kk j", kk=8)
    for j in range(2):
        for n in range(3):
            P.add('dve', lambda e, j=j, n=n: e.scalar_tensor_tensor(
                k.dv[:, j, n, :], m4[:, 3 * n, :, j], 1.0, k.ngT[:, l, n, :], ALU.add, ALU.mult),
                reads=['modT', 'ngT'], writes=['dv'])
            sc_ = 1.0 if n == 1 else 0.5
            P.add('dve', lambda e, j=j, n=n, sc_=sc_: e.tensor_scalar_mul(
                k.dv[:, j, 3 + n, :], m4[:, 3 * n + 2, :, j], sc_),
                reads=['modT'], writes=['dv'])


def norm_phase(k, l, n, out_h=True):
    P = k.P
    for tt in range(NT):
        j = 0 if tt == 0 else 1
        pb = k.bank()
        for kk in range(8):
            s = k.rot('sq', 2)
            P.add('act', lambda e, s=s, kk=kk, tt=tt: e.activation(k.sq[s][:], k.x[:, kk, tsl(tt)], AF.Square),
                  reads=[f'x{tt}_{kk}'], writes=[f'sq{s}'])
            P.add('pe', lambda e, s=s, kk=kk, pb=pb: e.matmul(k.psb[pb][:], k.ones_bf[:], k.sq[s][:], start=(kk == 0), stop=(kk == 7)),
                  reads=[f'sq{s}', 'ones_bf'], writes=[f'ps{pb}'])
        P.add('act', lambda e, pb=pb: e.activation(k.rstd[:], k.psb[pb][:], AF.Sqrt, bias=EPS, scale=1.0 / D),
              reads=[f'ps{pb}'], writes=['rstd'])
        P.add('dve', lambda e: e.reciprocal(k.rstd[:], k.rstd[:]), reads=['rstd'], writes=['rstd'])
        for kk in range(8):
            s = k.rot('tmp', 2)
            P.add('dve', lambda e, s=s, kk=kk, tt=tt: e.tensor_tensor(k.tmp[s][:], k.x[:, kk, tsl(tt)], k.rstd[:], ALU.mult),
                  reads=[f'x{tt}_{kk}', 'rstd'], writes=[f'tmp{s}'])
            if out_h:
                P.add('act', lambda e, s=s, kk=kk, tt=tt, j=j: e.activation(
                    k.h[:, kk, tsl(tt)], k.tmp[s][:], AF.Identity,
                    bias=k.modT[:, (3 * n + 1) * 8 + kk, j:j + 1], scale=k.dv[:, j, n, kk:kk + 1]),
                    reads=[f'tmp{s}', 'dv', 'modT'], writes=[f'h{tt}_{kk}'])
            else:
                s2 = k.rot('sg', 2)
                P.add('act', lambda e, s=s, s2=s2, kk=kk: e.activation(
                    k.sg[s2][:], k.tmp[s][:], AF.Identity, bias=0.0, scale=k.fgT[:, kk:kk + 1]),
                    reads=[f'tmp{s}', 'fgT'], writes=[f'sg{s2}'])
                P.add('sp', lambda e, s2=s2, kk=kk, tt=tt: e.dma_start(out=k.yT[kk * 128:(kk + 1) * 128, tsl(tt)], in_=k.sg[s2][:]),
                      reads=[f'sg{s2}'], chan=f'c_y{s2}')


def final_phase(k):
    norm_phase(k, 0, 0, out_h=False)


def ffn_phase(k, l, i, n, gi):
    P = k.P
    norm_phase(k, l, n)
    wiv = k.ffn_w_in[l, i].rearrange("(kk p) c -> p kk c", p=128)
    wov = k.ffn_w_out[l, i].rearrange("(kk p) c -> p kk c", p=128)
    for g in range(11):
        s = k.rot('w8', 2)
        so = k.rot('wo', 2)
        P.add('pool', lambda e, s=s, g=g: e.dma_start(out=k.wg[s], in_=wiv[:, :, g * 256:(g + 1) * 256]),
              writes=[f'wg{s}'], chan=f'c_wg{s}')
        P.add('pool', lambda e, s=s, g=g: e.dma_start(out=k.wu[s], in_=wiv[:, :, DFF + g * 256:DFF + (g + 1) * 256]),
              writes=[f'wu{s}'], chan=f'c_wu{s}')
        P.add('pool', lambda e, so=so, g=g: e.dma_start(out=k.wo[so], in_=wov[:, 2 * g:2 * g + 2, :]),
              writes=[f'wo{so}'], chan=f'c_wo{so}')
        for tt in range(NT):
            j = 0 if tt == 0 else 1
            bg = [k.bank(), k.bank()]
            bu = [k.bank(), k.bank()]
            for c in range(2):
                for wt, wk, bb in ((k.wg[s], f'wg{s}', bg[c]), (k.wu[s], f'wu{s}', bu[c])):
                    for kk in range(8):
                        P.add('pe', lambda e, wt=wt, kk=kk, bb=bb, c=c, tt=tt: e.matmul(
                            k.psb[bb][:], wt[:, kk, c * 128:(c + 1) * 128], k.h[:, kk, tsl(tt)],
                            start=(kk == 0), stop=(kk == 7)), reads=[wk, f'h{tt}_{kk}'], writes=[f'ps{bb}'])
            sa = k.rot('actt', 2)
            for c in range(2):
                s2 = k.rot('sg', 2)
                P.add('act', lambda e, s2=s2, c=c, bg=bg: e.activation(k.sg[s2][:], k.psb[bg[c]][:], AF.Silu),
                      reads=[f'ps{bg[c]}'], writes=[f'sg{s2}'])
                P.add('dve', lambda e, s2=s2, c=c, bu=bu, sa=sa: e.tensor_tensor(k.actt[sa][:, c, :], k.sg[s2][:], k.psb[bu[c]][:], ALU.mult),
                      reads=[f'sg{s2}', f'ps{bu[c]}'], writes=[f'actt{sa}_{c}'])
            for o in range(8):
                bo = k.bank()
                for c in range(2):
                    P.add('pe', lambda e, o=o, c=c, bo=bo, so=so, sa=sa: e.matmul(
                        k.psb[bo][:], k.wo[so][:, c, o * 128:(o + 1) * 128], k.actt[sa][:, c, :],
                        start=(c == 0), stop=(c == 1)), reads=[f'wo{so}', f'actt{sa}_{c}'], writes=[f'ps{bo}'])
                P.add('dve', lambda e, o=o, bo=bo, tt=tt, j=j: e.scalar_tensor_tensor(
                    k.x[:, o, tsl(tt)], k.psb[bo][:], k.dv[:, j, 3 + gi, o:o + 1], k.x[:, o, tsl(tt)], ALU.mult, ALU.add),
                    reads=[f'ps{bo}', 'dv', f'x{tt}_{o}'], writes=[f'x{tt}_{o}'])

ZQ, ZK, ZV, ZO = 0, 256, 512, 768
ZGQ, ZGK, ZGV, ZGZ = 1024, 1280, 1536, 1792
ZAQ, ZAK, ZAV = 2048, 2560, 2688
ZMI, ZMF, ZGA, ZGB = 2816, 2824, 2832, 2840
PAST = 512


def mix_decl(k, din, dout):
    nc = k.nc
    k.w_in = din("w_in_p", [2, D, INW])
    k.w_out = din("w_out", [2, D, D])
    k.ckT = din("ckT", [2, 128, PAST])
    k.cv = din("cv", [2, PAST, 128])
    k.C0 = din("C0", [2, 2, 128, 2, 65])
    k.m0 = din("m0", [2, 2, 4])
    k.S0 = din("S0", [2, 2, 128, 2, 64])
    k.gpar = din("gpar", [128, 2, 2, 3])
    k.fpar = din("fpar", [128, 2, 5])
    k.convw = din("convw", [128, 2, 6, 5])
    k.cosD = din("cosT", [128, 2048])
    k.sinD = din("sinT", [128, 2048])
    k.permD = din("perm", [128, 128])
    k.masksD = din("masks", [128, 10, 64])
    k.selD = din("sel", [128, 32, 4])
    k.bonesD = din("bones", [128, 128])
    if k.cfg.get('dbg'):
        k.zT = dout("zT_scr", [INW, NTOK])
        k.kTaD = nc.dram_tensor("kTaD", [128, PAST + 2048], BF16, kind="ExternalOutput").ap()
        k.QTD = nc.dram_tensor("QTD", [128, 4, 2048], BF16, kind="ExternalOutput").ap()
        k.VaD = nc.dram_tensor("VaD", [128, 2600], BF16, kind="ExternalOutput").ap()
        k.FBD = dout("FBD", [128, 6, 2048])
        k.DD = dout("DD", [16, 128, 2, 64])
        k.GSD = dout("GSD", [128, 6, 2])
        k.RdD = dout("RdD", [2, 128, 6, 64])
        k.RmD = dout("RmD", [2, 128, 6, 64])
    else:
        k.zT = nc.dram_tensor("zT_scr", [INW, NTOK], F32).ap()
    k.xpark = nc.dram_tensor("xpark", [128, 8 * NTOK], F32).ap()
    k.nkT = dout("nkT", [2, 128, 512])
    k.nvT = dout("nvT", [2, 128, 512])
    k.Cout = dout("Cout", [2, 2, 2, 128, 2, 65])
    k.Sout = dout("Sout", [2, 2, 2, 128, 2, 64])
    k.mout = dout("mout", [2, 2, 2, 4])
    if k.cfg.get('dbg'):
        k.catD = nc.dram_tensor("catD", [128, 8, NTOK], BF16, kind="ExternalOutput").ap()


def mix_setup(k):
    P = k.P
    k.wout = k.W8[:].rearrange("p a b c -> p (a b c)").rearrange("p (kk c) -> p kk c", kk=8)
    k.gparS = P.sb("gparS", [128, 2, 2, 3], F32)
    k.negA = P.sb("negA", [128, 2, 2], F32)
    k.fparS = P.sb("fparS", [128, 2, 5], F32)
    k.convS = P.sb("convS", [128, 2, 6, 5], F32)
    k.permS = P.sb("permS", [128, 128], F32)
    k.masks = P.sb("masks", [128, 10, 64], F32)
    k.sel = P.sb("sel", [128, 32, 4], F32)
    k.bones = P.sb("bones", [128, 128], F32)
    for nm, t, dsrc in (("gparS", k.gparS, k.gpar), ("fparS", k.fparS, k.fpar), ("convS", k.convS, k.convw),
                        ("permS", k.permS, k.permD),
                        ("masks", k.masks, k.masksD), ("sel", k.sel, k.selD), ("bones", k.bones, k.bonesD)):
        P.add('sp', lambda e, t=t, dsrc=dsrc: e.dma_start(out=t[:], in_=dsrc), writes=[nm], chan='c_' + nm)
    P.add('act', lambda e: e.activation(k.negA[:], k.gparS[:, :, :, 1], AF.Exp), reads=['gparS'], writes=['negA'])
    P.add('dve', lambda e: e.tensor_scalar_mul(k.negA[:], k.negA[:], -1.0), reads=['negA'], writes=['negA'])
    xb = k.x[:].rearrange("p a b -> p (a b)")
    k.FB = xb[:, 0:12288].rearrange("p (a b) -> p a b", a=6)
    k.HF = xb[:, 12288:16384].rearrange("p (a b) -> p a b", a=2)
    k.QT = xb[:, 16384:20480].bitcast(BF16).rearrange("p (a b) -> p a b", a=4)
    k.FBb = xb[:, 0:6144].bitcast(BF16).rearrange("p (a b) -> p a b", a=6)
    k.FBt = xb[:, 6144:8192]
    shb = xb[:, 8192:8192 + 260].bitcast(BF16)
    k.Cstb = [shb[:, i * 130:(i + 1) * 130].rearrange("p (a b) -> p a b", a=2) for i in range(2)]
    k.Sstb = [shb[:, 260 + i * 128:260 + (i + 1) * 128].rearrange("p (a b) -> p a b", a=2) for i in range(2)]
    k.cosS = k.FB[:, 0, :]
    k.sinS = k.FB[:, 1, :]
    k.kTa = k.FB[:, 2, :].bitcast(BF16)[:, 0:PAST + 2048]
    k.Va = k.FB[:, 3, :].bitcast(BF16)[:, 0:2600].rearrange("p (a g e) -> p a g e", a=20, g=2)
    k.PT = [k.FB[:, 4, :].bitcast(BF16)[:, i * 512:(i + 1) * 512] for i in range(2)]
    k.stg = [k.HF[:, i, :] for i in range(2)]
    k.Gin = [P.sb(f"Gin{d}", [128, 4, 64], F32) for d in range(2)]
    k.Rm = [P.sb(f"Rm{d}", [128, 6, 64], F32) for d in range(2)]
    k.Rd = [P.sb(f"Rd{d}", [128, 6, 64], F32) for d in range(2)]
    k.gw = [P.sb(f"gw{i}", [128, 64], F32) for i in range(6)]
    k.ST = P.sb("ST", [128, 8], F32)
    k.PC = P.sb("PC", [128, 8], F32)
    k.RW = P.sb("RW", [1, 3, 128], F32)
    k.MM = [P.sb(f"MM{d}", [1, 4, 33], F32) for d in range(2)]
    k.MR = P.sb("MR", [1, 3, 4, 32], F32)
    k.m0t = [P.sb(f"m0t{d}", [1, 4], F32) for d in range(2)]
    k.Cst = [P.sb(f"Cst{d}", [128, 2, 65], F32) for d in range(2)]
    k.Sst = [P.sb(f"Sst{d}", [128, 2, 64], F32) for d in range(2)]
    k.wk = {}
    k.identb = P.sb("identb", [128, 128], BF16)
    P.add('dve', lambda e: e.tensor_copy(k.identb[:], k.ident[:]), reads=['ident'], writes=['identb'])

    def work(name, shape, dtype=F32, n=None):
        if n is None:
            n = 2 if name in ('GS', 'RX') else 1
        name = name + k.tid
        if name not in k.wk:
            lst_ = []
            for i in range(n):
                sz = int(np.prod(shape[1:]))
                if dtype == BF16:
                    sz = (sz + 1) // 2
                if k.aux_ptr + sz <= 3072:
                    v = k.aux[:, k.aux_ptr:k.aux_ptr + sz]
                    if dtype == BF16:
                        v = v.bitcast(BF16)[:, 0:int(np.prod(shape[1:]))]
                    k.aux_ptr += sz
                    if len(shape) == 3:
                        v = v.rearrange("p (a b) -> p a b", a=shape[1])
                    elif len(shape) == 4:
                        v = v.rearrange("p (a b c) -> p a b c", a=shape[1], b=shape[2])
                    lst_.append(v)
                else:
                    lst_.append(P.sb(f"wk_{name}{i}", shape, dtype))
            k.wk[name] = lst_
        lst = k.wk[name]
        i = k.rot('wk_' + name, len(lst))
        return lst[i], f'wk_{name}{i}'
    k.work = work


def MM(e, out, lhsT, rhs, start=True, stop=True):
    kp = lhsT.partition_size()
    mp = out.partition_size()
    if kp < 128:
        return e.matmul(out, lhsT, rhs, start=start, stop=stop, tile_position=(lhsT.base_partition(), out.base_partition()))
    return e.matmul(out, lhsT, rhs, start=start, stop=stop)


def zkeys(blk, seq):
    tok0, T, samp = seq
    return [f'zT{blk}_{tt}' for tt in range(tok0 // TT, (tok0 + T + TT - 1) // TT)]


def head_sl(h):
    return slice(64 * (h % 2), 64 * (h % 2) + 64), h // 2


def cumsum64(k, src, skey, d):
    P = k.P
    cur, ckey = src, skey
    for si, s in enumerate((1, 2, 4, 8, 16, 32)):
        dst = k.gw[4 + (si % 2)][:]
        dkey = f'gw{4 + (si % 2)}'
        P.add('act', lambda e, dst=dst, cur=cur: e.activation(dst, cur, AF.Identity), reads=[ckey], writes=[dkey])
        if d == 0:
            P.add('dve', lambda e, dst=dst, cur=cur, s=s: e.tensor_tensor(dst[:, s:64], cur[:, s:64], cur[:, 0:64 - s], ALU.add),
                  reads=[ckey, dkey], writes=[dkey])
        else:
            P.add('dve', lambda e, dst=dst, cur=cur, s=s: e.tensor_tensor(dst[:, 0:64 - s], cur[:, 0:64 - s], cur[:, s:64], ALU.add),
                  reads=[ckey, dkey], writes=[dkey])
        cur, ckey = dst, dkey
    return cur, ckey


def gate_prepass(k, l, seq, d, m0_src):
    P = k.P
    tok0, T, samp = seq
    nch = T // 64
    G = k.Gin[d]
    RmF, RdF = k.Rm[d], k.Rd[d]
    Rm, Rd = RmF[:, :, 0:64], RdF[:, :, 0:64]
    last = 63 if d == 0 else 0
    P.add('dve', lambda e: e.memset(G[:], 0.0), writes=[f'Gin{d}'])
    for q, row0 in enumerate((ZMI, ZMF, ZGA, ZGB)):
        for h in range(4):
            r = row0 + d * 4 + h
            src = k.zT[r:r + 1, tok0:tok0 + T].rearrange("o (c t) -> (o c) t", t=64)
            P.add('sp', lambda e, q=q, h=h, src=src: e.dma_start(out=G[32 * h:32 * h + nch, q, :], in_=src),
                  reads=zkeys(22, seq), writes=[f'Gin{d}'], chan=f'c_gin{d}')
    gp = k.gparS
    gw = k.gw
    P.add('act', lambda e: e.activation(gw[0][:], G[:, 1, :], AF.Exp, bias=gp[:, l, d, 0:1], scale=-1.0),
          reads=[f'Gin{d}', 'gparS'], writes=['gw0'])
    P.add('act', lambda e: e.activation(gw[0][:], gw[0][:], AF.Ln, bias=1.0, scale=1.0), reads=['gw0'], writes=['gw0'])
    P.add('dve', lambda e: e.tensor_scalar_mul(Rm[:, 0, :], gw[0][:], -1.0), reads=['gw0'], writes=[f'Rm{d}'])
    b, bkey = cumsum64(k, Rm[:, 0, :], f'Rm{d}', d)
    ST = k.ST
    P.add('dve', lambda e: e.scalar_tensor_tensor(gw[1][:], G[:, 0, :], b[:, last:last + 1], b, ALU.add, ALU.subtract),
          reads=[f'Gin{d}', bkey], writes=['gw1'])
    P.add('dve', lambda e: e.tensor_copy(ST[:, 0:1], b[:, last:last + 1]), reads=[bkey], writes=['ST'])
    P.add('dve', lambda e: e.tensor_reduce(ST[:, 1:2], gw[1][:], AX.X, ALU.max), reads=['gw1'], writes=['ST'])
    P.add('dve', lambda e: e.tensor_reduce(ST[:, 2:3], G[:, 0, :], AX.X, ALU.max), reads=[f'Gin{d}'], writes=['ST'])
    pb = k.bank()
    for q in range(3):
        P.add('pe', lambda e, q=q, pb=pb: e.transpose(k.psb[pb][0:1, q * 128:(q + 1) * 128], ST[:, q:q + 1], k.ident[:]),
              reads=['ST', 'ident'], writes=[f'ps{pb}'])
    P.add('dve', lambda e, pb=pb: e.tensor_copy(k.RW[:].rearrange("o q n -> o (q n)"), k.psb[pb][0:1, 0:384]),
          reads=[f'ps{pb}'], writes=['RW'])
    RW4 = k.RW[:].rearrange("o q (h c) -> o q h c", h=4)
    MM = k.MM[d]
    mk = f'MM{d}'
    init_c = 0 if d == 0 else nch
    if m0_src is None:
        P.add('dve', lambda e: e.memset(MM[:], 0.0), writes=[mk])
    else:
        P.add('dve', lambda e: e.memset(MM[:], 0.0), writes=[mk])
        P.add('sp', lambda e: e.dma_start(out=k.m0t[d][:], in_=m0_src), writes=[f'm0t{d}'], chan=f'c_m0{d}')
        P.add('dve', lambda e: e.tensor_copy(MM[:, :, init_c], k.m0t[d][:]), reads=[f'm0t{d}'], writes=[mk])
    for s in range(nch):
        c = s if d == 0 else nch - 1 - s
        cb, ca = (c, c + 1) if d == 0 else (c + 1, c)
        P.add('dve', lambda e, c=c, cb=cb, ca=ca: e.tensor_tensor(MM[:, :, ca], MM[:, :, cb], RW4[:, 0, :, c], ALU.add),
              reads=[mk, 'RW'], writes=[mk])
        P.add('dve', lambda e, c=c, ca=ca: e.tensor_tensor(MM[:, :, ca], MM[:, :, ca], RW4[:, 1, :, c], ALU.max),
              reads=[mk, 'RW'], writes=[mk])
    MR = k.MR
    bsl, asl = (slice(0, nch), slice(1, nch + 1)) if d == 0 else (slice(1, nch + 1), slice(0, nch))
    P.add('dve', lambda e: e.memset(MR[:], 0.0), writes=['MR'])
    P.add('dve', lambda e: e.tensor_copy(MR[:, 0, :, 0:nch], MM[:, :, bsl]), reads=[mk], writes=['MR'])
    P.add('dve', lambda e: e.tensor_copy(MR[:, 1, :, 0:nch], MM[:, :, asl]), reads=[mk], writes=['MR'])
    P.add('dve', lambda e: e.tensor_tensor(MR[:, 2, :, 0:nch], MM[:, :, bsl], RW4[:, 2, :, 0:nch], ALU.max),
          reads=[mk, 'RW'], writes=['MR'])
    pb = k.bank()
    for q in range(3):
        P.add('pe', lambda e, q=q, pb=pb: e.transpose(k.psb[pb][:, q:q + 1], MR[:, q, :, :].rearrange("o h c -> o (h c)"),
                                                     k.ident[0:1, 0:1]), reads=['MR', 'ident'], writes=[f'ps{pb}'])
    PC = k.PC
    P.add('dve', lambda e, pb=pb: e.tensor_copy(PC[:, 0:3], k.psb[pb][:, 0:3]), reads=[f'ps{pb}'], writes=['PC'])
    P.add('dve', lambda e: e.tensor_tensor(PC[:, 3:4], PC[:, 0:1], PC[:, 2:3], ALU.subtract), reads=['PC'], writes=['PC'])
    P.add('dve', lambda e: e.tensor_tensor(PC[:, 4:5], PC[:, 0:1], PC[:, 1:2], ALU.subtract), reads=['PC'], writes=['PC'])
    P.add('dve', lambda e: e.tensor_tensor(PC[:, 4:5], PC[:, 4:5], ST[:, 0:1], ALU.add), reads=['PC', 'ST'], writes=['PC'])
    rk = f'Rm{d}'
    P.add('dve', lambda e: e.tensor_scalar(Rm[:, 1, :], G[:, 0, :], PC[:, 2:3], 0.0, ALU.subtract, ALU.add), reads=[f'Gin{d}', 'PC'], writes=[rk])
    P.add('dve', lambda e: e.tensor_scalar(Rm[:, 2, :], b, PC[:, 3:4], 0.0, ALU.add, ALU.add), reads=[bkey, 'PC'], writes=[rk])
    P.add('dve', lambda e: e.tensor_scalar(Rm[:, 3, :], gw[1][:], PC[:, 1:2], 0.0, ALU.subtract, ALU.add), reads=['gw1', 'PC'], writes=[rk])
    P.add('dve', lambda e: e.tensor_scalar(Rm[:, 4, :], b, 0.0, PC[:, 2:3], ALU.mult, ALU.subtract), reads=[bkey, 'PC'], writes=[rk])
    P.add('dve', lambda e: e.tensor_scalar(Rm[:, 5, :], b, 0.0, PC[:, 4:5], ALU.mult, ALU.add), reads=[bkey, 'PC'], writes=[rk])
    P.add('act', lambda e: e.activation(Rm[:, 2:6, :], Rm[:, 2:6, :], AF.Exp), reads=[rk], writes=[rk])
    dk = f'Rd{d}'
    P.add('act', lambda e: e.activation(gw[2][:], G[:, 2, :], AF.Exp, bias=gp[:, l, d, 2:3], scale=1.0),
          reads=[f'Gin{d}', 'gparS'], writes=['gw2'])
    P.add('act', lambda e: e.activation(gw[2][:], gw[2][:], AF.Ln, bias=1.0, scale=1.0), reads=['gw2'], writes=['gw2'])
    P.add('dve', lambda e: e.tensor_scalar(Rd[:, 0, :], gw[2][:], k.negA[:, l, d:d + 1], 0.0, ALU.mult, ALU.add), reads=['gw2', 'negA'], writes=[dk])
    P.add('act', lambda e: e.activation(Rd[:, 1, :], G[:, 3, :], AF.Sigmoid), reads=[f'Gin{d}'], writes=[dk])
    gc, gckey = cumsum64(k, Rd[:, 0, :], dk, d)
    P.add('act', lambda e: e.activation(Rd[:, 2, :], gc, AF.Exp), reads=[gckey], writes=[dk])
    P.add('act', lambda e: e.activation(Rd[:, 3, :], gc, AF.Exp, bias=gc[:, last:last + 1], scale=-1.0), reads=[gckey], writes=[dk])
    P.add('dve', lambda e: e.scalar_tensor_tensor(Rd[:, 4, :], Rd[:, 1, :], -1.0, Rd[:, 2, :], ALU.mult, ALU.mult), reads=[dk], writes=[dk])
    P.add('act', lambda e: e.activation(Rd[:, 5, :], gc, AF.Exp, bias=gc[:, last:last + 1], scale=0.0), reads=[gckey], writes=[dk])


def load_F(k, row0, nrows_tiles, seq, dst_list, eng='sp'):
    P = k.P
    tok0, T, samp = seq
    for i, (dst, dkey) in enumerate(dst_list):
        r = row0 + 128 * i
        blk = r // 128
        P.add(eng, lambda e, dst=dst, r=r: e.dma_start(out=dst, in_=k.zT[r:r + 128, tok0:tok0 + T]),
              reads=zkeys(blk, seq), writes=[dkey], chan='c_' + dkey)


def gs_all(k, R, rkey, nch, name):
    P = k.P
    if name not in k.wk:
        k.wk[name] = P.sb("gsall_" + name, [128, 32, 6, 2], F32)
    G = k.wk[name]
    gkey = 'gsall_' + name
    n = nch * 4
    rhs = k.sel[:, 0:nch, :].rearrange("p c h -> p (c h)")
    for q0, nq in ((0, 4), (4, 2)):
        pb = k.bank()
        for qq in range(nq):
            for a in range(2):
                P.add('pe', lambda e, qq=qq, q0=q0, pb=pb, a=a: MM(e, k.psb[pb][64 * a:64 * a + 64, qq * 128:qq * 128 + n], R[:, q0 + qq, :], rhs, start=True, stop=True),
                      reads=[rkey, 'sel'], writes=[f'ps{pb}'])
        for a in range(2):
            sl = slice(64 * a, 64 * a + 64)
            src = k.psb[pb][sl, 0:nq * 128].rearrange("p (q c hh a) -> p q c hh a", q=nq, c=32, hh=2)[:, :, 0:nch, :, a]
            dst = G[sl, 0:nch, q0:q0 + nq, :].rearrange("p c q hh -> p q c hh")
            P.add('dve', lambda e, src=src, dst=dst: e.tensor_copy(dst, src), reads=[f'ps{pb}'], writes=[gkey])
    return G, gkey


def to_T(k, t0, cols, name, aug=False, on='act'):
    P = k.P
    pb = k.bank()
    for hh in range(2):
        for a in range(2):
            sl = slice(64 * a, 64 * a + 64)
            P.add('pe', lambda e, hh=hh, sl=sl, pb=pb: MM(e, k.psb[pb][sl, hh * 64:(hh + 1) * 64], k.FBb[sl, t0 + hh, cols],
                                                        k.identb[sl, sl], start=True, stop=True),
                  reads=[f'FB{t0 + hh}', 'identb'], writes=[f'ps{pb}'])
    W = 65 if aug else 64
    t, tkey = k.work(name, [128, 2, W])
    src = k.psb[pb][:, 0:128].rearrange("p (h e) -> p h e", h=2)
    if on == 'act':
        P.add('act', lambda e: e.activation(t[:, :, 0:64], src, AF.Identity), reads=[f'ps{pb}'], writes=[tkey])
    else:
        P.add('dve', lambda e: e.tensor_copy(t[:, :, 0:64], src), reads=[f'ps{pb}'], writes=[tkey])
    if aug:
        P.add('dve', lambda e: e.memset(t[:, :, 64:65], 1.0), writes=[tkey])
    return t, tkey


def out_to_HF(k, Hout, hkey, cols):
    P = k.P
    pb = k.bank()
    for hh in range(2):
        for a in range(2):
            sl = slice(64 * a, 64 * a + 64)
            P.add('pe', lambda e, hh=hh, sl=sl, pb=pb: MM(e, k.psb[pb][sl, hh * 64:(hh + 1) * 64], Hout[sl, hh, 0:64], k.ident[sl, sl], start=True, stop=True),
                  reads=[hkey, 'ident'], writes=[f'ps{pb}'])
    P.add('dve', lambda e, pb=pb: e.tensor_tensor(k.HF[:, :, cols], k.HF[:, :, cols],
                                                  k.psb[pb][:, 0:128].rearrange("p (a b) -> p a b", a=2), ALU.add),
          reads=[f'ps{pb}', 'HF0', 'HF1'], writes=['HF0', 'HF1'])


def hmm(k, pb, w, lhs_fn, rhs_fn, reads, start=True, stop=True):
    P = k.P
    for hh in range(2):
        for a in range(2):
            sl = slice(64 * a, 64 * a + 64)
            P.add('pe', lambda e, hh=hh, sl=sl: MM(e, k.psb[pb][sl, hh * w:(hh + 1) * w], lhs_fn(sl, hh), rhs_fn(sl, hh), start=start, stop=stop),
                  reads=reads, writes=[f'ps{pb}'])


def pv3(k, pb, w):
    return k.psb[pb][:, 0:2 * w].rearrange("p (h e) -> p h e", h=2)


def mlstm_p1(k, seq, d, c):
    P = k.P
    cols = slice(c * 64, (c + 1) * 64)
    MLE = k.masks[:, 2 * d, :]
    B3 = [128, 2, 64]
    GS, gkey = k.gsall[('m', d)][0][:, c], k.gsall[('m', d)][1]
    kTl, kkey = to_T(k, 2, cols, 'kTl')
    vA, vkey = k.work('hbA', [128, 2, 65], BF16, n=2)
    pbv = k.bank()
    for hh in range(2):
        for a in range(2):
            sl = slice(64 * a, 64 * a + 64)
            P.add('pe', lambda e, hh=hh, sl=sl: MM(e, k.psb[pbv][sl, hh * 64:(hh + 1) * 64], k.FBb[sl, 4 + hh, cols], k.identb[sl, sl], start=True, stop=True),
                  reads=[f'FB{4 + hh}', 'identb'], writes=[f'ps{pbv}'])
    P.add('act', lambda e: e.activation(vA[:, :, 0:64], pv3(k, pbv, 64), AF.Identity), reads=[f'ps{pbv}'], writes=[vkey])
    P.add('pool', lambda e: e.memset(vA[:, :, 64:65], 1.0), writes=[vkey])
    A1, akey = k.work('A1', B3)
    P.add('pool', lambda e: e.tensor_tensor(A1[:], MLE.unsqueeze(1).broadcast_to(B3), GS[:, 0, :].unsqueeze(2).broadcast_to(B3), ALU.mult),
          reads=['masks', gkey], writes=[akey])
    pC, pD = k.bank(), k.bank()
    for a in range(2):
        sl = slice(64 * a, 64 * a + 64)
        P.add('pe', lambda e, sl=sl: MM(e, k.psb[pC][sl, 0:128], k.masks[sl, 2 * d + 1, :], A1[sl, :, :].rearrange("p a b -> p (a b)"), start=True, stop=True),
              reads=['masks', akey], writes=[f'ps{pC}'])
    hmm(k, pD, 64, lambda sl, hh: k.FBb[sl, 2 + hh, cols], lambda sl, hh: k.FBb[sl, 0 + hh, cols], ['FB0', 'FB1', 'FB2', 'FB3'])
    E, ekey = k.work('E', B3)
    for hh in range(2):
        P.add('act', lambda e, hh=hh: e.activation(E[:, hh, :], k.psb[pC][:, hh * 64:(hh + 1) * 64], AF.Exp, bias=GS[:, 1, hh:hh + 1], scale=1.0),
              reads=[f'ps{pC}', gkey], writes=[ekey])
    PTm, pkey = k.work('PTm', B3)
    P.add('dve', lambda e: e.tensor_tensor(PTm[:], pv3(k, pD, 64), MLE.unsqueeze(1).broadcast_to(B3), ALU.mult),
          reads=[f'ps{pD}', 'masks'], writes=[pkey])
    PTb, pbkey = k.work('bPTm', B3, BF16)
    P.add('pool', lambda e: e.tensor_tensor(PTb[:], PTm[:], E[:], ALU.mult), reads=[pkey, ekey], writes=[pbkey])
    pE = k.bank()
    hmm(k, pE, 65, lambda sl, hh: PTb[sl, hh, :], lambda sl, hh: vA[sl, hh, :], [pbkey, vkey])
    INTRA, ikey = k.work('h1', [128, 2, 65], n=2)
    P.add('act', lambda e: e.activation(INTRA[:], pv3(k, pE, 65), AF.Identity), reads=[f'ps{pE}'], writes=[ikey])
    KW, wkey = k.work('hbK', B3, BF16, n=2)
    P.add('pool', lambda e: e.tensor_tensor(KW[:], kTl[:, :, 0:64], GS[:, 3, :].unsqueeze(2).broadcast_to(B3), ALU.mult),
          reads=[kkey, gkey], writes=[wkey])
    return dict(GS=GS, gkey=gkey, vA=vA, vkey=vkey, INTRA=INTRA, ikey=ikey, KW=KW, wkey=wkey)


def mlstm_p2(k, seq, d, c, H):
    P = k.P
    cols = slice(c * 64, (c + 1) * 64)
    B3 = [128, 2, 64]
    GS, gkey, vA, vkey, INTRA, ikey, KW, wkey = (H[x] for x in ('GS', 'gkey', 'vA', 'vkey', 'INTRA', 'ikey', 'KW', 'wkey'))
    Cst = k.Cst[d]
    pF = k.bank()
    hmm(k, pF, 65, lambda sl, hh: k.FBb[sl, 0 + hh, cols], lambda sl, hh: k.Cstb[d][sl, hh, :], ['FB0', 'FB1', f'Cstb{d}'])
    NUM, nkey = k.work('NUM', [128, 2, 65])
    P.add('dve', lambda e: e.tensor_tensor(NUM[:], pv3(k, pF, 65), GS[:, 2, :].unsqueeze(2).broadcast_to([128, 2, 65]), ALU.mult),
          reads=[f'ps{pF}', gkey], writes=[nkey])
    P.add('dve', lambda e: e.tensor_tensor(NUM[:], NUM[:], INTRA[:], ALU.add), reads=[ikey, nkey], writes=[nkey])
    DEN, dkey = k.work('DEN', [128, 2])
    P.add('act', lambda e: e.activation(DEN[:], NUM[:, :, 64], AF.Abs), reads=[nkey], writes=[dkey])
    P.add('dve', lambda e: e.tensor_tensor(DEN[:], DEN[:], GS[:, 4, :], ALU.max), reads=[dkey, gkey], writes=[dkey])
    P.add('dve', lambda e: e.reciprocal(DEN[:], DEN[:]), reads=[dkey], writes=[dkey])
    Hout, hkey = k.work('Hout', B3)
    P.add('dve', lambda e: e.tensor_tensor(Hout[:], NUM[:, :, 0:64], DEN[:].unsqueeze(2).broadcast_to(B3), ALU.mult),
          reads=[nkey, dkey], writes=[hkey])
    out_to_HF(k, Hout, hkey, cols)
    pH = k.bank()
    hmm(k, pH, 65, lambda sl, hh: KW[sl, hh, :], lambda sl, hh: vA[sl, hh, :], [wkey, vkey])
    P.add('dve', lambda e: e.tensor_tensor(Cst[:], Cst[:], GS[:, 5, :].unsqueeze(2).broadcast_to([128, 2, 65]), ALU.mult),
          reads=[f'Cst{d}', gkey], writes=[f'Cst{d}'])
    P.add('dve', lambda e: e.tensor_tensor(Cst[:], Cst[:], pv3(k, pH, 65), ALU.add), reads=[f'Cst{d}', f'ps{pH}'], writes=[f'Cst{d}'])
    P.add('pool', lambda e: e.tensor_copy(k.Cstb[d], Cst[:]), reads=[f'Cst{d}'], writes=[f'Cstb{d}'])


def post_norm(k, l, seq, src_list, normcol, gate_row0, gate_func, cat0):
    P = k.P
    tok0, T, samp = seq
    for i, (src, skey) in enumerate(src_list):
        for ct in range((T + 511) // 512):
            w = min(512, T - ct * 512)
            cs = slice(ct * 512, ct * 512 + w)
            s = k.rot('sg', 2)
            P.add('act', lambda e, s=s, src=src, cs=cs, w=w: e.activation(k.sg[s][:, 0:w], src[:, cs], AF.Square), reads=[skey], writes=[f'sg{s}'])
            pb = k.bank()
            P.add('pe', lambda e, s=s, pb=pb, w=w: MM(e, k.psb[pb][:, 0:w], k.bones[:], k.sg[s][:, 0:w], start=True, stop=True),
                  reads=[f'sg{s}', 'bones'], writes=[f'ps{pb}'])
            P.add('act', lambda e, pb=pb, w=w: e.activation(k.rstd[:, 0:w], k.psb[pb][:, 0:w], AF.Sqrt, bias=EPS, scale=1.0 / 64), reads=[f'ps{pb}'], writes=['rstd'])
            P.add('dve', lambda e, w=w: e.reciprocal(k.rstd[:, 0:w], k.rstd[:, 0:w]), reads=['rstd'], writes=['rstd'])
            s2 = k.rot('tmp', 2)
            r = gate_row0 + 128 * i
            P.add('sp', lambda e, s2=s2, r=r, cs=cs, w=w: e.dma_start(out=k.tmp[s2][:, 0:w], in_=k.zT[r:r + 128, tok0 + cs.start:tok0 + cs.start + w]),
                  reads=zkeys(r // 128, seq), writes=[f'tmp{s2}'], chan=f'c_tmp{s2}')
            P.add('act', lambda e, s2=s2, w=w: e.activation(k.tmp[s2][:, 0:w], k.tmp[s2][:, 0:w], gate_func), reads=[f'tmp{s2}'], writes=[f'tmp{s2}'])
            P.add('dve', lambda e, s2=s2, w=w: e.tensor_tensor(k.tmp[s2][:, 0:w], k.tmp[s2][:, 0:w], k.rstd[:, 0:w], ALU.mult),
                  reads=[f'tmp{s2}', 'rstd'], writes=[f'tmp{s2}'])
            nc_ = normcol + (i if normcol == 0 else 0)
            P.add('dve', lambda e, s2=s2, src=src, cs=cs, w=w, nc_=nc_, i=i: e.scalar_tensor_tensor(
                k.h[:, cat0 + i, tok0 + cs.start:tok0 + cs.start + w], src[:, cs], k.fparS[:, l, nc_:nc_ + 1], k.tmp[s2][:, 0:w], ALU.mult, ALU.mult),
                reads=[skey, f'tmp{s2}', 'fparS'], writes=[f'cat{cat0 + i}'])

def heads_F(k, t0):
    return [(k.FB[64 * (h % 2):64 * (h % 2) + 64, t0 + h // 2, :], f'FB{t0 + h // 2}', 64 * (h % 2)) for h in range(4)]


def delta_prepass(k, l, seq):
    P = k.P
    tok0, T, samp = seq
    for i in range(6):
        s = k.rot('stg', 2)
        r = ZGQ + 128 * i
        P.add('sp', lambda e, s=s, r=r: e.dma_start(out=k.stg[s][:, 0:T], in_=k.zT[r:r + 128, tok0:tok0 + T]),
              reads=zkeys(r // 128, seq), writes=[f'HF{s}'], chan=f'c_stg{s}')
        dst = k.FBt
        fk = 'FBt'
        fin = k.FBb[:, i, :]
        src = k.stg[s]
        P.add('act', lambda e, dst=dst, src=src, i=i: e.activation(dst[:, 0:T], src[:, 0:T], AF.Identity, bias=0.0, scale=k.convS[:, l, i, 2:3]),
              reads=[f'HF{s}', 'convS'], writes=[fk])
        for tap in (0, 1, 3, 4):
            sh = tap - 2
            o0, o1 = max(0, -sh), T - max(0, sh)
            P.add('dve', lambda e, dst=dst, src=src, i=i, tap=tap, sh=sh, o0=o0, o1=o1: e.scalar_tensor_tensor(
                dst[:, o0:o1], src[:, o0 + sh:o1 + sh], k.convS[:, l, i, tap:tap + 1], dst[:, o0:o1], ALU.mult, ALU.add),
                reads=[f'HF{s}', 'convS', fk], writes=[fk])
        if i < 4:
            P.add('act', lambda e, dst=dst: e.activation(dst[:, 0:T], dst[:, 0:T], AF.Silu), reads=[fk], writes=[fk])
        else:
            P.add('act', lambda e, dst=dst, fin=fin: e.activation(fin[:, 0:T], dst[:, 0:T], AF.Silu), reads=[fk], writes=[f'FB{i}'])
        if i < 4:
            for ct in range((T + 511) // 512):
                w = min(512, T - ct * 512)
                cs = slice(ct * 512, ct * 512 + w)
                s2 = k.rot('sg', 2)
                P.add('act', lambda e, s2=s2, dst=dst, cs=cs, w=w: e.activation(k.sg[s2][:, 0:w], dst[:, cs], AF.Square), reads=[fk], writes=[f'sg{s2}'])
                pb = k.bank()
                P.add('pe', lambda e, s2=s2, pb=pb, w=w: MM(e, k.psb[pb][:, 0:w], k.bones[:], k.sg[s2][:, 0:w], start=True, stop=True),
                      reads=[f'sg{s2}', 'bones'], writes=[f'ps{pb}'])
                P.add('act', lambda e, pb=pb, w=w: e.activation(k.rstd[:, 0:w], k.psb[pb][:, 0:w], AF.Sqrt, bias=EPS, scale=1.0), reads=[f'ps{pb}'], writes=['rstd'])
                P.add('dve', lambda e, w=w: e.reciprocal(k.rstd[:, 0:w], k.rstd[:, 0:w]), reads=['rstd'], writes=['rstd'])
                scl = 0.125 if i < 2 else 1.0
                P.add('dve', lambda e, dst=dst, fin=fin, cs=cs, w=w, scl=scl: e.scalar_tensor_tensor(fin[:, cs], dst[:, cs], scl, k.rstd[:, 0:w], ALU.mult, ALU.mult),
                      reads=[fk, 'rstd'], writes=[f'FB{i}'])


def delta_p1(k, seq, d, c):
    P = k.P
    cols = slice(c * 64, (c + 1) * 64)
    MLE = k.masks[:, 2 * d, :]
    MST = k.masks[:, 2 * d + 1, :]
    I2 = k.masks[:, 4, :]
    B3 = [128, 2, 64]
    bc = lambda ap: ap.unsqueeze(1).broadcast_to(B3)
    gb = lambda q: GS[:, q, :].unsqueeze(2).broadcast_to(B3)
    idb = lambda sl, hh: k.ident[sl, sl]
    GS, gkey = k.gsall[('d', d)][0][:, c], k.gsall[('d', d)][1]
    kTl, kkey = to_T(k, 2, cols, 'kTl')
    vTl, vkey = to_T(k, 4, cols, 'vA', aug=True, on='dve')
    A1, akey = k.work('A1', B3)
    P.add('pool', lambda e: e.tensor_tensor(A1[:], bc(MLE), gb(0), ALU.mult), reads=['masks', gkey], writes=[akey])
    pA, pB = k.bank(), k.bank()
    for a in range(2):
        sl = slice(64 * a, 64 * a + 64)
        P.add('pe', lambda e, sl=sl: MM(e, k.psb[pA][sl, 0:128], k.masks[sl, 2 * d + 1, :], A1[sl, :, :].rearrange("p a b -> p (a b)"), start=True, stop=True),
              reads=['masks', akey], writes=[f'ps{pA}'])
    hmm(k, pB, 64, lambda sl, hh: A1[sl, hh, :], lambda sl, hh: k.masks[sl, 2 * d + 1, :], ['masks', akey])
    DTm, dtkey = k.work('E', B3)
    DB, dbkey = k.work('PTm', B3)
    P.add('act', lambda e: e.activation(DTm[:], pv3(k, pA, 64), AF.Exp), reads=[f'ps{pA}'], writes=[dtkey])
    P.add('act', lambda e: e.activation(DB[:], pv3(k, pB, 64), AF.Exp), reads=[f'ps{pB}'], writes=[dbkey])
    P.add('pool', lambda e: e.tensor_tensor(DTm[:], DTm[:], bc(MLE), ALU.mult), reads=[dtkey, 'masks'], writes=[dtkey])
    P.add('pool', lambda e: e.tensor_tensor(DB[:], DB[:], bc(MST), ALU.mult), reads=[dbkey, 'masks'], writes=[dbkey])
    P.add('dve', lambda e: e.tensor_tensor(DB[:], DB[:], gb(1), ALU.mult), reads=[dbkey, gkey], writes=[dbkey])
    pG, pQ = k.bank(), k.bank()
    hmm(k, pG, 64, lambda sl, hh: k.FBb[sl, 2 + hh, cols], lambda sl, hh: k.FBb[sl, 2 + hh, cols], ['FB2', 'FB3'])
    hmm(k, pQ, 64, lambda sl, hh: k.FBb[sl, 2 + hh, cols], lambda sl, hh: k.FBb[sl, 0 + hh, cols], ['FB0', 'FB1', 'FB2', 'FB3'])
    wb = lambda nm, n=None: k.work(nm, B3, BF16, n=n)
    Z, zkey = wb('bZ')
    QKM, qkkey = wb('hb0', 2)
    P.add('dve', lambda e: e.tensor_tensor(Z[:], pv3(k, pG, 64), DB[:], ALU.mult), reads=[f'ps{pG}', dbkey], writes=[zkey])
    P.add('dve', lambda e: e.tensor_tensor(QKM[:], pv3(k, pQ, 64), DTm[:], ALU.mult), reads=[f'ps{pQ}', dtkey], writes=[qkkey])
    idbb = lambda sl, hh: k.identb[sl, sl]
    pX = k.bank()
    hmm(k, pX, 64, lambda sl, hh: Z[sl, hh, :], idbb, [zkey, 'identb'])
    X0, xkey = wb('bX0')
    P.add('act', lambda e: e.activation(X0[:], pv3(k, pX, 64), AF.Identity), reads=[f'ps{pX}'], writes=[xkey])
    BD16 = k.masks[:, 5, :]
    M16 = k.masks[:, 6 + 2 * d, :]
    M32 = k.masks[:, 7 + 2 * d, :]
    M16T = k.masks[:, 6 + 2 * (1 - d), :]
    cp_i = [0]

    def evac(dst, pb, dkey):
        if cp_i[0] % 2 == 0:
            P.add('act', lambda e: e.activation(dst[:], pv3(k, pb, 64), AF.Identity), reads=[f'ps{pb}'], writes=[dkey])
        else:
            P.add('dve', lambda e: e.tensor_copy(dst[:], pv3(k, pb, 64)), reads=[f'ps{pb}'], writes=[dkey])
        cp_i[0] += 1

    def mm4(lhs, lkey, rhs, rkey_):
        pb = k.bank()
        hmm(k, pb, 64, lambda sl, hh: lhs[sl, hh, :], lambda sl, hh: rhs[sl, hh, :], [lkey, rkey_])
        return pb
    ND, ndk = wb('bND')
    XD, xdk = wb('bXD')
    P.add('pool', lambda e: e.tensor_tensor(ND[:], Z[:], bc(BD16), ALU.mult), reads=[zkey, 'masks'], writes=[ndk])
    P.add('pool', lambda e: e.tensor_tensor(XD[:], X0[:], bc(BD16), ALU.mult), reads=[xkey, 'masks'], writes=[xdk])
    RX, rxkey = k.work('bRX', [128, 2, 2, 64], BF16, n=2)
    P.add('dve', lambda e, RX=RX: e.tensor_tensor(RX[:, :, 0, :], bc(I2), XD[:], ALU.subtract), reads=['masks', xdk], writes=[rxkey])
    p1 = mm4(ND, ndk, XD, xdk)
    p2 = mm4(XD, xdk, ND, ndk)
    P.add('act', lambda e, RX=RX: e.activation(RX[:, :, 1, :], pv3(k, p1, 64), AF.Identity), reads=[f'ps{p1}'], writes=[rxkey])
    Zk, zkkey = wb('bZk')
    evac(Zk, p2, zkkey)
    for lev in range(1, 4):
        pa = k.bank()
        if lev < 3:
            hmm(k, pa, 128, lambda sl, hh, Zk=Zk: Zk[sl, hh, :], lambda sl, hh, RX=RX: RX[sl, hh, :, :].rearrange("p a b -> p (a b)"), [zkkey, rxkey])
            pz = k.bank()
            hmm(k, pz, 64, lambda sl, hh, RX=RX: RX[sl, hh, 1, :], lambda sl, hh, Zk=Zk: Zk[sl, hh, :], [zkkey, rxkey])
        else:
            hmm(k, pa, 64, lambda sl, hh, Zk=Zk: Zk[sl, hh, :], lambda sl, hh, RX=RX: RX[sl, hh, 0, :], [zkkey, rxkey])
        RXn, rxnkey = k.work('bRX', [128, 2, 2, 64], BF16, n=2)
        pav = k.psb[pa][:, 0:256].rearrange("p (h a b) -> p h a b", h=2, a=2)
        pa0 = pav[:, :, 0, :] if lev < 3 else pv3(k, pa, 64)
        P.add('dve', lambda e, RX=RX, RXn=RXn, pa0=pa0: e.tensor_tensor(RXn[:, :, 0, :], RX[:, :, 0, :], pa0, ALU.add),
              reads=[rxkey, f'ps{pa}'], writes=[rxnkey])
        if lev < 3:
            P.add('act', lambda e, RXn=RXn, pav=pav: e.activation(RXn[:, :, 1, :], pav[:, :, 1, :], AF.Identity), reads=[f'ps{pa}'], writes=[rxnkey])
            Zn, znkey = wb('bZk')
            evac(Zn, pz, znkey)
            Zk, zkkey = Zn, znkey
        RX, rxkey = RXn, rxnkey
    DT, dtk = wb('bDT')
    P.add('dve', lambda e, RX=RX: e.tensor_copy(DT[:], RX[:, :, 0, :]), reads=[rxkey], writes=[dtk])
    pb = k.bank()
    hmm(k, pb, 64, lambda sl, hh: DT[sl, hh, :], idbb, [dtk, 'identb'])
    Dm, dmk = wb('bDm')
    evac(Dm, pb, dmk)
    Cm, cmk = wb('bND')
    CT, ctk = wb('bXD')
    P.add('pool', lambda e: e.tensor_tensor(Cm[:], Z[:], bc(M16), ALU.mult), reads=[zkey, 'masks'], writes=[cmk])
    P.add('pool', lambda e: e.tensor_tensor(CT[:], X0[:], bc(M16T), ALU.mult), reads=[xkey, 'masks'], writes=[ctk])
    pb = mm4(CT, ctk, Dm, dmk)
    T1, t1k = wb('bT1')
    evac(T1, pb, t1k)
    pb2 = mm4(Cm, cmk, DT, dtk)
    T1p, t1pk = wb('bT1p')
    evac(T1p, pb2, t1pk)
    pb = mm4(DT, dtk, T1, t1k)
    pb2 = mm4(Dm, dmk, T1p, t1pk)
    D32, d32k = wb('bD32')
    DT32, dt32k = wb('bDT32')
    P.add('dve', lambda e, pb=pb: e.tensor_tensor(D32[:], Dm[:], pv3(k, pb, 64), ALU.subtract), reads=[dmk, f'ps{pb}'], writes=[d32k])
    P.add('dve', lambda e, pb2=pb2: e.tensor_tensor(DT32[:], DT[:], pv3(k, pb2, 64), ALU.subtract), reads=[dtk, f'ps{pb2}'], writes=[dt32k])
    Cm2, cm2k = wb('bND')
    P.add('pool', lambda e: e.tensor_tensor(Cm2[:], Z[:], bc(M32), ALU.mult), reads=[zkey, 'masks'], writes=[cm2k])
    pb = mm4(Cm2, cm2k, DT32, dt32k)
    T1q, t1qk = wb('bT1p')
    evac(T1q, pb, t1qk)
    pb = mm4(D32, d32k, T1q, t1qk)
    TIt, tik = wb('hb1', 2)
    P.add('dve', lambda e, pb=pb: e.tensor_tensor(TIt[:], DT32[:], pv3(k, pb, 64), ALU.subtract), reads=[dt32k, f'ps{pb}'], writes=[tik])
    VB, vbkey = wb('hb2', 2)
    KBG, kbkey = wb('bKBG')
    KD, kdkey = wb('hb3', 2)
    P.add('pool', lambda e: e.tensor_tensor(VB[:], vTl[:, :, 0:64], gb(1), ALU.mult), reads=[vkey, gkey], writes=[vbkey])
    P.add('pool', lambda e: e.tensor_tensor(KBG[:], kTl[:, :, 0:64], gb(4), ALU.mult), reads=[kkey, gkey], writes=[kbkey])
    P.add('pool', lambda e: e.tensor_tensor(KD[:], kTl[:, :, 0:64], gb(3), ALU.mult), reads=[kkey, gkey], writes=[kdkey])
    pW = k.bank()
    hmm(k, pW, 64, lambda sl, hh: KBG[sl, hh, :], lambda sl, hh: TIt[sl, hh, :], [kbkey, tik])
    WT_, wtkey = k.work('h4', [128, 2, 65], n=2)
    WT = WT_[:, :, 0:64]
    P.add('act', lambda e: e.activation(WT[:], pv3(k, pW, 64), AF.Identity), reads=[f'ps{pW}'], writes=[wtkey])
    return dict(GS=GS, gkey=gkey, TIt=TIt, tik=tik, VB=VB, vbkey=vbkey, WT=WT, wtkey=wtkey, QKM=QKM, qkkey=qkkey, KD=KD, kdkey=kdkey)


def delta_p2(k, seq, d, c, H):
    P = k.P
    cols = slice(c * 64, (c + 1) * 64)
    B3 = [128, 2, 64]
    GS, gkey, TIt, tik, VB, vbkey, WT, wtkey, QKM, qkkey, KD, kdkey = (H[x] for x in ('GS', 'gkey', 'TIt', 'tik', 'VB', 'vbkey', 'WT', 'wtkey', 'QKM', 'qkkey', 'KD', 'kdkey'))
    gb = lambda q: GS[:, q, :].unsqueeze(2).broadcast_to(B3)
    Sst = k.Sst[d]
    skey = f'Sst{d}'
    pV = k.bank()
    for hh in range(2):
        for a in range(2):
            sl = slice(64 * a, 64 * a + 64)
            P.add('pe', lambda e, hh=hh, sl=sl: MM(e, k.psb[pV][sl, hh * 64:(hh + 1) * 64], TIt[sl, hh, :], VB[sl, hh, :], start=True, stop=False),
                  reads=[tik, vbkey], writes=[f'ps{pV}'])
            P.add('pe', lambda e, hh=hh, sl=sl: MM(e, k.psb[pV][sl, hh * 64:(hh + 1) * 64], WT[sl, hh, :], Sst[sl, hh, :], start=False, stop=True),
                  reads=[wtkey, skey], writes=[f'ps{pV}'])
    VN, vnkey = k.work('pVNb', B3, BF16)
    P.add('act', lambda e: e.activation(VN[:], pv3(k, pV, 64), AF.Identity), reads=[f'ps{pV}'], writes=[vnkey])
    pO1, pO2 = k.bank(), k.bank()
    hmm(k, pO1, 64, lambda sl, hh: k.FBb[sl, 0 + hh, cols], lambda sl, hh: k.Sstb[d][sl, hh, :], ['FB0', 'FB1', f'Sstb{d}'])
    hmm(k, pO2, 64, lambda sl, hh: QKM[sl, hh, :], lambda sl, hh: VN[sl, hh, :], [qkkey, vnkey])
    OO, ookey = k.work('Hout', B3)
    P.add('dve', lambda e: e.tensor_tensor(OO[:], pv3(k, pO1, 64), gb(2), ALU.mult), reads=[f'ps{pO1}', gkey], writes=[ookey])
    P.add('dve', lambda e: e.tensor_tensor(OO[:], OO[:], pv3(k, pO2, 64), ALU.add), reads=[f'ps{pO2}', ookey], writes=[ookey])
    out_to_HF(k, OO, ookey, cols)
    pS = k.bank()
    hmm(k, pS, 64, lambda sl, hh: KD[sl, hh, :], lambda sl, hh: VN[sl, hh, :], [kdkey, vnkey])
    P.add('dve', lambda e: e.tensor_tensor(Sst[:], Sst[:], gb(5), ALU.mult), reads=[skey, gkey], writes=[skey])
    P.add('dve', lambda e: e.tensor_tensor(Sst[:], Sst[:], pv3(k, pS, 64), ALU.add), reads=[skey, f'ps{pS}'], writes=[skey])
    P.add('pool', lambda e: e.tensor_copy(k.Sstb[d], Sst[:]), reads=[skey], writes=[f'Sstb{d}'])


def qk_norm_tile(k, l, seq, src, skey, normcol, rope, dst_bf, dkey, out_f32=None):
    P = k.P
    tok0, T, samp = seq
    for ct in range((T + 511) // 512):
        w = min(512, T - ct * 512)
        cs = slice(ct * 512, ct * 512 + w)
        s2 = k.rot('sg', 2)
        P.add('act', lambda e, s2=s2, cs=cs, w=w: e.activation(k.sg[s2][:, 0:w], src[:, cs], AF.Square), reads=[skey], writes=[f'sg{s2}'])
        pb = k.bank()
        P.add('pe', lambda e, s2=s2, pb=pb, w=w: MM(e, k.psb[pb][:, 0:w], k.bones[:], k.sg[s2][:, 0:w], start=True, stop=True),
              reads=[f'sg{s2}', 'bones'], writes=[f'ps{pb}'])
        P.add('act', lambda e, pb=pb, w=w: e.activation(k.rstd[:, 0:w], k.psb[pb][:, 0:w], AF.Sqrt, bias=EPS, scale=1.0 / 64), reads=[f'ps{pb}'], writes=['rstd'])
        P.add('dve', lambda e, w=w: e.reciprocal(k.rstd[:, 0:w], k.rstd[:, 0:w]), reads=['rstd'], writes=['rstd'])
        s3 = k.rot('tmp', 2)
        xn = k.tmp[s3]
        P.add('dve', lambda e, xn=xn, cs=cs, w=w: e.scalar_tensor_tensor(xn[:, 0:w], src[:, cs], k.fparS[:, l, normcol:normcol + 1], k.rstd[:, 0:w], ALU.mult, ALU.mult),
              reads=[skey, 'fparS', 'rstd'], writes=[f'tmp{s3}'])
        if rope:
            pb2 = k.bank()
            P.add('pe', lambda e, pb2=pb2, xn=xn, w=w: MM(e, k.psb[pb2][:, 0:w], k.permS[:], xn[:, 0:w], start=True, stop=True),
                  reads=[f'tmp{s3}', 'permS'], writes=[f'ps{pb2}'])
            s4 = k.rot('sg', 2)
            P.add('dve', lambda e, s4=s4, pb2=pb2, cs=cs, w=w: e.tensor_tensor(k.sg[s4][:, 0:w], k.psb[pb2][:, 0:w], k.sinS[:, cs], ALU.mult),
                  reads=[f'ps{pb2}', 'sinS'], writes=[f'sg{s4}'])
            P.add('dve', lambda e, xn=xn, cs=cs, w=w: e.tensor_tensor(xn[:, 0:w], xn[:, 0:w], k.cosS[:, cs], ALU.mult),
                  reads=[f'tmp{s3}', 'cosS'], writes=[f'tmp{s3}'])
            P.add('dve', lambda e, s4=s4, xn=xn, cs=cs, w=w: e.tensor_tensor(dst_bf[:, cs], xn[:, 0:w], k.sg[s4][:, 0:w], ALU.add),
                  reads=[f'tmp{s3}', f'sg{s4}'], writes=[dkey])
        else:
            P.add('act', lambda e, xn=xn, cs=cs, w=w: e.activation(dst_bf[:, cs], xn[:, 0:w], AF.Identity), reads=[f'tmp{s3}'], writes=[dkey])
            if out_f32 is not None:
                P.add('sp', lambda e, xn=xn, cs=cs, w=w: e.dma_start(out=out_f32[:, cs], in_=xn[:, 0:w]), reads=[f'tmp{s3}'], chan=f'c_tmpo{s3}')


def attention(k, l, seq, si):
    P = k.P
    tok0, T, samp = seq
    koff = PAST if samp else 0
    nkt = (koff + T) // 128
    AK = [f'FB{i}' for i in range(6)] + ['kTa', 'Va', 'PT0', 'PT1', 'cosS', 'sinS', 'Cstb0', 'Cstb1', 'Sstb0', 'Sstb1', 'FBt']
    P.add('dve', lambda e: e.memset(k.ST[:, 6:7], 0.0), reads=[], writes=AK)
    if samp:
        P.add('sp', lambda e: e.dma_start(out=k.cosS, in_=k.cosD), writes=['cosS'], chan='c_cos')
        P.add('sp', lambda e: e.dma_start(out=k.sinS, in_=k.sinD), writes=['sinS'], chan='c_sin')
    s = k.rot('stg', 2)
    P.add('sp', lambda e, s=s: e.dma_start(out=k.stg[s][:, 0:T], in_=k.zT[ZAK:ZAK + 128, tok0:tok0 + T]), reads=zkeys(20, seq), writes=[f'HF{s}'], chan=f'c_stg{s}')
    qk_norm_tile(k, l, seq, k.stg[s], f'HF{s}', 4, samp, k.kTa[:, koff:koff + T], 'kTa',
                 out_f32=None if samp else k.nkT[l][:, tok0:tok0 + T])
    if samp:
        P.add('pool', lambda e: e.dma_start(out=k.kTa[:, 0:PAST], in_=k.ckT[l]), writes=['kTa'], chan='c_ck')
        for kt in range(4):
            P.add('pool', lambda e, kt=kt: e.dma_start(out=k.Va[:, kt, :, 0:64], in_=k.cv[l][kt * 128:(kt + 1) * 128, :].rearrange("p (g e) -> p g e", g=2)),
                  writes=['Va'], chan='c_cv')
    else:
        P.add('sp', lambda e: e.dma_start(out=k.nvT[l][:, tok0:tok0 + T], in_=k.zT[ZAV:ZAV + 128, tok0:tok0 + T]), reads=zkeys(21, seq), chan='c_nv')
    s = k.rot('stg', 2)
    P.add('sp', lambda e, s=s: e.dma_start(out=k.stg[s][:, 0:T], in_=k.zT[ZAV:ZAV + 128, tok0:tok0 + T]), reads=zkeys(21, seq), writes=[f'HF{s}'], chan=f'c_stg{s}')
    P.add('dve', lambda e: e.memset(k.Va[:, :, :, 64:65], 1.0), writes=['Va'])
    for b in range(T // 128):
        pb = k.bank()
        P.add('pe', lambda e, s=s, b=b, pb=pb: e.transpose(k.psb[pb][:, 0:128], k.stg[s][:, b * 128:(b + 1) * 128], k.ident[:]),
              reads=[f'HF{s}', 'ident'], writes=[f'ps{pb}'])
        P.add('act', lambda e, b=b, pb=pb: e.activation(k.Va[:, koff // 128 + b, :, 0:64], k.psb[pb][:, 0:128].rearrange("p (g e) -> p g e", g=2), AF.Identity),
              reads=[f'ps{pb}'], writes=['Va'])
    for i in range(4):
        s = k.rot('stg', 2)
        r = ZAQ + 128 * i
        P.add('sp', lambda e, s=s, r=r: e.dma_start(out=k.stg[s][:, 0:T], in_=k.zT[r:r + 128, tok0:tok0 + T]), reads=zkeys(r // 128, seq), writes=[f'HF{s}'], chan=f'c_stg{s}')
        qk_norm_tile(k, l, seq, k.stg[s], f'HF{s}', 3, samp, k.QT[:, i, 0:T], f'QT{i}')
    for g in range(2):
        gp = 64 * g
        for qi in range(T // 128):
            qsl = slice(qi * 128, (qi + 1) * 128)
            pOs = []
            for _ in range(4):
                b_ = k.bank()
                k.reserved.add(b_)
                pOs.append(b_)
            def scores(kt):
                pS_ = k.bank()
                P.add('pe', lambda e, kt=kt, pS=pS_, gp=gp, qsl=qsl: MM(e, k.psb[pS][:, 0:512], k.kTa[gp:gp + 64, kt * 128:(kt + 1) * 128], k.QT[gp:gp + 64, :, qsl], start=True, stop=True),
                      reads=['kTa'] + [f'QT{i}' for i in range(4)], writes=[f'ps{pS_}'])
                return pS_
            pS_next = scores(0)
            for kt in range(nkt):
                pS = pS_next
                if kt + 1 < nkt:
                    pS_next = scores(kt + 1)
                sp_ = k.rot('PT', 2)
                P.add('act', lambda e, sp_=sp_, pS=pS: e.activation(k.PT[sp_][:], k.psb[pS][:, 0:512], AF.Exp, bias=0.0, scale=0.125), reads=[f'ps{pS}'], writes=[f'PT{sp_}'])
                for hig in range(4):
                    P.add('pe', lambda e, kt=kt, hig=hig, sp_=sp_, pO=pOs[hig], g=g: MM(e, k.psb[pO][:, 0:65], k.PT[sp_][:, hig * 128:(hig + 1) * 128], k.Va[:, kt, g, :],
                                                                               start=(kt == 0), stop=(kt == nkt - 1)),
                          reads=[f'PT{sp_}', 'Va'], writes=[f'ps{pOs[hig]}'])
            for b_ in pOs:
                k.reserved.discard(b_)
            REC, rkey = k.work('REC', [128, 4])
            AO, aokey = k.work('AO', [128, 4, 64])
            for hig in range(4):
                pO = pOs[hig]
                P.add('dve', lambda e, pO=pO, REC=REC, hig=hig: e.reciprocal(REC[:, hig:hig + 1], k.psb[pO][:, 64:65]), reads=[f'ps{pO}'], writes=[rkey])
                P.add('act', lambda e, pO=pO, REC=REC, AO=AO, hig=hig: e.activation(AO[:, hig, :], k.psb[pO][:, 0:64], AF.Identity, bias=0.0, scale=REC[:, hig:hig + 1]),
                      reads=[f'ps{pO}', rkey], writes=[aokey])
            pT = k.bank()
            for a in range(2):
                P.add('pe', lambda e, a=a, pT=pT, AO=AO: e.transpose(k.psb[pT][:, a * 128:(a + 1) * 128], AO[:, 2 * a:2 * a + 2, :].rearrange("p a b -> p (a b)"), k.ident[:]),
                      reads=[aokey, 'ident'], writes=[f'ps{pT}'])
            P.add('act', lambda e, pT=pT, qi=qi, g=g: e.activation(k.h[:, 4 + 2 * g:6 + 2 * g, tok0 + qi * 128:tok0 + (qi + 1) * 128],
                                                             k.psb[pT][:, 0:256].rearrange("p (a q) -> p a q", a=2), AF.Identity),
                  reads=[f'ps{pT}'], writes=[f'cat{4 + 2 * g}', f'cat{5 + 2 * g}'])
    if k.cfg.get('dbg') and samp:
        P.add('sp', lambda e: e.dma_start(out=k.kTaD, in_=k.kTa), reads=['kTa'], chan='c_dbg1')
        for i in range(4):
            P.add('sp', lambda e, i=i: e.dma_start(out=k.QTD[:, i, :], in_=k.QT[:, i, :]), reads=[f'QT{i}'], chan='c_dbg2')
        P.add('sp', lambda e: e.dma_start(out=k.VaD, in_=k.FB[:, 3, :].bitcast(BF16)[:, 0:2600]), reads=['Va'], chan='c_dbg3')
    P.add('dve', lambda e: e.memset(k.ST[:, 6:7], 0.0), reads=[], writes=AK)


def scans(k, l, seq, si, which):
    P = k.P
    tok0, T, samp = seq
    nch = T // 64
    st = k.Cst if which == 'm' else k.Sst
    stn = 'Cst' if which == 'm' else 'Sst'
    W = 65 if which == 'm' else 64
    for d in range(2):
        if samp:
            src = (k.C0 if which == 'm' else k.S0)[l, d]
            P.add('sp', lambda e, d=d, src=src: e.dma_start(out=st[d][:], in_=src), writes=[f'{stn}{d}'], chan=f'c_{stn}{d}')
        else:
            P.add('dve', lambda e, d=d: e.memset(st[d][:], 0.0), writes=[f'{stn}{d}'])
    P.add('dve', lambda e: e.memset(k.HF[:, :, 0:T], 0.0), writes=['HF0', 'HF1'])
    for d in range(2):
        stb = k.Cstb if which == 'm' else k.Sstb
        P.add('act', lambda e, d=d, stb=stb: e.activation(stb[d], st[d][:], AF.Identity), reads=[f'{stn}{d}'], writes=[f'{stn}b{d}'])
    if not hasattr(k, 'gsall'):
        k.gsall = {}
    for d in range(2):
        R = k.Rm[d] if which == 'm' else k.Rd[d]
        k.gsall[(which, d)] = gs_all(k, R, ('Rm' if which == 'm' else 'Rd') + str(d), nch, 'g' + str(d))
    p1f, p2f = (mlstm_p1, mlstm_p2) if which == 'm' else (delta_p1, delta_p2)
    rec1 = {}
    rec2 = {}
    for d in range(2):
        for s in range(nch):
            c = s if d == 0 else nch - 1 - s
            P.thread_begin()
            k.bank_pool = (4 * d, 2)
            k.tid = f'_t{d}'
            H = p1f(k, seq, d, c)
            rec1[(d, s)] = P.thread_end()
            P.thread_begin()
            k.bank_pool = (4 * d + 2, 2)
            p2f(k, seq, d, c, H)
            rec2[(d, s)] = P.thread_end()
    k.bank_pool = None
    k.tid = ''
    for s in range(nch + 1):
        th = []
        for d in range(2):
            if s < nch:
                th.append(rec1[(d, s)])
            if s >= 1:
                th.append(rec2[(d, s - 1)])
        P.interleave(th)
    if not samp:
        for d in range(2):
            dst = (k.Cout if which == 'm' else k.Sout)[si, l, d]
            P.add('sp', lambda e, d=d, dst=dst: e.dma_start(out=dst, in_=st[d][:]), reads=[f'{stn}{d}'], chan=f'c_{stn}o{d}')


def mixer_phase(k, l):
    P = k.P
    cfg = k.cfg
    norm_phase(k, l, 1)
    wv = k.w_in[l].rearrange("(kk p) c -> p kk c", p=128)
    for blk in range(23):
        ncol = 128 if blk < 22 else 32
        s = k.rot('w8', 2)
        P.add('pool', lambda e, s=s, blk=blk, ncol=ncol: e.dma_start(out=k.wg[s][:, :, 0:ncol], in_=wv[:, :, blk * 128:blk * 128 + ncol]),
              writes=[f'wg{s}'], chan=f'c_wg{s}')
        for tt in range(NT):
            pb = k.bank()
            for kk in range(8):
                P.add('pe', lambda e, s=s, kk=kk, pb=pb, tt=tt, ncol=ncol: MM(e, k.psb[pb][0:ncol, :], k.wg[s][:, kk, 0:ncol], k.h[:, kk, tsl(tt)],
                                                                                   start=(kk == 0), stop=(kk == 7)),
                      reads=[f'wg{s}', f'h{tt}_{kk}'], writes=[f'ps{pb}'])
            s2 = k.rot('sg', 2)
            if (blk * NT + tt) % 2 == 0:
                P.add('act', lambda e, s2=s2, pb=pb, ncol=ncol: e.activation(k.sg[s2][0:ncol, :], k.psb[pb][0:ncol, :], AF.Identity), reads=[f'ps{pb}'], writes=[f'sg{s2}'])
            else:
                P.add('dve', lambda e, s2=s2, pb=pb, ncol=ncol: e.tensor_copy(k.sg[s2][0:ncol, :], k.psb[pb][0:ncol, :]), reads=[f'ps{pb}'], writes=[f'sg{s2}'])
            P.add('sp', lambda e, s2=s2, blk=blk, tt=tt, ncol=ncol: e.dma_start(out=k.zT[blk * 128:blk * 128 + ncol, tsl(tt)], in_=k.sg[s2][0:ncol, :]),
                  reads=[f'sg{s2}'], writes=[f'zT{blk}_{tt}'], chan=f'c_sgo{s2}')
    wov = k.w_out[l].rearrange("(kk p) c -> p kk c", p=128)
    W8K = ['wg0', 'wu0', 'wg1', 'wu1']
    for q4 in range(4):
        P.add('pool', lambda e, q4=q4: e.dma_start(out=k.wout[:, 2 * q4:2 * q4 + 2, :], in_=wov[:, 2 * q4:2 * q4 + 2, :]), writes=W8K, chan='c_wout')
    xkeys = [f'x{tt}_{kk}' for tt in range(NT) for kk in range(8)]
    mixkeys = [f'FB{i}' for i in range(6)] + ['HF0', 'HF1'] + [f'QT{i}' for i in range(4)] + ['kTa', 'Va', 'PT0', 'PT1', 'cosS', 'sinS', 'Cstb0', 'Cstb1', 'Sstb0', 'Sstb1', 'FBt']
    hkeys = [f'h{tt}_{kk}' for tt in range(NT) for kk in range(8)]
    catkeys = [f'cat{i}' for i in range(8)]
    for a in range(8):
        P.add('sp', lambda e, a=a: e.dma_start(out=k.xpark[:, a * NTOK:(a + 1) * NTOK], in_=k.x[:, a, :]), reads=xkeys, writes=mixkeys + ['xpark'], chan='c_park')
    P.add('dve', lambda e: e.memset(k.ST[:, 7:8], 0.0), reads=[], writes=hkeys + catkeys)
    parts = cfg.get('parts', 'gmda')
    P.barrier(lambda e: e.memset(k.ST[:, 7:8], 0.0))
    for si, seq in enumerate(SEQS):
        tok0, T, samp = seq
        if si not in cfg.get('seqs', (0, 1, 2)):
            continue
        if 'g' in parts:
          for d in range(2):
            gate_prepass(k, l, seq, d, k.m0[l, d:d + 1, :] if samp else None)
        if 'm' in parts:
          load_F(k, ZQ, 6, seq, [(k.FBb[:, i, 0:T], f'FB{i}') for i in range(6)], eng='pool')
          for i in (2, 3):
            P.add('act', lambda e, i=i, T=T: e.activation(k.FBb[:, i, 0:T], k.FBb[:, i, 0:T], AF.Identity, bias=0.0, scale=0.125), reads=[f'FB{i}'], writes=[f'FB{i}'])
          if cfg.get('msub', 15) & 2:
            scans(k, l, seq, si, 'm')
          if not samp and cfg.get('msub', 15) & 4:
            for d in range(2):
                fin = T // 64 if d == 0 else 0
                P.add('dve', lambda e, d=d, fin=fin: e.tensor_copy(k.m0t[d][:], k.MM[d][:, :, fin]), reads=[f'MM{d}'], writes=[f'm0t{d}'])
                P.add('sp', lambda e, d=d, si=si: e.dma_start(out=k.mout[si, l, d:d + 1, :], in_=k.m0t[d][:]), reads=[f'm0t{d}'], chan=f'c_mo{d}')
          if cfg.get('msub', 15) & 8:
            post_norm(k, l, seq, [(k.HF[:, i, :], f'HF{i}') for i in range(2)], 0, ZO, AF.Sigmoid, 0)
        if 'd' in parts:
          delta_prepass(k, l, seq)
          if cfg.get('dbg'):
            for i in range(6):
                P.add('sp', lambda e, i=i: e.dma_start(out=k.FBD[:, i, :], in_=k.FB[:, i, :]), reads=[f'FB{i}'], chan='c_dbg4')
            for d in range(2):
                P.add('sp', lambda e, d=d: e.dma_start(out=k.RdD[d], in_=k.Rd[d][:]), reads=[f'Rd{d}'], chan='c_dbg5')
                P.add('sp', lambda e, d=d: e.dma_start(out=k.RmD[d], in_=k.Rm[d][:]), reads=[f'Rm{d}'], chan='c_dbg5')
          scans(k, l, seq, si, 'd')
          post_norm(k, l, seq, [(k.HF[:, i, :], f'HF{i}') for i in range(2)], 2, ZGZ, AF.Silu, 2)
        if 'a' in parts:
          attention(k, l, seq, si)
    if cfg.get('dbg'):
        for kk in range(8):
            P.add('sp', lambda e, kk=kk: e.dma_start(out=k.catD[:, kk, :], in_=k.h[:, kk, :]), reads=catkeys, chan='c_catD')
    P.barrier(lambda e: e.memset(k.ST[:, 7:8], 0.0))
    for a in range(8):
        P.add('sp', lambda e, a=a: e.dma_start(out=k.x[:, a, :], in_=k.xpark[:, a * NTOK:(a + 1) * NTOK]), reads=['xpark'], writes=xkeys + mixkeys, chan='c_unpark')
    for tt in range(NT):
        j = 0 if tt == 0 else 1
        for o in range(8):
            bo = k.bank()
            for kk in range(8):
                P.add('pe', lambda e, o=o, kk=kk, bo=bo, tt=tt: MM(e, k.psb[bo][:], k.wout[:, kk, o * 128:(o + 1) * 128], k.h[:, kk, tsl(tt)],
                                                                        start=(kk == 0), stop=(kk == 7)),
                      reads=W8K + [f'cat{kk}'], writes=[f'ps{bo}'])
            P.add('dve', lambda e, o=o, bo=bo, tt=tt, j=j: e.scalar_tensor_tensor(
                k.x[:, o, tsl(tt)], k.psb[bo][:], k.dv[:, j, 4, o:o + 1], k.x[:, o, tsl(tt)], ALU.mult, ALU.add),
                reads=[f'ps{bo}', 'dv', f'x{tt}_{o}'], writes=[f'x{tt}_{o}'])
    P.add('dve', lambda e: e.memset(k.ST[:, 7:8], 0.0), reads=[], writes=hkeys + catkeys)

def _consts():
    c = {}
    c['ident'] = np.eye(128, dtype=np.float32)
    i = np.arange(64)
    m = np.zeros((64, 10, 64), np.float32)
    tt_, jj_ = i[:, None], i[None, :]
    m[:, 5, :] = (tt_ // 16 == jj_ // 16)
    m16 = ((tt_ // 16) % 2 == 1) & (jj_ // 16 == tt_ // 16 - 1)
    m32 = (tt_ // 32 == 1) & (jj_ // 32 == 0)
    m[:, 6, :] = m16
    m[:, 7, :] = m32
    m[:, 8, :] = m16.T
    m[:, 9, :] = m32.T
    m[:, 0, :] = (i[:, None] <= i[None, :])
    m[:, 1, :] = (i[None, :] < i[:, None])
    m[:, 2, :] = (i[:, None] >= i[None, :])
    m[:, 3, :] = (i[None, :] > i[:, None])
    m[:, 4, :] = (i[:, None] == i[None, :])
    c['masks'] = np.concatenate([m, m], 0)
    sel = np.zeros((128, 32, 4), np.float32)
    for h in range(4):
        for cc in range(32):
            sel[32 * h + cc, cc, h] = 1.0
    c['sel'] = sel
    b = np.zeros((128, 128), np.float32)
    b[:64, :64] = 1.0
    b[64:, 64:] = 1.0
    c['bones'] = b
    p = np.arange(128)
    dd = p % 64
    half = dd // 32
    r = dd % 32
    f = r % 16
    second = r // 16
    inv = (10000.0 ** (-(np.arange(16, dtype=np.float32)) / 16.0)).astype(np.float32)
    t = np.arange(2048)
    row = (t // 64).astype(np.float32)
    col = (t % 64).astype(np.float32)
    pos = np.where(half[:, None] == 0, row[None, :], col[None, :]).astype(np.float32)
    ang = (pos * inv[f][:, None]).astype(np.float32)
    c['cosT'] = np.cos(ang).astype(np.float32)
    c['sinT'] = (np.sin(ang) * np.where(second[:, None] == 0, -1.0, 1.0)).astype(np.float32)
    perm = np.zeros((128, 128), np.float32)
    for mm in range(128):
        src = mm + 16 if (mm % 32) < 16 else mm - 16
        perm[src, mm] = 1.0
    c['perm'] = perm
    return c


def _win_perm():
    mq, mk, mv, mo, mi, mf, gq, gk, gv, gz, ga, gb, aq, ak, av = (0, 256, 512, 768, 1024, 1032, 1040, 1296, 1552, 1808, 2064, 2072, 2080, 2592, 2720)
    idx = []
    for o in (mq, mk, mv, mo, gq, gk, gv, gz):
        idx += list(range(o, o + 256))
    for hig in range(4):
        for g in range(2):
            h = g * 4 + hig
            idx += list(range(aq + h * 64, aq + h * 64 + 64))
    idx += list(range(ak, ak + 128)) + list(range(av, av + 128))
    for o in (mi, mf, ga, gb):
        idx += list(range(o, o + 8))
    return np.array(idx)


def _shared(inp):
    f = np.float32
    sh = {}
    sh['ada_w'] = np.ascontiguousarray(inp['ada_w'], f)
    sh['ada_bT'] = np.ascontiguousarray(inp['ada_b'].reshape(2, 72, 128).transpose(0, 2, 1), f)
    sh['norm_gT'] = np.ascontiguousarray(inp['norm_g'].reshape(2, 3, 8, 128).transpose(3, 0, 1, 2), f)
    sh['final_gT'] = np.ascontiguousarray(inp['final_norm'].reshape(8, 128).T, f)
    sh['ffn_w_in'] = np.ascontiguousarray(inp['ffn_w_in'], f)
    sh['ffn_w_out'] = np.ascontiguousarray(inp['ffn_w_out'], f)
    sh['w_in_p'] = np.ascontiguousarray(inp['w_in'][:, :, _win_perm()], f)
    sh['w_out'] = np.ascontiguousarray(inp['w_out'], f)
    p = np.arange(128)
    gpar = np.zeros((128, 2, 2, 3), f)
    hh = p // 32
    gpar[:, :, :, 0] = -inp['mlstm_f_bias'].transpose(2, 0, 1)[hh]
    gpar[:, :, :, 1] = inp['delta_a_log'].transpose(2, 0, 1)[hh]
    gpar[:, :, :, 2] = inp['delta_dt_bias'].transpose(2, 0, 1)[hh]
    sh['gpar'] = gpar
    fpar = np.zeros((128, 2, 5), f)
    fpar[:, :, 0] = inp['mlstm_norm'][:, 0:128].T
    fpar[:, :, 1] = inp['mlstm_norm'][:, 128:256].T
    fpar[:, :, 2] = inp['delta_norm'][:, p % 64].T
    fpar[:, :, 3] = inp['attn_q_norm'][:, p % 64].T
    fpar[:, :, 4] = inp['attn_k_norm'][:, p % 64].T
    sh['fpar'] = fpar
    sh['convw'] = np.ascontiguousarray(inp['delta_conv'].reshape(2, 5, 6, 128).transpose(3, 0, 2, 1), f)
    sh.update(_consts())
    return sh


def _state_layout(C, n=None):
    l, dr, H, dk, e = C.shape
    W = e + (1 if n is not None else 0)
    out = np.zeros((l, dr, 128, 2, W), np.float32)
    for h in range(4):
        out[:, :, 64 * (h % 2):64 * (h % 2) + 64, h // 2, :e] = C[:, :, h]
        if n is not None:
            out[:, :, 64 * (h % 2):64 * (h % 2) + 64, h // 2, e] = n[:, :, h]
    return out


def prep_core(inp, core, sh, mix=True):
    f = np.float32
    b = core // 2
    xp = inp['x_prompt'][2 * core:2 * core + 2].reshape(512, 1024)
    xs = inp['x_sample'][b]
    d = dict(sh) if mix else {kk: sh[kk] for kk in ('ada_w', 'ada_bT', 'norm_gT', 'final_gT', 'ffn_w_in', 'ffn_w_out', 'ident')}
    d['xT'] = np.ascontiguousarray(np.concatenate([xp, xs], 0).T, f)
    cond = np.stack([inp['c_ctx'], inp['c'][b]], -1)
    d['condT'] = np.ascontiguousarray(cond.reshape(8, 128, 2).transpose(1, 0, 2), f)
    if mix:
        d['ckT'] = np.ascontiguousarray(inp['cache_k'][b].reshape(2, 512, 128).transpose(0, 2, 1), f)
        d['cv'] = np.ascontiguousarray(inp['cache_v'][b].reshape(2, 512, 128), f)
        d['C0'] = _state_layout(inp['state_mlstm_C'][b], inp['state_mlstm_n'][b])
        d['S0'] = _state_layout(inp['state_delta_S'][b])
        d['m0'] = np.ascontiguousarray(inp['state_mlstm_m'][b], f)
    return d


def assemble(results):
    f = np.float32
    y_prompt = np.zeros((16, 256, 1024), f)
    y_sample = np.zeros((4, 2048, 1024), f)
    new_k = np.zeros((16, 2, 256, 2, 64), f)
    new_v = np.zeros((16, 2, 256, 2, 64), f)
    new_C = np.zeros((16, 2, 2, 4, 64, 64), f)
    new_n = np.zeros((16, 2, 2, 4, 64), f)
    new_m = np.zeros((16, 2, 2, 4), f)
    new_S = np.zeros((16, 2, 2, 4, 64, 64), f)
    for core, r in enumerate(results):
        yT = r['yT']
        for i in range(2):
            bi = 2 * core + i
            y_prompt[bi] = yT[:, i * 256:(i + 1) * 256].T
            new_k[bi] = r['nkT'][:, :, i * 256:(i + 1) * 256].reshape(2, 2, 64, 256).transpose(0, 3, 1, 2)
            new_v[bi] = r['nvT'][:, :, i * 256:(i + 1) * 256].reshape(2, 2, 64, 256).transpose(0, 3, 1, 2)
            Co = r['Cout'][i]
            So = r['Sout'][i]
            for h in range(4):
                blk = Co[:, :, 64 * (h % 2):64 * (h % 2) + 64, h // 2, :]
                new_C[bi, :, :, h] = blk[..., :64]
                new_n[bi, :, :, h] = blk[..., 64]
                new_S[bi, :, :, h] = So[:, :, 64 * (h % 2):64 * (h % 2) + 64, h // 2, :]
            new_m[bi] = r['mout'][i]
        if core % 2 == 0:
            y_sample[core // 2] = yT[:, 512:].T
    return (y_prompt, y_sample, new_k, new_v, new_C, new_n, new_m, new_S)


_NC_CACHE = {}


def kernel(**inputs):
    inp = {kk: np.asarray(v) for kk, v in inputs.items()}
    if 'nc' not in _NC_CACHE:
        _NC_CACHE['nc'] = build(dict(mix=True, layers=2))
    nc = _NC_CACHE['nc']
    sh = _shared(inp)
    in_maps = [prep_core(inp, c, sh) for c in range(8)]
    res = run_bass_kernel_spmd(nc, in_maps, core_ids=list(range(8)))
    return assemble(res.results)
```
